# Optimizing a Trainium2 kernel written in Bass

```python
import math
import jax, jax.numpy as jnp
from jax import lax
import numpy as np

D_MODEL = 1024
BATCH = 4
SEQ = 4096
DEPTH = 2
DEC_BATCH = 128
DEC_SEQ = 8
PAST_LEN = 2048
PAGE_SIZE = 128

MIX_WIDTH = D_MODEL
N_GROUPS = 4
W_GROUP = MIX_WIDTH // N_GROUPS
H_LRU = 4
LRU_CONV = 4
LRU_C = 8.0
SC_CONV = 3
H_FOX = 4
DH_FOX = W_GROUP // H_FOX
H_DIFF = 4
DV_DIFF = W_GROUP // H_DIFF
DH_DIFF = DV_DIFF // 2
N_BUCKETS = 32
MAX_DISTANCE = 128
Q_BLOCK = 128
D_FF = -(-8 * D_MODEL // (3 * 256)) * 256
EPS = 1e-6
IN_SIZES = (W_GROUP, W_GROUP,
            W_GROUP, W_GROUP, W_GROUP,
            H_FOX * DH_FOX, H_FOX * DH_FOX, H_FOX * DH_FOX, H_FOX,
            H_DIFF * 2 * DH_DIFF, H_DIFF * 2 * DH_DIFF, H_DIFF * DV_DIFF)
N_IN = sum(IN_SIZES)

kernel_name = 'hybrid_parallel_groups_step'


def rmsnorm(x, g):
    xf = x.astype(jnp.float32)
    y = xf * lax.rsqrt(jnp.mean(xf * xf, axis=-1, keepdims=True) + EPS)
    return (y * g.astype(jnp.float32)).astype(x.dtype)


def split_columns(z):
    points = np.cumsum(np.array(IN_SIZES))[:-1].tolist()
    return jnp.split(z, points, axis=-1)


def causal_dwconv(x, buf, w):
    k_w = w.shape[0]
    t = x.shape[1]
    xp = jnp.concatenate([buf.astype(x.dtype), x], axis=1)
    y = xp[:, 0:t] * w[0]
    for k in range(1, k_w):
        y = y + xp[:, k:k + t] * w[k]
    return y, xp[:, t:]


def rg_lru(x, h0, w_a, b_a, w_x, b_x, lam):
    b, t, c = x.shape
    xb = x.reshape(b, t, H_LRU, c // H_LRU)
    r = jax.nn.sigmoid(jnp.einsum('bthi,hij->bthj', xb, w_a).reshape(b, t, c) + b_a).astype(jnp.float32)
    i = jax.nn.sigmoid(jnp.einsum('bthi,hij->bthj', xb, w_x).reshape(b, t, c) + b_x).astype(jnp.float32)
    log_a = -LRU_C * r * jax.nn.softplus(-lam.astype(jnp.float32))
    a = jnp.exp(log_a)
    u = jnp.sqrt(-jnp.expm1(2.0 * log_a)) * i * x.astype(jnp.float32)

    def step(h, au):
        a_t, u_t = au
        h = a_t * h + u_t
        return h, h

    h_last, hs = lax.scan(step, h0.astype(jnp.float32), (jnp.swapaxes(a, 0, 1), jnp.swapaxes(u, 0, 1)))
    return jnp.swapaxes(hs, 0, 1), h_last


def t5_bucket(rel):
    n = jnp.maximum(rel, 0)
    max_exact = N_BUCKETS // 2
    nf = jnp.maximum(n, 1).astype(jnp.float32)
    large = max_exact + (jnp.log(nf / max_exact) / math.log(MAX_DISTANCE / max_exact)
                         * (N_BUCKETS - max_exact)).astype(jnp.int32)
    large = jnp.minimum(large, N_BUCKETS - 1)
    return jnp.where(n < max_exact, n, large)


def sweep_query_blocks(block_fn, q_inputs, q_offset):
    t_q = q_inputs[0].shape[1]
    qb = min(Q_BLOCK, t_q)
    nb = -(-t_q // qb)
    pad = nb * qb - t_q

    def to_blocks(a):
        a = jnp.pad(a, [(0, 0), (0, pad)] + [(0, 0)] * (a.ndim - 2))
        a = a.reshape((a.shape[0], nb, qb) + a.shape[2:])
        return jnp.moveaxis(a, 1, 0)

    blocks = tuple(to_blocks(a) for a in q_inputs)
    pos = q_offset + jnp.arange(nb * qb, dtype=jnp.int32).reshape(nb, qb)
    out = lax.map(lambda args: block_fn(*args), blocks + (pos,))
    out = jnp.moveaxis(out, 0, 1)
    out = out.reshape((out.shape[0], nb * qb) + out.shape[3:])
    return out[:, :t_q]


def fox_attention(q, k, v, f_q, f_k, q_offset):
    scale = DH_FOX ** -0.5
    k_pos = jnp.arange(k.shape[1], dtype=jnp.int32)
    fk_t = jnp.swapaxes(f_k, 1, 2)[:, :, None, :]

    def block(qb, fqb, pos):
        s = jnp.einsum('bqhd,bkhd->bhqk', qb, k, preferred_element_type=jnp.float32) * scale
        s = s + jnp.swapaxes(fqb, 1, 2)[..., None] - fk_t
        s = jnp.where(k_pos[None, :] <= pos[:, None], s, -jnp.inf)
        p = jax.nn.softmax(s, axis=-1)
        return jnp.einsum('bhqk,bkhd->bqhd', p.astype(v.dtype), v)

    return sweep_query_blocks(block, (q, f_q), q_offset)


def diff_attention(q, k, v, lam, rel_bias, q_offset):
    scale = DH_DIFF ** -0.5
    k_pos = jnp.arange(k.shape[1], dtype=jnp.int32)
    k1, k2 = k[..., :DH_DIFF], k[..., DH_DIFF:]

    def block(qb, pos):
        rel = pos[:, None] - k_pos[None, :]
        bias = jnp.moveaxis(rel_bias[t5_bucket(rel)], -1, 0).astype(jnp.float32)
        mask = rel >= 0

        def probs(qq, kk):
            s = jnp.einsum('bqhd,bkhd->bhqk', qq, kk, preferred_element_type=jnp.float32) * scale + bias
            return jax.nn.softmax(jnp.where(mask, s, -jnp.inf), axis=-1)

        a = probs(qb[..., :DH_DIFF], k1) - lam * probs(qb[..., DH_DIFF:], k2)
        return jnp.einsum('bhqk,bkhd->bqhd', a, v.astype(jnp.float32))

    return sweep_query_blocks(block, (q,), q_offset)


def lambda_init_fn(layer_idx):
    return 0.8 - 0.6 * math.exp(-0.3 * layer_idx)


def mixer_block(xn, w_in, w_out, lru_conv_w, lru_conv_b, lru_w_a, lru_b_a, lru_w_x, lru_b_x, lru_lambda,
                sc_conv_w, fox_b_f, diff_lambda, diff_norm_g, rel_bias, lambda_init,
                lru_h0, lru_buf, sc_buf, fox_k_past, fox_v_past, fox_logf_past, diff_k_past, diff_v_past):
    b, t, _ = xn.shape
    past = fox_k_past.shape[1]
    z = xn @ w_in
    (x_lru, g_lru, sc_b, sc_c, sc_h, fq, fk, fv, f_logit, dq, dk, dv) = split_columns(z)

    xc, lru_buf_new = causal_dwconv(x_lru, lru_buf, lru_conv_w)
    xc = xc + lru_conv_b
    hs, h_last = rg_lru(xc, lru_h0, lru_w_a, lru_b_a, lru_w_x, lru_b_x, lru_lambda)
    y_lru = hs.astype(xn.dtype) * jax.nn.gelu(g_lru)

    cx = sc_c * sc_h
    conv, sc_buf_new = causal_dwconv(cx, sc_buf, sc_conv_w)
    y_sc = sc_b * conv

    fq = fq.reshape(b, t, H_FOX, DH_FOX)
    fk = fk.reshape(b, t, H_FOX, DH_FOX)
    fv = fv.reshape(b, t, H_FOX, DH_FOX)
    logf = jax.nn.log_sigmoid((f_logit + fox_b_f).astype(jnp.float32))
    k_all = jnp.concatenate([fox_k_past.astype(fk.dtype), fk], axis=1)
    v_all = jnp.concatenate([fox_v_past.astype(fv.dtype), fv], axis=1)
    cum_f = jnp.cumsum(jnp.concatenate([fox_logf_past.astype(jnp.float32), logf], axis=1), axis=1)
    y_fox = fox_attention(fq, k_all, v_all, cum_f[:, past:], cum_f, past).reshape(b, t, H_FOX * DH_FOX)

    dq = dq.reshape(b, t, H_DIFF, 2 * DH_DIFF)
    dk = dk.reshape(b, t, H_DIFF, 2 * DH_DIFF)
    dv = dv.reshape(b, t, H_DIFF, DV_DIFF)
    lam_par = diff_lambda.astype(jnp.float32)
    lam = (jnp.exp(jnp.sum(lam_par[0] * lam_par[1])) - jnp.exp(jnp.sum(lam_par[2] * lam_par[3]))
           + lambda_init)
    dk_all = jnp.concatenate([diff_k_past.astype(dk.dtype), dk], axis=1)
    dv_all = jnp.concatenate([diff_v_past.astype(dv.dtype), dv], axis=1)
    od = diff_attention(dq, dk_all, dv_all, lam, rel_bias, past)
    od = rmsnorm(od, diff_norm_g) * (1.0 - lambda_init)
    y_diff = od.reshape(b, t, H_DIFF * DV_DIFF).astype(xn.dtype)

    mix = jnp.concatenate([y_lru, y_sc, y_fox.astype(xn.dtype), y_diff], axis=-1) @ w_out
    return mix, (fk, fv, logf, dk, dv, h_last, lru_buf_new, sc_buf_new)


def swiglu(x, w_gate, w_up, w_down):
    return (jax.nn.silu(x @ w_gate) * (x @ w_up)) @ w_down


def gather_pages(cache_l, page_table):
    g = cache_l[page_table]
    return g.reshape((page_table.shape[0], page_table.shape[1] * cache_l.shape[1]) + cache_l.shape[2:])


def setup_inputs(seed: int = 0) -> dict:
    key = jax.random.key(seed)
    ks = iter(jax.random.split(key, 48))

    def nrm(shape, s=1.0):
        return s * jax.random.normal(next(ks), shape, jnp.float32)

    n_pages = PAST_LEN // PAGE_SIZE
    n_used = DEC_BATCH * n_pages
    n_pool = n_used + (-(-n_used // 4))
    page_table = jax.random.permutation(next(ks), n_pool)[:n_used].reshape(DEC_BATCH, n_pages).astype(jnp.int32)

    u = jax.random.uniform(next(ks), (DEPTH, W_GROUP), jnp.float32, minval=0.9, maxval=0.999)
    s_a = u ** (1.0 / LRU_C)
    lru_lambda = jnp.log(s_a) - jnp.log1p(-s_a)
    gb = W_GROUP // H_LRU
    return {
        'x_prompt': nrm((BATCH, SEQ, D_MODEL)),
        'x_sample': nrm((DEC_BATCH, DEC_SEQ, D_MODEL)),
        'cache_fox_k': nrm((DEPTH, n_pool, PAGE_SIZE, H_FOX, DH_FOX)),
        'cache_fox_v': nrm((DEPTH, n_pool, PAGE_SIZE, H_FOX, DH_FOX)),
        'cache_fox_logf': jax.nn.log_sigmoid(2.0 + nrm((DEPTH, n_pool, PAGE_SIZE, H_FOX))),
        'cache_diff_k': nrm((DEPTH, n_pool, PAGE_SIZE, H_DIFF, 2 * DH_DIFF)),
        'cache_diff_v': nrm((DEPTH, n_pool, PAGE_SIZE, H_DIFF, DV_DIFF)),
        'state_lru_h': nrm((DEPTH, DEC_BATCH, W_GROUP), 0.3),
        'state_lru_conv': nrm((DEPTH, DEC_BATCH, LRU_CONV - 1, W_GROUP)),
        'state_sconv': nrm((DEPTH, DEC_BATCH, SC_CONV - 1, W_GROUP)),
        'page_table': page_table,
        'norm_mix_g': 1.0 + nrm((DEPTH, D_MODEL), 0.01),
        'w_in': nrm((DEPTH, D_MODEL, N_IN), D_MODEL ** -0.5),
        'w_out': nrm((DEPTH, MIX_WIDTH, D_MODEL), MIX_WIDTH ** -0.5),
        'lru_conv_w': nrm((DEPTH, LRU_CONV, W_GROUP), LRU_CONV ** -0.5),
        'lru_conv_b': nrm((DEPTH, W_GROUP), 0.01),
        'lru_w_a': nrm((DEPTH, H_LRU, gb, gb), gb ** -0.5),
        'lru_b_a': nrm((DEPTH, W_GROUP), 0.01),
        'lru_w_x': nrm((DEPTH, H_LRU, gb, gb), gb ** -0.5),
        'lru_b_x': nrm((DEPTH, W_GROUP), 0.01),
        'lru_lambda': lru_lambda,
        'sc_conv_w': nrm((DEPTH, SC_CONV, W_GROUP), SC_CONV ** -0.5),
        'fox_b_f': 2.0 + nrm((DEPTH, H_FOX), 0.1),
        'diff_lambda': nrm((DEPTH, 4, DH_DIFF), 0.1),
        'diff_norm_g': 1.0 + nrm((DEPTH, DV_DIFF), 0.01),
        'rel_bias': nrm((N_BUCKETS, H_DIFF), 0.5),
        'norm_ffn_g': 1.0 + nrm((DEPTH, D_MODEL), 0.01),
        'w_gate': nrm((DEPTH, D_MODEL, D_FF), D_MODEL ** -0.5),
        'w_up': nrm((DEPTH, D_MODEL, D_FF), D_MODEL ** -0.5),
        'w_down': nrm((DEPTH, D_FF, D_MODEL), D_FF ** -0.5),
        'norm_final_g': 1.0 + nrm((D_MODEL,), 0.01),
    }


def reference(x_prompt, x_sample, cache_fox_k, cache_fox_v, cache_fox_logf, cache_diff_k, cache_diff_v,
              state_lru_h, state_lru_conv, state_sconv, page_table,
              norm_mix_g, w_in, w_out, lru_conv_w, lru_conv_b, lru_w_a, lru_b_a, lru_w_x, lru_b_x,
              lru_lambda, sc_conv_w, fox_b_f, diff_lambda, diff_norm_g, rel_bias,
              norm_ffn_g, w_gate, w_up, w_down, norm_final_g):

    def layer(x, l, lru_h0, lru_buf, sc_buf, fk_p, fv_p, flf_p, dk_p, dv_p):
        xn = rmsnorm(x, norm_mix_g[l])
        mix, st = mixer_block(xn, w_in[l], w_out[l], lru_conv_w[l], lru_conv_b[l], lru_w_a[l], lru_b_a[l],
                              lru_w_x[l], lru_b_x[l], lru_lambda[l], sc_conv_w[l], fox_b_f[l],
                              diff_lambda[l], diff_norm_g[l], rel_bias, lambda_init_fn(l),
                              lru_h0, lru_buf, sc_buf, fk_p, fv_p, flf_p, dk_p, dv_p)
        x = x + mix
        x = x + swiglu(rmsnorm(x, norm_ffn_g[l]), w_gate[l], w_up[l], w_down[l])
        return x, st

    b = x_prompt.shape[0]
    dt = x_prompt.dtype
    hp = x_prompt
    p_states = []
    for l in range(DEPTH):
        hp, st = layer(hp, l,
                       jnp.zeros((b, W_GROUP), dt),
                       jnp.zeros((b, LRU_CONV - 1, W_GROUP), dt),
                       jnp.zeros((b, SC_CONV - 1, W_GROUP), dt),
                       jnp.zeros((b, 0, H_FOX, DH_FOX), dt),
                       jnp.zeros((b, 0, H_FOX, DH_FOX), dt),
                       jnp.zeros((b, 0, H_FOX), jnp.float32),
                       jnp.zeros((b, 0, H_DIFF, 2 * DH_DIFF), dt),
                       jnp.zeros((b, 0, H_DIFF, DV_DIFF), dt))
        p_states.append(st)
    y_prompt = rmsnorm(hp, norm_final_g)

    hs_ = x_sample
    s_states = []
    for l in range(DEPTH):
        hs_, st = layer(hs_, l, state_lru_h[l], state_lru_conv[l], state_sconv[l],
                        gather_pages(cache_fox_k[l], page_table),
                        gather_pages(cache_fox_v[l], page_table),
                        gather_pages(cache_fox_logf[l], page_table),
                        gather_pages(cache_diff_k[l], page_table),
                        gather_pages(cache_diff_v[l], page_table))
        s_states.append(st)
    y_sample = rmsnorm(hs_, norm_final_g)

    (p_fox_k, p_fox_v, p_fox_logf, p_diff_k, p_diff_v, p_lru_h, p_lru_conv, p_sconv) = [
        jnp.stack(s, axis=0) for s in zip(*p_states)]
    (s_fox_k, s_fox_v, s_fox_logf, s_diff_k, s_diff_v, s_lru_h, s_lru_conv, s_sconv) = [
        jnp.stack(s, axis=0) for s in zip(*s_states)]
    return (y_prompt, y_sample,
            p_fox_k, p_fox_v, p_fox_logf, p_diff_k, p_diff_v, p_lru_h, p_lru_conv, p_sconv,
            s_fox_k, s_fox_v, s_fox_logf, s_diff_k, s_diff_v, s_lru_h, s_lru_conv, s_sconv)
```

```python
import math
from contextlib import ExitStack

import numpy as np
import concourse.bass as bass
import concourse.mybir as mybir
from concourse.bass_utils import run_bass_kernel_spmd

F32 = mybir.dt.float32
BF16 = mybir.dt.bfloat16
I32 = mybir.dt.int32
U32 = mybir.dt.uint32
AF = mybir.ActivationFunctionType
ALU = mybir.AluOpType

ENGS = ("pe", "act", "dve", "pool", "sp")
N_DMA_SEMS = 6

L = 2
D = 1024
T = 4096
NS = 16
TS = 8
NTS = NS * TS
NT = T + NTS
NIN = 2820
DFF = 2816
NFC = DFF // 128
PAST = 2048
NPG = 16
NPOOL = 2560
EPS = 1e-6
N_CORES = 8


class Prog:
    def __init__(self):
        self.ops = {e: [] for e in ENGS}
        self.cnt = {}
        self.last_w = {}
        self.readers = {}
        self.waited = {e: {} for e in ENGS}
        self.rr = {e: 0 for e in ENGS}
        self.pending = {e: [] for e in ENGS}

    def barrier(self):
        cur = [(sk, v) for sk, v in self.cnt.items() if v > 0]
        for e in ENGS:
            self.pending[e] = list(cur)
        self.last_w = {}
        self.readers = {}

    def emit(self, eng, fn, reads=(), writes=(), dma=False):
        deps = list(self.pending[eng])
        self.pending[eng] = []
        for k in reads:
            t = self.last_w.get(k)
            if t is not None:
                deps.append(t)
        for k in writes:
            t = self.last_w.get(k)
            if t is not None:
                deps.append(t)
            deps.extend(self.readers.get(k, ()))
        if dma:
            slot = self.rr[eng]
            self.rr[eng] = (slot + 1) % N_DMA_SEMS
            sk = ("dma", eng, slot)
            inc = 16
            if self.cnt.get(sk, 0) > 0:
                deps.append((sk, self.cnt[sk]))
        else:
            sk = ("eng", eng)
            inc = 1
        val = self.cnt.get(sk, 0) + inc
        self.cnt[sk] = val
        tok = (sk, val)
        waits = []
        wd = self.waited[eng]
        need = {}
        for (dk, dv) in deps:
            if dv > need.get(dk, 0):
                need[dk] = dv
        for dk, dv in need.items():
            if wd.get(dk, 0) < dv:
                wd[dk] = dv
                waits.append((dk, dv))
        self.ops[eng].append((fn, waits, sk, inc))
        for k in writes:
            self.last_w[k] = tok
            self.readers[k] = []
        for k in reads:
            self.readers.setdefault(k, []).append(tok)
        return tok

    def replay(self, nc, stack):
        sems = {}
        for sk in self.cnt:
            sems[sk] = stack.enter_context(nc.semaphore("s_" + "_".join(str(x) for x in sk)))
        block = stack.enter_context(nc.Block())
        finals = [(sk, v) for sk, v in self.cnt.items()]

        def mk(eng_name):
            def body(e):
                for (fn, waits, sk, inc) in self.ops[eng_name]:
                    for (wk, wv) in waits:
                        e.wait_ge(sems[wk], wv)
                    fn(e).then_inc(sems[sk], inc)
                if eng_name == "sp":
                    for (fk, fv) in finals:
                        e.wait_ge(sems[fk], fv)
            return body

        block.tensor(mk("pe"))
        block.scalar(mk("act"))
        block.vector(mk("dve"))
        block.gpsimd(mk("pool"))
        block.sync(mk("sp"))


class Arena:
    def __init__(self, ap, words):
        self.ap = ap
        self.words = words
        self.off = 0

    def reset(self):
        self.off = 0

    def alloc(self, shape, dtype, parts=128):
        n = 1
        for s in shape[1:]:
            n *= s
        w = n if dtype in (F32, I32, U32) else (n + 1) // 2
        w = (w + 7) // 8 * 8
        assert self.off + w <= self.words, ("arena overflow", self.off, w, self.words)
        v = self.ap[0:shape[0], self.off:self.off + w]
        self.off += w
        if dtype != F32:
            v = v.bitcast(dtype)
        v = v[:, 0:n]
        if len(shape) == 3:
            v = v.rearrange("p (a b) -> p a b", b=shape[2])
        elif len(shape) == 4:
            v = v.rearrange("p (a b c) -> p a b c", b=shape[2], c=shape[3])
        return v


def t5_onehot():
    oh = np.zeros((32, 384), np.float32)
    for j in range(383):
        rel = j - 127
        if rel < 0:
            continue
        if rel < 16:
            b = rel
        else:
            b = 16 + int(np.float32(np.log(np.float32(rel) / np.float32(16.0)) / np.float32(math.log(8.0))
                                    * np.float32(16.0)))
            b = min(b, 31)
        oh[b, j] += 1.0
        oh[31, j] -= 1.0
    return oh


def build_nc(sample_attn=True):
    nc = bass.Bass("TRN2", target_bir_lowering=False)
    P = Prog()

    def din(name, shape, dt=F32):
        return nc.dram_tensor(name, list(shape), dt, kind="ExternalInput").ap()

    def dout(name, shape, dt=F32):
        return nc.dram_tensor(name, list(shape), dt, kind="ExternalOutput").ap()

    def dscr(name, shape, dt):
        return nc.dram_tensor(name, list(shape), dt, kind="Internal").ap()

    xp = din("xp", [T, D])
    xs = din("xs", [NTS, D])
    w_in = din("w_in", [L, D, NIN])
    w_out = din("w_out", [L, D, D])
    w_gate = din("w_gate", [L, D, DFF])
    w_up = din("w_up", [L, D, DFF])
    w_down = din("w_down", [L, DFF, D])
    norm_mix_g = din("norm_mix_g", [L, D])
    norm_ffn_g = din("norm_ffn_g", [L, D])
    norm_final_g = din("norm_final_g", [D])
    lru_conv_w = din("lru_conv_w", [L, 4, 256])
    lru_conv_b = din("lru_conv_b", [L, 256])
    lru_w_a = din("lru_w_a", [L, 4, 64, 64])
    lru_b_a = din("lru_b_a", [L, 256])
    lru_w_x = din("lru_w_x", [L, 4, 64, 64])
    lru_b_x = din("lru_b_x", [L, 256])
    lru_lambda = din("lru_lambda", [L, 256])
    sc_conv_w = din("sc_conv_w", [L, 3, 256])
    fox_b_f = din("fox_b_f", [L, 4])
    diff_lambda = din("diff_lambda", [L, 128])
    diff_norm_g = din("diff_norm_g", [L, 64])
    rel_bias = din("rel_bias", [32, 4])
    c_t5 = din("c_t5", [32, 384])
    st_lru_h = din("st_lru_h", [L, NS, 256])
    st_lru_conv = din("st_lru_conv", [L, NS, 3, 256])
    st_sconv = din("st_sconv", [L, NS, 2, 256])
    if sample_attn:
        c_fk = din("c_fk", [L * NPOOL * 128, 256])
        c_fv = din("c_fv", [L * NPOOL * 128, 256])
        c_flf = din("c_flf", [L * NPOOL * 128, 4])
        c_dk = din("c_dk", [L * NPOOL * 128, 256])
        c_dv = din("c_dv", [L * NPOOL * 128, 256])
        ptab = din("ptab", [NS * NPG], I32)

    o_yp = dout("o_yp", [T, D])
    o_ys = dout("o_ys", [NTS, D])
    o_pfk = dout("o_pfk", [L, T, 256])
    o_pfv = dout("o_pfv", [L, T, 256])
    o_pflf = dout("o_pflf", [L, T, 4])
    o_pdk = dout("o_pdk", [L, T, 256])
    o_pdv = dout("o_pdv", [L, T, 256])
    o_plh = dout("o_plh", [L, 256])
    o_plc = dout("o_plc", [L, 3, 256])
    o_psc = dout("o_psc", [L, 2, 256])
    o_sfk = dout("o_sfk", [L, NTS, 256])
    o_sfv = dout("o_sfv", [L, NTS, 256])
    o_sflf = dout("o_sflf", [L, NTS, 4])
    o_sdk = dout("o_sdk", [L, NTS, 256])
    o_sdv = dout("o_sdv", [L, NTS, 256])
    o_slh = dout("o_slh", [L, NS, 256])
    o_slc = dout("o_slc", [L, NS, 3, 256])
    o_ssc = dout("o_ssc", [L, NS, 2, 256])

    wb_in = dscr("wb_in", [L, D, NIN], BF16)
    wb_out = dscr("wb_out", [L, D, D], BF16)
    wb_gate = dscr("wb_gate", [L, D, DFF], BF16)
    wb_up = dscr("wb_up", [L, D, DFF], BF16)
    wb_down = dscr("wb_down", [L, DFF, D], BF16)
    xres = dscr("xres", [NT, D], F32)
    zfm = dscr("zfm", [1280, NT], BF16)
    fqa = dscr("fqa", [4, 67, NT], BF16)
    fka = dscr("fka", [4, 67, NT], BF16)
    dqs = dscr("dqs", [4, 64, NT], BF16)
    dks = dscr("dks", [4, 64, NT], BF16)
    flT = dscr("flT", [4, NT], F32)
    vsf = dscr("vsf", [NT, 256], BF16)
    vsd = dscr("vsd", [NT, 256], BF16)
    lfs = dscr("lfs", [NT, 4], F32)
    mixT = dscr("mixT", [D, NT], BF16)
    aT = dscr("aT", [DFF, NT], BF16)
    t5s = dscr("t5s", [4, 384], BF16)
    t5b = dscr("t5b", [4, 128, 384], BF16)

    tiles = [(i * 512, 512) for i in range(T // 512)] + [(T, NTS)]

    with ExitStack() as st:
        st.enter_context(nc.allow_non_contiguous_dma(reason="layout transforms"))
        st.enter_context(nc.allow_low_precision(reason="bf16 matmul operands per problem tolerance"))
        AW = 42000
        arena_t = st.enter_context(nc.sbuf_tensor("arena", [128, AW], F32))
        A = Arena(arena_t, AW)
        cst_t = st.enter_context(nc.sbuf_tensor("cst", [128, 2600], F32))
        C = Arena(cst_t, 2600)
        ident = C.alloc([128, 128], BF16)
        tri = C.alloc([128, 256], BF16)
        ones_b = C.alloc([128, 128], BF16)
        ones_f = C.alloc([128, 64], F32)
        gb = C.alloc([128, 8, 128], BF16)
        gcol = C.alloc([128, 8], F32)
        small = C.alloc([128, 256], F32)
        eb = C.alloc([128, 4, 256], BF16)
        psb = [st.enter_context(nc.psum_tensor("ps%d" % i, [128, 512], F32))[:, :] for i in range(7)]
        pst = st.enter_context(nc.psum_tensor("pst", [128, 1024], BF16))[:, :]

        def dma(q, out, in_, reads=(), writes=()):
            return P.emit(q, lambda e: e.dma_start(out=out, in_=in_), reads, writes, dma=True)

        def mm(out, lhsT, rhs, start, stop, reads, writes):
            return P.emit("pe", lambda e: e.matmul(out, lhsT=lhsT, rhs=rhs, start=start, stop=stop), reads, writes)

        def act(out, in_, func, reads, writes, bias=None, scale=None, accum_out=None):
            kw = {}
            if bias is not None:
                kw["bias"] = bias
            if scale is not None:
                kw["scale"] = scale
            if accum_out is not None:
                kw["accum_out"] = accum_out
            return P.emit("act", lambda e: e.activation(out=out, in_=in_, func=func, **kw), reads, writes)

        def ts(eng, out, in0, s1, s2, op0, op1, reads, writes):
            if s2 is None:
                return P.emit(eng, lambda e: e.tensor_scalar(out=out, in0=in0, scalar1=s1, scalar2=None, op0=op0),
                              reads, writes)
            return P.emit(eng, lambda e: e.tensor_scalar(out=out, in0=in0, scalar1=s1, scalar2=s2, op0=op0, op1=op1),
                          reads, writes)

        def tt(eng, out, in0, in1, op, reads, writes):
            return P.emit(eng, lambda e: e.tensor_tensor(out=out, in0=in0, in1=in1, op=op), reads, writes)

        def stt(out, in0, scalar, in1, op0, op1, reads, writes):
            return P.emit("dve", lambda e: e.scalar_tensor_tensor(out=out, in0=in0, scalar=scalar, in1=in1,
                                                                   op0=op0, op1=op1), reads, writes)

        def cp(eng, out, in_, reads, writes):
            if eng == "act":
                return P.emit("act", lambda e: e.copy(out=out, in_=in_), reads, writes)
            return P.emit(eng, lambda e: e.tensor_copy(out=out, in_=in_), reads, writes)

        def memset(eng, ap, val, writes):
            return P.emit(eng, lambda e: e.memset(ap, val), (), writes)

        memset("pool", ones_b, 1.0, ["ones_b"])
        memset("pool", ones_f, 1.0, ["ones_f"])
        memset("pool", ident, 1.0, ["ident"])
        P.emit("pool", lambda e: e.affine_select(out=ident, in_=ident, pattern=[[-1, 128]], compare_op=ALU.is_equal,
                                                 fill=0.0, base=0, channel_multiplier=1), ["ident"], ["ident"])
        memset("pool", tri, 1.0, ["tri"])
        P.emit("pool", lambda e: e.affine_select(out=tri[:, 0:128], in_=tri[:, 0:128], pattern=[[1, 128]],
                                                 compare_op=ALU.is_ge, fill=0.0, base=0, channel_multiplier=-1),
               ["tri"], ["tri"])
        for l in range(L):
            for (src, dst, rows) in ((w_in, wb_in, D), (w_out, wb_out, D), (w_gate, wb_gate, D), (w_up, wb_up, D),
                                     (w_down, wb_down, DFF)):
                nsp = 4
                rr_ = rows // nsp
                for i in range(nsp):
                    dma("pool", dst[l, i * rr_:(i + 1) * rr_, :], src[l, i * rr_:(i + 1) * rr_, :],
                        writes=[("wb", id(dst), l)])
        cr = A.alloc([4, NT], BF16)
        memset("dve", cr, 1.0, ["cr"])
        dma("sp", fka[:, 64, :], cr, ["cr"], [])
        cr2 = A.alloc([4, NT], BF16)
        memset("dve", cr2, -8.0, ["cr2"])
        dma("sp", fqa[:, 65, :], cr2, ["cr2"], [])
        dma("sp", fqa[:, 66, :], cr2, ["cr2"], [])
        rb = A.alloc([32, 4], F32)
        oh = A.alloc([32, 384], F32)
        dma("sp", rb, rel_bias, (), ["rb"])
        dma("sp", oh, c_t5, (), ["oh"])
        mm(psb[0][0:4, 0:384], rb, oh, True, True, ["rb", "oh"], ["ps0"])
        gex = A.alloc([4, 384], BF16)
        act(gex, psb[0][0:4, 0:384], AF.Exp, ["ps0"], ["gex"])
        memset("dve", gex[:, 0:127], 0.0, ["gex"])
        dma("sp", t5s, gex, ["gex"], ["t5s"])
        for h in range(4):
            bsrc = bass.AP(tensor=t5s.tensor, offset=t5s.offset + h * 384, ap=[[0, 128], [1, 384]])
            dma("sp", t5b[h], bsrc, ["t5s"], ["t5b"])
            src = bass.AP(tensor=t5b.tensor, offset=t5b.offset + h * 128 * 384 + 127, ap=[[383, 128], [1, 256]])
            dma("sp", eb[:, h, :], src, ["t5b"], ["eb"])
        P.barrier()

        def load_gb(gsrc):
            dma("sp", gcol, gsrc.rearrange("(c p) -> p c", p=128), (), ["gcol"])
            for c in range(8):
                ts("pool", gb[:, c, :], ones_b, gcol[:, c:c + 1], None, ALU.mult, None, ["gcol", "ones_b"], ["gb"])

        def rmsnorm_T(xt, nsub, xnT, W):
            for s in range(nsub):
                act(W["junk"], xt[:, s, :], AF.Square, ["xt"], ["junk", "ssq"], accum_out=W["ssq"][:, s:s + 1])
            ts("dve", W["rstd"][:, 0:nsub], W["ssq"][:, 0:nsub], 1.0 / D, EPS, ALU.mult, ALU.add, ["ssq"], ["rstd"])
            act(W["rstd"][:, 0:nsub], W["rstd"][:, 0:nsub], AF.Sqrt, ["rstd"], ["rstd"])
            P.emit("dve", lambda e: e.reciprocal(out=W["rstd"][:, 0:nsub], in_=W["rstd"][:, 0:nsub]), ["rstd"], ["rstd"])
            for s in range(nsub):
                ts("dve" if s % 2 == 0 else "pool", W["xsb"][:, s, :], xt[:, s, :], W["rstd"][:, s:s + 1], None,
                   ALU.mult, None, ["xt", "rstd"], ["xsb"])
            for s in range(nsub):
                for c in range(8):
                    P.emit("pe", lambda e, s=s, c=c: e.transpose(out=pst[:, c * 128:(c + 1) * 128],
                                                                  in_=W["xsb"][:, s, c * 128:(c + 1) * 128],
                                                                  identity=ident), ["xsb", "ident"], ["pst"])
                tt("dve", xnT[:, :, s * 128:(s + 1) * 128], pst.rearrange("p (c t) -> p c t", t=128), gb, ALU.mult,
                   ["pst", "gb"], ["xnT"])

        def load_w(dst, src, key):
            K = src.shape[0] // 128
            for k in range(K):
                dma("sp", dst[:, k, :], src[k * 128:(k + 1) * 128, :], [key[0]], [key[1]])

        evac_rr = [0]

        def evac(out, in_, reads, writes):
            evac_rr[0] ^= 1
            return cp("act" if evac_rr[0] else "dve", out, in_, reads, writes)

        def phase1(l):
            A.reset()
            wA = A.alloc([128, 8, NIN], BF16)
            W = dict(junk=A.alloc([128, 1024], F32), ssq=A.alloc([128, 4], F32), rstd=A.alloc([128, 4], F32),
                     xsb=A.alloc([128, 4, 1024], BF16))
            xt = A.alloc([128, 4, 1024], F32)
            xnT = A.alloc([128, 8, 512], BF16)
            zst = [A.alloc([128, 512], BF16) for _ in range(4)]
            zsf = A.alloc([4, 512], F32)
            ztm = A.alloc([128, 4, 1028], F32)
            vst = A.alloc([128, 4, 512], BF16)
            bfb = A.alloc([128, 4], F32)
            lt = A.alloc([128, 4, 4], F32)
            load_w(wA, wb_in[l], (("wb", id(wb_in), l), "wA"))
            load_gb(norm_mix_g[l])
            dma("sp", bfb, fox_b_f[l].partition_broadcast(128), (), ["bfb"])
            groups = [(c * 128, 128, zfm[c * 128:(c + 1) * 128, :]) for c in range(10)]
            for h in range(4):
                groups.append((1280 + 64 * h, 64, fqa[h, 0:64, :]))
                groups.append((1536 + 64 * h, 64, fka[h, 0:64, :]))
                groups.append((2052 + 64 * h, 64, dqs[h, 0:64, :]))
                groups.append((2308 + 64 * h, 64, dks[h, 0:64, :]))
            for (t0, n) in tiles:
                nsub = n // 128
                if l == 0:
                    src = xp[t0:t0 + n, :] if t0 < T else xs
                else:
                    src = xres[t0:t0 + n, :]
                dma("sp", xt[:, 0:nsub, :], src.rearrange("(s p) d -> p s d", p=128), ["xres"], ["xt"])
                rmsnorm_T(xt, nsub, xnT, W)
                for gi, (c0, M, dst) in enumerate(groups):
                    ps = psb[gi % 4]
                    pk = "ps%d" % (gi % 4)
                    for k in range(8):
                        mm(ps[0:M, 0:n], wA[:, k, c0:c0 + M], xnT[:, k, 0:n], k == 0, k == 7, ["wA", "xnT"], [pk])
                    zk = "zst%d" % (gi % 4)
                    evac(zst[gi % 4][0:M, 0:n], ps[0:M, 0:n], [pk], [zk])
                    dma("pool" if gi % 2 else "sp", dst[:, t0:t0 + n], zst[gi % 4][0:M, 0:n], [zk], [])
                for k in range(8):
                    mm(psb[0][0:4, 0:n], wA[:, k, 2048:2052], xnT[:, k, 0:n], k == 0, k == 7, ["wA", "xnT"], ["ps0"])
                cp("dve", zsf[:, 0:n], psb[0][0:4, 0:n], ["ps0"], ["zsf"])
                dma("sp", flT[:, t0:t0 + n], zsf[:, 0:n], ["zsf"], [])
                for s in range(nsub):
                    for (pi, c0, ncol) in ((4, 1536, 512), (5, 2308, 512), (6, 2048, 4)):
                        for k in range(8):
                            mm(psb[pi][:, 0:ncol], xnT[:, k, s * 128:(s + 1) * 128], wA[:, k, c0:c0 + ncol],
                               k == 0, k == 7, ["wA", "xnT"], ["ps%d" % pi])
                    cp("act", ztm[:, s, 0:512], psb[4], ["ps4"], ["ztm"])
                    cp("dve", ztm[:, s, 512:1024], psb[5], ["ps5"], ["ztm"])
                    tt("dve", lt[:, s, :], psb[6][:, 0:4], bfb, ALU.add, ["ps6", "bfb"], ["lt"])
                act(lt[:, 0:nsub, :], lt[:, 0:nsub, :], AF.Exp, ["lt"], ["lt"], scale=-1.0)
                act(lt[:, 0:nsub, :], lt[:, 0:nsub, :], AF.Ln, ["lt"], ["lt"], bias=1.0)
                ts("dve", ztm[:, 0:nsub, 1024:1028], lt[:, 0:nsub, :], -1.0, None, ALU.mult, None, ["lt"], ["ztm"])
                cp("pool", vst[:, 0:nsub, 0:256], ztm[:, 0:nsub, 256:512], ["ztm"], ["vst"])
                cp("pool", vst[:, 0:nsub, 256:512], ztm[:, 0:nsub, 768:1024], ["ztm"], ["vst"])
                if t0 < T:
                    outs = (o_pfk, o_pfv, o_pdk, o_pdv, o_pflf)
                    r0 = t0
                else:
                    outs = (o_sfk, o_sfv, o_sdk, o_sdv, o_sflf)
                    r0 = 0
                for oi, o in enumerate(outs):
                    c0, cn = (oi * 256, 256) if oi < 4 else (1024, 4)
                    dma("pool", o[l, r0:r0 + n, :].rearrange("(s p) c -> p s c", p=128), ztm[:, 0:nsub, c0:c0 + cn],
                        ["ztm"], [])
                dma("sp", lfs[t0:t0 + n, :].rearrange("(s p) c -> p s c", p=128), ztm[:, 0:nsub, 1024:1028],
                    ["ztm"], [])
                dma("sp", vsf[t0:t0 + n, :].rearrange("(s p) c -> p s c", p=128), vst[:, 0:nsub, 0:256],
                    ["vst"], [])
                dma("sp", vsd[t0:t0 + n, :].rearrange("(s p) c -> p s c", p=128), vst[:, 0:nsub, 256:512],
                    ["vst"], [])
            P.barrier()

        def phase2_frows(l):
            A.reset()
            fl = A.alloc([4, T], F32)
            Fc = A.alloc([4, T], F32)
            fhi = A.alloc([4, T], BF16)
            flo = A.alloc([4, T], BF16)
            f8 = A.alloc([4, T], BF16)
            bfc = A.alloc([4, 1], F32)
            onesT = A.alloc([4, T], F32)
            dma("sp", fl, flT[:, 0:T], (), ["fl"])
            dma("sp", bfc, fox_b_f[l].rearrange("(h o) -> h o", o=1), (), ["bfc"])
            memset("pool", onesT, 1.0, ["onesT"])
            ts("dve", fl, fl, bfc[:, 0:1], -1.0, ALU.add, ALU.mult, ["fl", "bfc"], ["fl"])
            act(fl, fl, AF.Exp, ["fl"], ["fl"])
            act(fl, fl, AF.Ln, ["fl"], ["fl"], bias=1.0)
            ts("dve", fl, fl, -1.0, None, ALU.mult, None, ["fl"], ["fl"])
            P.emit("dve", lambda e: e.tensor_tensor_scan(out=Fc, data0=onesT, data1=fl, initial=0.0, op0=ALU.mult,
                                                         op1=ALU.add), ["fl", "onesT"], ["Fc"])
            cp("dve", fhi, Fc, ["Fc"], ["fhi"])
            tt("dve", flo, Fc, fhi, ALU.subtract, ["Fc", "fhi"], ["flo"])
            ts("dve", f8, fhi, 8.0, None, ALU.mult, None, ["fhi"], ["f8"])
            dma("sp", fka[:, 65, 0:T], fhi, ["fhi"], [])
            dma("sp", fka[:, 66, 0:T], flo, ["flo"], [])
            dma("sp", fqa[:, 64, 0:T], f8, ["f8"], [])
            P.barrier()

        def phase2_rec(l, sample):
            A.reset()
            S, TT, nt = (NS, TS, 1) if sample else (1, 1024, T // 1024)
            tb = T if sample else 0
            NW = S * TT
            cw = A.alloc([128, 2, 4], F32)
            cbias = A.alloc([128, 2], F32)
            ba = A.alloc([128, 2], F32)
            bx = A.alloc([128, 2], F32)
            lam = A.alloc([128, 2], F32)
            cl = A.alloc([128, 2], F32)
            scw = A.alloc([128, 2, 3], F32)
            wa_f = A.alloc([128, 2, 128], F32)
            wx_f = A.alloc([128, 2, 128], F32)
            wa_b = A.alloc([128, 2, 128], BF16)
            wx_b = A.alloc([128, 2, 128], BF16)
            for c_ in range(2):
                dma("sp", cw[:, c_, :], lru_conv_w[l, :, c_ * 128:(c_ + 1) * 128].rearrange("k p -> p k"), (), ["cw"])
            dma("sp", cbias, lru_conv_b[l].rearrange("(c p) -> p c", p=128), (), ["cw"])
            dma("sp", ba, lru_b_a[l].rearrange("(c p) -> p c", p=128), (), ["cw"])
            dma("sp", bx, lru_b_x[l].rearrange("(c p) -> p c", p=128), (), ["cw"])
            dma("sp", lam, lru_lambda[l].rearrange("(c p) -> p c", p=128), (), ["lam"])
            for c_ in range(2):
                dma("sp", scw[:, c_, :], sc_conv_w[l, :, c_ * 128:(c_ + 1) * 128].rearrange("k p -> p k"), (), ["cw"])
            memset("dve", wa_f, 0.0, ["wa_f"])
            memset("dve", wx_f, 0.0, ["wx_f"])
            for c in range(2):
                for b in range(2):
                    dma("sp", wa_f[b * 64:(b + 1) * 64, c, b * 64:(b + 1) * 64], lru_w_a[l, 2 * c + b], (), ["wa_f"])
                    dma("sp", wx_f[b * 64:(b + 1) * 64, c, b * 64:(b + 1) * 64], lru_w_x[l, 2 * c + b], (), ["wx_f"])
            cp("dve", wa_b, wa_f, ["wa_f"], ["wa_b"])
            cp("dve", wx_b, wx_f, ["wx_f"], ["wx_b"])
            act(cl, lam, AF.Exp, ["lam"], ["cl"], scale=-1.0)
            act(cl, cl, AF.Ln, ["cl"], ["cl"], bias=1.0)
            ts("dve", cl, cl, -8.0, None, ALU.mult, None, ["cl"], ["cl"])

            zb = A.alloc([128, NW], BF16)
            xl = A.alloc([128, S, 3 + TT], F32)
            xc = A.alloc([128, S, TT], F32)
            xcb = A.alloc([128, NW], BF16)
            rg = A.alloc([128, NW], F32)
            ig = A.alloc([128, NW], F32)
            av = A.alloc([128, NW], F32)
            uv = A.alloc([128, NW], F32)
            hv = A.alloc([128, NW], F32)
            gt = A.alloc([128, NW], BF16)
            g2 = A.alloc([128, NW], F32)
            yb = A.alloc([128, NW], BF16)
            h0 = A.alloc([128, S], F32)
            hist = A.alloc([128, S, 3], F32)
            cx = A.alloc([128, S, 2 + TT], F32)
            z2 = A.alloc([128, NW], BF16)
            z3 = A.alloc([128, NW], BF16)
            v3 = lambda ap: ap.rearrange("p (s t) -> p s t", t=TT)
            for c in range(2):
                r_ = slice(c * 128, (c + 1) * 128)
                for j in range(nt):
                    t0 = tb + j * NW
                    dma("sp", zb, zfm[c * 128:(c + 1) * 128, t0:t0 + NW], (), ["zb"])
                    if j > 0:
                        cp("dve", hist, xl[:, :, TT:TT + 3], ["xl"], ["hist"])
                    cp("dve", xl[:, :, 3:3 + TT], v3(zb), ["zb"], ["xl"])
                    if j > 0:
                        cp("dve", xl[:, :, 0:3], hist, ["hist"], ["xl"])
                    elif sample:
                        for k_ in range(3):
                            dma("sp", xl[:, :, k_], st_lru_conv[l, :, k_, r_].rearrange("s p -> p s"), (), ["xl"])
                        dma("sp", h0, st_lru_h[l, :, r_].rearrange("s p -> p s"), (), ["h0"])
                    else:
                        memset("dve", xl[:, :, 0:3], 0.0, ["xl"])
                    ts("dve", xc, xl[:, :, 0:TT], cw[:, c, 0:1], cbias[:, c:c + 1], ALU.mult, ALU.add, ["xl", "cw"], ["xc"])
                    for k in range(1, 4):
                        stt(xc, xl[:, :, k:k + TT], cw[:, c, k:k + 1], xc, ALU.mult, ALU.add, ["xl", "cw", "xc"], ["xc"])
                    cp("act", v3(xcb), xc, ["xc"], ["xcb"])
                    for hf in range(0, NW, 512):
                        n = min(512, NW - hf)
                        mm(psb[0][:, 0:n], wa_b[:, c, :], xcb[:, hf:hf + n], True, True, ["wa_b", "xcb"], ["ps0"])
                        act(rg[:, hf:hf + n], psb[0][:, 0:n], AF.Sigmoid, ["ps0", "cw"], ["rg"], bias=ba[:, c:c + 1])
                        mm(psb[1][:, 0:n], wx_b[:, c, :], xcb[:, hf:hf + n], True, True, ["wx_b", "xcb"], ["ps1"])
                        act(ig[:, hf:hf + n], psb[1][:, 0:n], AF.Sigmoid, ["ps1", "cw"], ["ig"], bias=bx[:, c:c + 1])
                    act(av, rg, AF.Exp, ["rg", "cl"], ["av"], scale=cl[:, c:c + 1])
                    tt("dve", uv, av, av, ALU.mult, ["av"], ["uv"])
                    ts("dve", uv, uv, -1.0, 1.0, ALU.mult, ALU.add, ["uv"], ["uv"])
                    ts("dve", uv, uv, 0.0, None, ALU.max, None, ["uv"], ["uv"])
                    act(uv, uv, AF.Sqrt, ["uv"], ["uv"])
                    tt("dve", uv, uv, ig, ALU.mult, ["uv", "ig"], ["uv"])
                    tt("dve", v3(uv), v3(uv), xc, ALU.mult, ["uv", "xc"], ["uv"])
                    if j > 0:
                        cp("dve", h0[:, 0:1], hv[:, NW - 1:NW], ["hv"], ["h0"])
                    for s in range(S):
                        init = 0.0 if (not sample and j == 0) else h0[:, s:s + 1]
                        P.emit("dve", lambda e, s=s, init=init: e.tensor_tensor_scan(
                            out=hv[:, s * TT:(s + 1) * TT], data0=av[:, s * TT:(s + 1) * TT],
                            data1=uv[:, s * TT:(s + 1) * TT], initial=init, op0=ALU.mult, op1=ALU.add),
                            ["av", "uv", "h0"], ["hv"])
                    dma("sp", gt, zfm[256 + c * 128:256 + (c + 1) * 128, t0:t0 + NW], (), ["gt"])
                    tt("pool", g2, gt, gt, ALU.mult, ["gt"], ["g2"])
                    ts("pool", g2, g2, 0.044715 * 0.7978845608028654, 0.7978845608028654, ALU.mult, ALU.add, ["g2"], ["g2"])
                    tt("pool", g2, g2, gt, ALU.mult, ["g2", "gt"], ["g2"])
                    act(g2, g2, AF.Tanh, ["g2"], ["g2"])
                    stt(g2, g2, 1.0, gt, ALU.add, ALU.mult, ["g2", "gt"], ["g2"])
                    stt(yb, g2, 0.5, hv, ALU.mult, ALU.mult, ["g2", "hv"], ["yb"])
                    dma("pool", mixT[c * 128:(c + 1) * 128, t0:t0 + NW], yb, ["yb"], [])
                if sample:
                    dma("pool", o_slh[l, :, r_].rearrange("s p -> p s"), v3(hv)[:, :, TT - 1], ["hv"], [])
                    for k_ in range(3):
                        dma("pool", o_slc[l, :, k_, r_].rearrange("s p -> p s"), xl[:, :, TT + k_], ["xl"], [])
                else:
                    dma("pool", o_plh[l, r_].rearrange("(p o) -> p o", o=1), hv[:, NW - 1:NW], ["hv"], [])
                    dma("pool", o_plc[l, :, r_].rearrange("k p -> p k"), xl[:, 0, TT:TT + 3], ["xl"], [])
                for j in range(nt):
                    t0 = tb + j * NW
                    dma("sp", zb, zfm[512 + c * 128:512 + (c + 1) * 128, t0:t0 + NW], (), ["zb"])
                    dma("sp", z2, zfm[768 + c * 128:768 + (c + 1) * 128, t0:t0 + NW], (), ["z2"])
                    dma("sp", z3, zfm[1024 + c * 128:1024 + (c + 1) * 128, t0:t0 + NW], (), ["z3"])
                    if j > 0:
                        cp("dve", hist[:, :, 0:2], cx[:, :, TT:TT + 2], ["cx"], ["hist"])
                    tt("dve", cx[:, :, 2:2 + TT], v3(z2), v3(z3), ALU.mult, ["z2", "z3"], ["cx"])
                    if j > 0:
                        cp("dve", cx[:, :, 0:2], hist[:, :, 0:2], ["hist"], ["cx"])
                    elif sample:
                        for k_ in range(2):
                            dma("sp", cx[:, :, k_], st_sconv[l, :, k_, r_].rearrange("s p -> p s"), (), ["cx"])
                    else:
                        memset("dve", cx[:, :, 0:2], 0.0, ["cx"])
                    ts("dve", xc, cx[:, :, 0:TT], scw[:, c, 0:1], None, ALU.mult, None, ["cx", "cw"], ["xc"])
                    for k in range(1, 3):
                        stt(xc, cx[:, :, k:k + TT], scw[:, c, k:k + 1], xc, ALU.mult, ALU.add, ["cx", "cw", "xc"], ["xc"])
                    tt("dve", v3(yb), xc, v3(zb), ALU.mult, ["xc", "zb"], ["yb"])
                    dma("pool", mixT[256 + c * 128:256 + (c + 1) * 128, t0:t0 + NW], yb, ["yb"], [])
                if sample:
                    for k_ in range(2):
                        dma("pool", o_ssc[l, :, k_, r_].rearrange("s p -> p s"), cx[:, :, TT + k_], ["cx"], [])
                else:
                    dma("pool", o_psc[l, :, r_].rearrange("k p -> p k"), cx[:, 0, TT:TT + 2], ["cx"], [])
            P.barrier()

        def lam_col(l, dst):
            lp = A.alloc([128, 128], F32)
            pr = A.alloc([128, 64], F32)
            sm = A.alloc([128, 2], F32)
            dma("sp", lp, diff_lambda[l].partition_broadcast(128), (), ["lp"])
            tt("dve", pr[:, 0:32], lp[:, 0:32], lp[:, 32:64], ALU.mult, ["lp"], ["pr"])
            tt("dve", pr[:, 32:64], lp[:, 64:96], lp[:, 96:128], ALU.mult, ["lp"], ["pr"])
            P.emit("dve", lambda e: e.reduce_sum(out=sm, in_=pr.rearrange("p (a b) -> p a b", b=32),
                                                 axis=mybir.AxisListType.X), ["pr"], ["sm"])
            act(sm, sm, AF.Exp, ["sm"], ["sm"])
            tt("dve", dst, sm[:, 0:1], sm[:, 1:2], ALU.subtract, ["sm"], ["lamc"])
            ts("dve", dst, dst, 0.8 - 0.6 * math.exp(-0.3 * l), None, ALU.add, None, ["lamc"], ["lamc"])

        def phase2_attn_prompt(l):
            A.reset()
            lam_init = 0.8 - 0.6 * math.exp(-0.3 * l)
            Qa = A.alloc([67, T], BF16)
            Ka = A.alloc([67, T], BF16)
            Va = A.alloc([128, T // 128, 128], BF16)
            pts = [A.alloc([128, 512], BF16) for _ in range(3)]
            rs = A.alloc([64, 512], F32)
            o1 = A.alloc([64, 512], F32)
            o2 = A.alloc([64, 512], F32)
            sq = A.alloc([64, 512], BF16)
            ys = [A.alloc([64, 512], BF16) for _ in range(2)]
            lamc = A.alloc([128, 1], F32)
            dg = A.alloc([64, 1], F32)
            o64 = A.alloc([64, 64], BF16)
            lam_col(l, lamc)
            dma("sp", dg, diff_norm_g[l].rearrange("(p o) -> p o", o=1), (), ["dg"])
            ts("dve", dg, dg, 1.0 - lam_init, None, ALU.mult, None, ["dg"], ["dg"])
            memset("pool", o64, 1.0 / 64.0, ["o64"])
            memset("pool", Va[:, :, 64:128], 1.0, ["Va1"])
            NQ = T // 512
            yi = [0]

            def run_map(qt, kq, kk, krows, scale, band, acc, acck, sti):
                q0 = qt * 512
                nkc = 4 * qt + 4
                steps = []
                for kc in range(nkc):
                    j = kc - 4 * qt
                    n0 = 128 * max(0, j)
                    steps.append((kc, j, n0, 512 - n0))

                def qk(i):
                    kc, j, n0, N = steps[i]
                    b = sti + (i % 2)
                    mm(psb[b][:, 0:N], kk[krows, kc * 128:(kc + 1) * 128], kq[krows, q0 + n0:q0 + 512], True, True,
                       ["Ka", "Qa"], ["ps%d" % b])

                qk(0)
                for i, (kc, j, n0, N) in enumerate(steps):
                    if i + 1 < len(steps):
                        qk(i + 1)
                    b = sti + (i % 2)
                    pt = pts[i % 3]
                    pk = "pt%d" % (i % 3)
                    act(pt[:, 0:N], psb[b][:, 0:N], AF.Exp, ["ps%d" % b], [pk], scale=scale)
                    if band is not None:
                        if j >= 0:
                            w = 256 if j <= 2 else 128
                            tt("pool", pt[:, 0:w], pt[:, 0:w], band[:, 0:w], ALU.mult, [pk, "eb", "tri"], [pk])
                        elif j == -1 and band is not tri:
                            tt("pool", pt[:, 0:128], pt[:, 0:128], band[:, 128:256], ALU.mult, [pk, "eb"], [pk])
                    mm(acc[:, n0:512], Va[:, kc, :], pt[:, 0:N], i == 0, i == len(steps) - 1, ["Va", "Va1", pk], [acck])

            for typ in ("fox", "diff"):
                for h in range(4):
                    if typ == "fox":
                        dma("sp", Qa, fqa[h, :, 0:T], (), ["Qa"])
                        dma("sp", Ka, fka[h, :, 0:T], (), ["Ka"])
                        vsrc = vsf
                    else:
                        dma("sp", Qa[0:64, :], dqs[h, :, 0:T], (), ["Qa"])
                        dma("sp", Ka[0:64, :], dks[h, :, 0:T], (), ["Ka"])
                        vsrc = vsd
                    dma("sp", Va[:, :, 0:64], vsrc[0:T, h * 64:(h + 1) * 64].rearrange("(c p) e -> p c e", p=128),
                        (), ["Va"])
                    for qt in range(NQ):
                        y = ys[yi[0] % 2]
                        yk = "ys%d" % (yi[0] % 2)
                        yi[0] += 1
                        if typ == "fox":
                            run_map(qt, Qa, Ka, slice(0, 67), 0.125, tri, psb[4], "ps4", 0)
                            P.emit("dve", lambda e: e.reciprocal(out=rs, in_=psb[4][64:128, :]), ["ps4"], ["rs"])
                            tt("dve", y, psb[4][0:64, :], rs, ALU.mult, ["ps4", "rs"], [yk])
                            dma("pool", mixT[512 + h * 64:512 + (h + 1) * 64, qt * 512:(qt + 1) * 512], y, [yk], [])
                        else:
                            sc_ = 32 ** -0.5
                            run_map(qt, Qa, Ka, slice(0, 32), sc_, eb[:, h, :], psb[4], "ps4", 0)
                            run_map(qt, Qa, Ka, slice(32, 64), sc_, eb[:, h, :], psb[5], "ps5", 2)
                            P.emit("dve", lambda e: e.reciprocal(out=rs, in_=psb[4][64:128, :]), ["ps4"], ["rs"])
                            tt("dve", o1, psb[4][0:64, :], rs, ALU.mult, ["ps4", "rs"], ["o1"])
                            P.emit("dve", lambda e: e.reciprocal(out=rs, in_=psb[5][64:128, :]), ["ps5"], ["rs"])
                            tt("dve", o2, psb[5][0:64, :], rs, ALU.mult, ["ps5", "rs"], ["o2"])
                            stt(o1, o2, lamc[0:64, 0:1], o1, ALU.mult, ALU.subtract, ["o1", "o2", "lamc"], ["o1"])
                            tt("pool", sq, o1, o1, ALU.mult, ["o1"], ["sq"])
                            mm(psb[6][0:64, :], o64, sq, True, True, ["o64", "sq"], ["ps6"])
                            ts("dve", rs, psb[6][0:64, :], EPS, None, ALU.add, None, ["ps6"], ["rs"])
                            act(rs, rs, AF.Sqrt, ["rs"], ["rs"])
                            P.emit("dve", lambda e: e.reciprocal(out=rs, in_=rs), ["rs"], ["rs"])
                            tt("dve", o1, o1, rs, ALU.mult, ["o1", "rs"], ["o1"])
                            ts("dve", y, o1, dg[:, 0:1], -1.0, ALU.mult, ALU.mult, ["o1", "dg"], [yk])
                            dma("pool", mixT[768 + h * 64:768 + (h + 1) * 64, qt * 512:(qt + 1) * 512], y, [yk], [])
            P.barrier()


        def phase2_attn_sample(l):
            A.reset()
            lam_init = 0.8 - 0.6 * math.exp(-0.3 * l)
            triF = A.alloc([128, 128], F32)
            onesF = A.alloc([128, 128], F32)
            memset("dve", onesF, 1.0, ["onesF"])
            memset("pool", triF, 1.0, ["triF"])
            P.emit("pool", lambda e: e.affine_select(out=triF, in_=triF, pattern=[[1, 128]], compare_op=ALU.is_ge,
                                                     fill=0.0, base=0, channel_multiplier=-1), ["triF"], ["triF"])
            pti = A.alloc([128, NS * NPG], I32)
            idxf = A.alloc([128, NS * NPG], F32)
            idxi = A.alloc([128, NS * NPG], I32)
            iopi = A.alloc([128, 1], I32)
            iop = A.alloc([128, 1], F32)
            dma("sp", pti, ptab.partition_broadcast(128), (), ["pti"])
            P.emit("pool", lambda e: e.iota(iopi, pattern=[[0, 1]], base=0, channel_multiplier=1), (), ["iopi"])
            cp("dve", iop, iopi, ["iopi"], ["iop"])
            cp("dve", idxf, pti, ["pti"], ["idxf"])
            ts("dve", idxf, idxf, 128.0, iop[:, 0:1], ALU.mult, ALU.add, ["idxf", "iop"], ["idxf"])
            ts("dve", idxf, idxf, float(l * NPOOL * 128), None, ALU.add, None, ["idxf"], ["idxf"])
            cp("dve", idxi, idxf, ["idxf"], ["idxi"])
            qs = A.alloc([64, 4, NTS], BF16)
            ks = A.alloc([64, 4, NTS], BF16)
            qd = A.alloc([64, 4, NTS], BF16)
            kd = A.alloc([64, 4, NTS], BF16)
            for h in range(4):
                dma("sp", qs[:, h, :], fqa[h, 0:64, T:NT], (), ["qs"])
                dma("sp", ks[:, h, :], fka[h, 0:64, T:NT], (), ["qs"])
                dma("sp", qd[:, h, :], dqs[h, :, T:NT], (), ["qs"])
                dma("sp", kd[:, h, :], dks[h, :, T:NT], (), ["qs"])
            lamc = A.alloc([128, 1], F32)
            lam_col(l, lamc)
            dg2 = A.alloc([128, 1], F32)
            for b in range(2):
                dma("sp", dg2[b * 64:(b + 1) * 64, :], diff_norm_g[l].rearrange("(p o) -> p o", o=1), (), ["dg2"])
            ts("dve", dg2, dg2, -(1.0 - lam_init), None, ALU.mult, None, ["dg2"], ["dg2"])
            o64 = A.alloc([128, 128], BF16)
            memset("pool", o64, 0.0, ["o64"])
            memset("pool", o64[0:64, 0:64], 1.0 / 64.0, ["o64"])
            memset("pool", o64[64:128, 64:128], 1.0 / 64.0, ["o64"])
            gk = A.alloc([128, NPG, 256], F32)
            gv = A.alloc([128, NPG, 256], F32)
            glf = A.alloc([128, NPG * 4], F32)
            bk = A.alloc([128, NPG, 256], BF16)
            bv = A.alloc([128, NPG, 256], BF16)
            ktT = [A.alloc([64, 512], BF16) for _ in range(2)]
            pt = [A.alloc([128, 64], BF16) for _ in range(3)]
            bias = A.alloc([128, NPG * 4], F32)
            wth = A.alloc([128, NPG * 4], F32)
            tot = A.alloc([128, NPG * 4], F32)
            inc = A.alloc([128, NPG * 4], F32)
            ones16 = A.alloc([128, NPG], F32)
            memset("dve", ones16, 1.0, ["ones16"])
            lfn = A.alloc([8, 4], F32)
            bnew = A.alloc([8, 4], F32)
            vnew = A.alloc([8, 256], BF16)
            rsum = A.alloc([128, 64], F32)
            yf = A.alloc([128, 2, NTS], BF16)
            od = A.alloc([128, 2, 2, NTS], F32)
            odn = A.alloc([128, 2, NTS], F32)
            sqb = A.alloc([128, 2, NTS], BF16)
            rst = A.alloc([128, 2 * NTS], F32)
            yd = A.alloc([128, 2, NTS], BF16)
            ti = [0]

            def gather(dst, cache, s, key):
                for j in range(NPG):
                    col = s * NPG + j
                    P.emit("pool", lambda e, j=j, col=col: e.indirect_dma_start(
                        out=dst[:, j, :] if len(dst.shape) == 3 else dst[:, j * 4:(j + 1) * 4], out_offset=None,
                        in_=cache, in_offset=bass.IndirectOffsetOnAxis(ap=idxi[:, col:col + 1], axis=0)),
                        ["idxi"], [key], dma=True)

            for s in range(NS):
                c0 = s * TS
                for typ in ("fox", "diff"):
                    fox = typ == "fox"
                    gather(gk, c_fk if fox else c_dk, s, "gk")
                    gather(gv, c_fv if fox else c_dv, s, "gv")
                    cp("dve", bk, gk, ["gk"], ["bk"])
                    cp("act", bv, gv, ["gv"], ["bv"])
                    dma("sp", vnew, (vsf if fox else vsd)[T + c0:T + c0 + TS, :], (), ["vnew"])
                    if fox:
                        gather(glf, c_flf, s, "glf")
                        mm(psb[2][:, 0:64], triF, glf, True, True, ["triF", "glf"], ["ps2"])
                        mm(psb[3][:, 0:64], onesF, glf, True, True, ["onesF", "glf"], ["ps3"])
                        cp("dve", wth, psb[2][:, 0:64], ["ps2"], ["wth"])
                        cp("dve", tot, psb[3][:, 0:64], ["ps3"], ["tot"])
                        for h in range(4):
                            P.emit("dve", lambda e, h=h: e.tensor_tensor_scan(
                                out=inc.rearrange("p (j h) -> p j h", h=4)[:, :, h], data0=ones16,
                                data1=tot.rearrange("p (j h) -> p j h", h=4)[:, :, h], initial=0.0, op0=ALU.mult,
                                op1=ALU.add), ["tot", "ones16"], ["inc"])
                        tt("dve", bias, tot, inc, ALU.subtract, ["tot", "inc"], ["bias"])
                        tt("dve", bias, bias, wth, ALU.subtract, ["bias", "wth"], ["bias"])
                        for h in range(4):
                            bh = bias.rearrange("p (j h) -> p j h", h=4)[:, :, h]
                            ts("dve", bh, bh, inc[:, 60 + h:61 + h], None, ALU.add, None, ["bias", "inc"], ["bias"])
                        dma("sp", lfn, lfs[T + c0:T + c0 + TS, :], (), ["lfn"])
                        mm(psb[2][0:8, 64:68], triF[0:8, 0:8], lfn, True, True, ["triF", "lfn"], ["ps2"])
                        ts("dve", bnew, psb[2][0:8, 64:68], -1.0, None, ALU.mult, None, ["ps2"], ["bnew"])
                    ncol = 32 if fox else 64
                    qq, kk = (qs, ks) if fox else (qd, kd)
                    for j in range(NPG + 1):
                        new = j == NPG
                        i = ti[0]
                        ti[0] += 1
                        sb_ = i % 2
                        sk_ = "ps%d" % sb_
                        p_ = pt[i % 3]
                        pk = "ptS%d" % (i % 3)
                        nk = 8 if new else 128
                        if not new:
                            kt = ktT[i % 2]
                            kk_ = "ktT%d" % (i % 2)
                            for h in range(4):
                                P.emit("pe", lambda e, h=h, j=j: e.transpose(
                                    out=pst[0:64, h * 128:(h + 1) * 128], in_=bk[:, j, h * 64:(h + 1) * 64],
                                    identity=ident), ["bk", "ident"], ["pst"])
                            evac(kt, pst[0:64, 0:512], ["pst"], [kk_])
                        for h in range(4):
                            for m in range(1 if fox else 2):
                                rows = slice(0, 64) if fox else slice(32 * m, 32 * m + 32)
                                lhs = kk[rows, h, c0:c0 + TS] if new else kt[rows, h * 128:(h + 1) * 128]
                                cc = (h * 8) if fox else (h * 2 + m) * 8
                                mm(psb[sb_][0:nk, cc:cc + 8], lhs, qq[rows, h, c0:c0 + TS], True, True,
                                   ["qs"] if new else [kk_, "qs"], [sk_])
                        if fox:
                            for h in range(4):
                                bcol = bnew[:, h:h + 1] if new else bias[:, j * 4 + h:j * 4 + h + 1]
                                act(p_[0:nk, h * 8:(h + 1) * 8], psb[sb_][0:nk, h * 8:(h + 1) * 8], AF.Exp,
                                    [sk_, "bias", "bnew"], [pk], bias=bcol, scale=0.125)
                            if new:
                                tt("dve", p_[0:8, 0:32].rearrange("p (h q) -> p h q", q=8),
                                   p_[0:8, 0:32].rearrange("p (h q) -> p h q", q=8),
                                   tri[0:8, 0:8].unsqueeze(1).broadcast_to([8, 4, 8]), ALU.mult, [pk, "tri"], [pk])
                        else:
                            act(p_[0:nk, 0:64], psb[sb_][0:nk, 0:64], AF.Exp, [sk_], [pk], scale=32 ** -0.5)
                            if new or j == NPG - 1:
                                for h in range(4):
                                    band = eb[0:8, h, 0:8] if new else eb[:, h, 128:136]
                                    v_ = p_[0:nk, h * 16:(h + 1) * 16].rearrange("p (m q) -> p m q", q=8)
                                    tt("dve", v_, v_, band.unsqueeze(1).broadcast_to([nk, 2, 8]), ALU.mult,
                                       [pk, "eb"], [pk])
                        first = j == 0
                        for hp in range(2):
                            lhs = vnew[:, hp * 128:(hp + 1) * 128] if new else bv[:, j, hp * 128:(hp + 1) * 128]
                            mm(psb[4 + hp][:, 0:ncol], lhs, p_[0:nk, 0:ncol], first, new,
                               (["vnew"] if new else ["bv"]) + [pk], ["ps%d" % (4 + hp)])
                        mm(psb[6][:, 0:ncol], ones_b[0:nk, :], p_[0:nk, 0:ncol], first, new, ["ones_b", pk], ["ps6"])
                    P.emit("dve", lambda e, ncol=ncol: e.reciprocal(out=rsum[:, 0:ncol], in_=psb[6][:, 0:ncol]),
                           ["ps6"], ["rsum"])
                    for h in range(4):
                        pr = slice((h % 2) * 64, (h % 2) * 64 + 64)
                        ab = psb[4 + h // 2]
                        ak = "ps%d" % (4 + h // 2)
                        if fox:
                            tt("dve", yf[pr, h // 2, c0:c0 + TS], ab[pr, h * 8:(h + 1) * 8], rsum[pr, h * 8:(h + 1) * 8],
                               ALU.mult, [ak, "rsum"], ["yf"])
                        else:
                            for m in range(2):
                                cc = (h * 2 + m) * 8
                                tt("dve", od[pr, m, h // 2, c0:c0 + TS], ab[pr, cc:cc + 8], rsum[pr, cc:cc + 8],
                                   ALU.mult, [ak, "rsum"], ["od"])
            stt(odn, od[:, 1], lamc[:, 0:1], od[:, 0], ALU.mult, ALU.subtract, ["od", "lamc"], ["odn"])
            tt("dve", sqb, odn, odn, ALU.mult, ["odn"], ["sqb"])
            mm(psb[0][:, 0:2 * NTS], o64, sqb.rearrange("p a t -> p (a t)"), True, True, ["o64", "sqb"], ["ps0"])
            ts("dve", rst, psb[0][:, 0:2 * NTS], EPS, None, ALU.add, None, ["ps0"], ["rst"])
            act(rst, rst, AF.Sqrt, ["rst"], ["rst"])
            P.emit("dve", lambda e: e.reciprocal(out=rst, in_=rst), ["rst"], ["rst"])
            tt("dve", odn, odn, rst.rearrange("p (a t) -> p a t", t=NTS), ALU.mult, ["odn", "rst"], ["odn"])
            ts("dve", yd, odn, dg2[:, 0:1], None, ALU.mult, None, ["odn", "dg2"], ["yd"])
            for c in range(2):
                dma("pool", mixT[512 + c * 128:512 + (c + 1) * 128, T:NT], yf[:, c, :], ["yf"], [])
                dma("pool", mixT[768 + c * 128:768 + (c + 1) * 128, T:NT], yd[:, c, :], ["yd"], [])
            P.barrier()

        def phase3(l):
            A.reset()
            wA = A.alloc([128, 8, DFF], BF16)
            wB = A.alloc([128, 8, DFF], BF16)
            wC = A.alloc([128, 8, D], BF16)
            W = dict(junk=A.alloc([128, 1024], F32), ssq=A.alloc([128, 4], F32), rstd=A.alloc([128, 4], F32),
                     xsb=A.alloc([128, 4, 1024], BF16))
            xt = A.alloc([128, 4, 1024], F32)
            xnT = A.alloc([128, 8, 512], BF16)
            mt = A.alloc([128, 8, 512], BF16)
            sg = [A.alloc([128, 512], F32) for _ in range(2)]
            ast = [A.alloc([128, 512], BF16) for _ in range(3)]
            load_w(wC, wb_out[l], (("wb", id(wb_out), l), "wC"))
            load_w(wA, wb_gate[l], (("wb", id(wb_gate), l), "wA"))
            load_w(wB, wb_up[l], (("wb", id(wb_up), l), "wB"))
            load_gb(norm_ffn_g[l])
            for (t0, n) in tiles:
                nsub = n // 128
                if l == 0:
                    src = xp[t0:t0 + n, :] if t0 < T else xs
                else:
                    src = xres[t0:t0 + n, :]
                dma("sp", xt[:, 0:nsub, :], src.rearrange("(s p) d -> p s d", p=128), ["xres"], ["xt"])
                dma("sp", mt[:, :, 0:n], mixT[:, t0:t0 + n].rearrange("(k p) t -> p k t", p=128), (), ["mt"])
                for s in range(nsub):
                    for hf in range(2):
                        b = (2 * s + hf) % 4
                        for k in range(8):
                            mm(psb[b], mt[:, k, s * 128:(s + 1) * 128], wC[:, k, hf * 512:(hf + 1) * 512], k == 0, k == 7,
                               ["mt", "wC"], ["ps%d" % b])
                        tt("dve", xt[:, s, hf * 512:(hf + 1) * 512], xt[:, s, hf * 512:(hf + 1) * 512], psb[b], ALU.add,
                           ["xt", "ps%d" % b], ["xt"])
                dma("pool", xres[t0:t0 + n, :].rearrange("(s p) d -> p s d", p=128), xt[:, 0:nsub, :], ["xt"], ["xres"])
                rmsnorm_T(xt, nsub, xnT, W)
                for fc in range(NFC):
                    bg = (2 * fc) % 4
                    bu = bg + 1
                    for k in range(8):
                        mm(psb[bg][:, 0:n], wA[:, k, fc * 128:(fc + 1) * 128], xnT[:, k, 0:n], k == 0, k == 7,
                           ["wA", "xnT"], ["ps%d" % bg])
                    for k in range(8):
                        mm(psb[bu][:, 0:n], wB[:, k, fc * 128:(fc + 1) * 128], xnT[:, k, 0:n], k == 0, k == 7,
                           ["wB", "xnT"], ["ps%d" % bu])
                    sgt = sg[fc % 2]
                    sgk = "sg%d" % (fc % 2)
                    act(sgt[:, 0:n], psb[bg][:, 0:n], AF.Silu, ["ps%d" % bg], [sgk])
                    a_ = ast[fc % 3]
                    ak = "ast%d" % (fc % 3)
                    tt("dve", a_[:, 0:n], sgt[:, 0:n], psb[bu][:, 0:n], ALU.mult, [sgk, "ps%d" % bu], [ak])
                    dma("pool" if fc % 2 else "sp", aT[fc * 128:(fc + 1) * 128, t0:t0 + n], a_[:, 0:n], [ak], [])
            P.barrier()

        def phase4(l):
            A.reset()
            wA = A.alloc([128, NFC, D], BF16)
            xt = A.alloc([128, 4, 1024], F32)
            at = A.alloc([128, NFC, 512], BF16)
            W = dict(junk=A.alloc([128, 1024], F32), ssq=A.alloc([128, 4], F32), rstd=A.alloc([128, 4], F32))
            gfb = A.alloc([128, 1024], F32)
            load_w(wA, wb_down[l], (("wb", id(wb_down), l), "wA"))
            last = (l == L - 1)
            if last:
                dma("sp", gfb, norm_final_g.partition_broadcast(128), (), ["gfb"])
            for (t0, n) in tiles:
                nsub = n // 128
                dma("sp", xt[:, 0:nsub, :], xres[t0:t0 + n, :].rearrange("(s p) d -> p s d", p=128), ["xres"], ["xt"])
                dma("sp", at[:, :, 0:n], aT[:, t0:t0 + n].rearrange("(k p) t -> p k t", p=128), (), ["at"])
                for s in range(nsub):
                    for hf in range(2):
                        b = (2 * s + hf) % 4
                        for k in range(NFC):
                            mm(psb[b], at[:, k, s * 128:(s + 1) * 128], wA[:, k, hf * 512:(hf + 1) * 512], k == 0,
                               k == NFC - 1, ["at", "wA"], ["ps%d" % b])
                        tt("dve", xt[:, s, hf * 512:(hf + 1) * 512], xt[:, s, hf * 512:(hf + 1) * 512], psb[b], ALU.add,
                           ["xt", "ps%d" % b], ["xt"])
                if not last:
                    dma("pool", xres[t0:t0 + n, :].rearrange("(s p) d -> p s d", p=128), xt[:, 0:nsub, :], ["xt"], ["xres"])
                else:
                    for s in range(nsub):
                        act(W["junk"], xt[:, s, :], AF.Square, ["xt"], ["junk", "ssq"], accum_out=W["ssq"][:, s:s + 1])
                    ts("dve", W["rstd"][:, 0:nsub], W["ssq"][:, 0:nsub], 1.0 / D, EPS, ALU.mult, ALU.add, ["ssq"], ["rstd"])
                    act(W["rstd"][:, 0:nsub], W["rstd"][:, 0:nsub], AF.Sqrt, ["rstd"], ["rstd"])
                    P.emit("dve", lambda e, nsub=nsub: e.reciprocal(out=W["rstd"][:, 0:nsub], in_=W["rstd"][:, 0:nsub]),
                           ["rstd"], ["rstd"])
                    for s in range(nsub):
                        stt(xt[:, s, :], xt[:, s, :], W["rstd"][:, s:s + 1], gfb, ALU.mult, ALU.mult,
                            ["xt", "rstd", "gfb"], ["xt"])
                    dst = o_yp[t0:t0 + n, :] if t0 < T else o_ys
                    dma("pool", dst.rearrange("(s p) d -> p s d", p=128), xt[:, 0:nsub, :], ["xt"], [])
            P.barrier()

        for l in range(L):
            phase1(l)
            phase2_frows(l)
            phase2_rec(l, False)
            phase2_rec(l, True)
            phase2_attn_prompt(l)
            if sample_attn:
                phase2_attn_sample(l)
            phase3(l)
            phase4(l)

        P.replay(nc, st)
    return nc


_IN_NAMES = ["w_in", "w_out", "w_gate", "w_up", "w_down", "norm_mix_g", "norm_ffn_g", "norm_final_g", "lru_conv_w",
             "lru_conv_b", "lru_w_a", "lru_b_a", "lru_w_x", "lru_b_x", "lru_lambda", "sc_conv_w", "fox_b_f",
             "diff_norm_g", "rel_bias"]


def make_in_maps(inp, cores, sample_attn=True):
    f = lambda a: np.ascontiguousarray(np.asarray(a, dtype=np.float32))
    shared = {k: f(inp[k]) for k in _IN_NAMES}
    shared["diff_lambda"] = f(inp["diff_lambda"]).reshape(L, 128)
    shared["c_t5"] = t5_onehot()
    if sample_attn:
        shared["c_fk"] = f(inp["cache_fox_k"]).reshape(L * NPOOL * 128, 256)
        shared["c_fv"] = f(inp["cache_fox_v"]).reshape(L * NPOOL * 128, 256)
        shared["c_flf"] = f(inp["cache_fox_logf"]).reshape(L * NPOOL * 128, 4)
        shared["c_dk"] = f(inp["cache_diff_k"]).reshape(L * NPOOL * 128, 256)
        shared["c_dv"] = f(inp["cache_diff_v"]).reshape(L * NPOOL * 128, 256)
    maps = []
    for c in cores:
        m = dict(shared)
        sl = slice(c * NS, (c + 1) * NS)
        m["xp"] = f(inp["x_prompt"][c % 4])
        m["xs"] = f(inp["x_sample"][sl]).reshape(NTS, D)
        m["st_lru_h"] = f(np.asarray(inp["state_lru_h"])[:, sl])
        m["st_lru_conv"] = f(np.asarray(inp["state_lru_conv"])[:, sl])
        m["st_sconv"] = f(np.asarray(inp["state_sconv"])[:, sl])
        if sample_attn:
            m["ptab"] = np.ascontiguousarray(np.asarray(inp["page_table"], dtype=np.int32)[sl]).reshape(NS * NPG)
        maps.append(m)
    return maps


def assemble(results, cores):
    B = 4
    G = 128
    out = {}
    yp = np.zeros((B, T, D), np.float32)
    ys = np.zeros((G, TS, D), np.float32)
    pk = {n: np.zeros((L, B, T, 256 if n != "flf" else 4), np.float32) for n in ("fk", "fv", "flf", "dk", "dv")}
    plh = np.zeros((L, B, 256), np.float32)
    plc = np.zeros((L, B, 3, 256), np.float32)
    psc = np.zeros((L, B, 2, 256), np.float32)
    sk = {n: np.zeros((L, G, TS, 256 if n != "flf" else 4), np.float32) for n in ("fk", "fv", "flf", "dk", "dv")}
    slh = np.zeros((L, G, 256), np.float32)
    slc = np.zeros((L, G, 3, 256), np.float32)
    ssc = np.zeros((L, G, 2, 256), np.float32)
    for r, c in zip(results, cores):
        sl = slice(c * NS, (c + 1) * NS)
        if c < 4:
            yp[c] = r["o_yp"]
            for n in pk:
                pk[n][:, c] = r["o_p" + n]
            plh[:, c] = r["o_plh"]
            plc[:, c] = r["o_plc"]
            psc[:, c] = r["o_psc"]
        ys[sl] = r["o_ys"].reshape(NS, TS, D)
        for n in sk:
            sk[n][:, sl] = r["o_s" + n].reshape(L, NS, TS, -1)
        slh[:, sl] = r["o_slh"]
        slc[:, sl] = r["o_slc"]
        ssc[:, sl] = r["o_ssc"]
    return (yp, ys,
            pk["fk"].reshape(L, B, T, 4, 64), pk["fv"].reshape(L, B, T, 4, 64), pk["flf"],
            pk["dk"].reshape(L, B, T, 4, 64), pk["dv"].reshape(L, B, T, 4, 64), plh, plc, psc,
            sk["fk"].reshape(L, G, TS, 4, 64), sk["fv"].reshape(L, G, TS, 4, 64), sk["flf"],
            sk["dk"].reshape(L, G, TS, 4, 64), sk["dv"].reshape(L, G, TS, 4, 64), slh, slc, ssc)


def kernel(**inputs):
    cores = list(range(N_CORES))
    nc = build_nc(sample_attn=True)
    in_maps = make_in_maps(inputs, cores, sample_attn=True)
    res = run_bass_kernel_spmd(nc, in_maps, core_ids=cores)
    return assemble(res.results, cores)
```

```python
import math
from contextlib import ExitStack

import numpy as np
import concourse.bass as bass
import concourse.mybir as mybir
from concourse.bass_utils import run_bass_kernel_spmd

F32 = mybir.dt.float32
BF16 = mybir.dt.bfloat16
I32 = mybir.dt.int32
U32 = mybir.dt.uint32
AF = mybir.ActivationFunctionType
ALU = mybir.AluOpType

ENGS = ("pe", "act", "dve", "pool", "sp")
N_DMA_SEMS = 6

L = 2
D = 1024
T = 4096
NS = 16
TS = 8
NTS = NS * TS
NT = T + NTS
NIN = 2820
DFF = 2816
NFC = DFF // 128
PAST = 2048
NPG = 16
NPOOL = 2560
EPS = 1e-6
N_CORES = 8


class Prog:
    def __init__(self):
        self.ops = {e: [] for e in ENGS}
        self.cnt = {}
        self.last_w = {}
        self.readers = {}
        self.waited = {e: {} for e in ENGS}
        self.rr = {e: 0 for e in ENGS}
        self.pending = {e: [] for e in ENGS}

    def barrier(self):
        cur = [(sk, v) for sk, v in self.cnt.items() if v > 0]
        for e in ENGS:
            self.pending[e] = list(cur)
        self.last_w = {}
        self.readers = {}

    def emit(self, eng, fn, reads=(), writes=(), dma=False):
        psr = [k for k in reads if isinstance(k, str) and k.startswith("ps") and k not in writes]
        if psr:
            writes = list(writes) + psr
        deps = list(self.pending[eng])
        self.pending[eng] = []
        for k in reads:
            t = self.last_w.get(k)
            if t is not None:
                deps.append(t)
        for k in writes:
            t = self.last_w.get(k)
            if t is not None:
                deps.append(t)
            deps.extend(self.readers.get(k, ()))
        if dma:
            slot = self.rr[eng]
            self.rr[eng] = (slot + 1) % N_DMA_SEMS
            sk = ("dma", eng, slot)
            inc = 16
            if self.cnt.get(sk, 0) > 0:
                deps.append((sk, self.cnt[sk]))
        else:
            sk = ("eng", eng)
            inc = 1
        val = self.cnt.get(sk, 0) + inc
        self.cnt[sk] = val
        tok = (sk, val)
        waits = []
        wd = self.waited[eng]
        need = {}
        for (dk, dv) in deps:
            if eng == "pe" and dk == ("eng", "pe"):
                continue
            if dv > need.get(dk, 0):
                need[dk] = dv
        for dk, dv in need.items():
            if wd.get(dk, 0) < dv:
                wd[dk] = dv
                waits.append((dk, dv))
        self.ops[eng].append((fn, waits, sk, inc))
        for k in writes:
            self.last_w[k] = tok
            self.readers[k] = []
        for k in reads:
            self.readers.setdefault(k, []).append(tok)
        return tok

    def replay(self, nc, stack):
        sems = {}
        for sk in self.cnt:
            sems[sk] = stack.enter_context(nc.semaphore("s_" + "_".join(str(x) for x in sk)))
        block = stack.enter_context(nc.Block())
        finals = [(sk, v) for sk, v in self.cnt.items()]
        sig = {}
        for en in ENGS:
            for (fn, waits, sk, inc) in self.ops[en]:
                for (wk, wv) in waits:
                    if wk[0] == "eng":
                        sig.setdefault(wk, set()).add(wv)
        for (fk, fv) in finals:
            if fk[0] == "eng":
                sig.setdefault(fk, set()).add(fv)
        rank = {k: {v: i + 1 for i, v in enumerate(sorted(vs))} for k, vs in sig.items()}

        def wval(wk, wv):
            return rank[wk][wv] if wk[0] == "eng" else wv

        def mk(eng_name):
            def body(e):
                idx = {}
                for (fn, waits, sk, inc) in self.ops[eng_name]:
                    for (wk, wv) in waits:
                        e.wait_ge(sems[wk], wval(wk, wv))
                    ins = fn(e)
                    if sk[0] == "eng":
                        idx[sk] = idx.get(sk, 0) + 1
                        if idx[sk] in sig.get(sk, ()):
                            ins.then_inc(sems[sk], 1)
                    else:
                        ins.then_inc(sems[sk], inc)
                if eng_name == "sp":
                    for (fk, fv) in finals:
                        e.wait_ge(sems[fk], wval(fk, fv))
            return body

        block.tensor(mk("pe"))
        block.scalar(mk("act"))
        block.vector(mk("dve"))
        block.gpsimd(mk("pool"))
        block.sync(mk("sp"))


class Arena:
    def __init__(self, ap, words):
        self.ap = ap
        self.words = words
        self.off = 0

    def reset(self):
        self.off = 0

    def alloc(self, shape, dtype, parts=128):
        n = 1
        for s in shape[1:]:
            n *= s
        w = n if dtype in (F32, I32, U32) else (n + 1) // 2
        w = (w + 7) // 8 * 8
        assert self.off + w <= self.words, ("arena overflow", self.off, w, self.words)
        v = self.ap[0:shape[0], self.off:self.off + w]
        self.off += w
        if dtype != F32:
            v = v.bitcast(dtype)
        v = v[:, 0:n]
        if len(shape) == 3:
            v = v.rearrange("p (a b) -> p a b", b=shape[2])
        elif len(shape) == 4:
            v = v.rearrange("p (a b c) -> p a b c", b=shape[2], c=shape[3])
        return v


def t5_onehot():
    oh = np.zeros((32, 384), np.float32)
    for j in range(383):
        rel = j - 127
        if rel < 0:
            continue
        if rel < 16:
            b = rel
        else:
            b = 16 + int(np.float32(np.log(np.float32(rel) / np.float32(16.0)) / np.float32(math.log(8.0))
                                    * np.float32(16.0)))
            b = min(b, 31)
        oh[b, j] += 1.0
        oh[31, j] -= 1.0
    return oh


def build_nc(sample_attn=True, dbg_only_sample=0):
    nc = bass.Bass("TRN2", target_bir_lowering=False)
    P = Prog()

    def din(name, shape, dt=F32):
        return nc.dram_tensor(name, list(shape), dt, kind="ExternalInput").ap()

    def dout(name, shape, dt=F32):
        return nc.dram_tensor(name, list(shape), dt, kind="ExternalOutput").ap()

    def dscr(name, shape, dt):
        return nc.dram_tensor(name, list(shape), dt, kind="Internal").ap()

    xp = din("xp", [T, D])
    xs = din("xs", [NTS, D])
    w_in = din("w_in", [L, D, NIN])
    w_out = din("w_out", [L, D, D])
    w_gate = din("w_gate", [L, D, DFF])
    w_up = din("w_up", [L, D, DFF])
    w_down = din("w_down", [L, DFF, D])
    norm_mix_g = din("norm_mix_g", [L, D])
    norm_ffn_g = din("norm_ffn_g", [L, D])
    norm_final_g = din("norm_final_g", [D])
    lru_conv_w = din("lru_conv_w", [L, 4, 256])
    lru_conv_b = din("lru_conv_b", [L, 256])
    lru_w_a = din("lru_w_a", [L, 4, 64, 64])
    lru_b_a = din("lru_b_a", [L, 256])
    lru_w_x = din("lru_w_x", [L, 4, 64, 64])
    lru_b_x = din("lru_b_x", [L, 256])
    lru_lambda = din("lru_lambda", [L, 256])
    sc_conv_w = din("sc_conv_w", [L, 3, 256])
    fox_b_f = din("fox_b_f", [L, 4])
    diff_lambda = din("diff_lambda", [L, 128])
    diff_norm_g = din("diff_norm_g", [L, 64])
    rel_bias = din("rel_bias", [32, 4])
    c_t5 = din("c_t5", [32, 384])
    st_lru_h = din("st_lru_h", [L, NS, 256])
    st_lru_conv = din("st_lru_conv", [L, NS, 3, 256])
    st_sconv = din("st_sconv", [L, NS, 2, 256])
    if sample_attn:
        c_fk = din("c_fk", [L * NPOOL * 128, 256])
        c_fv = din("c_fv", [L * NPOOL * 128, 256])
        c_flf = din("c_flf", [L * NPOOL * 128, 4])
        c_dk = din("c_dk", [L * NPOOL * 128, 256])
        c_dv = din("c_dv", [L * NPOOL * 128, 256])
        ptab = din("ptab", [NS * NPG], I32)

    o_yp = dout("o_yp", [T, D])
    o_ys = dout("o_ys", [NTS, D])
    o_pfk = dout("o_pfk", [L, T, 256])
    o_pfv = dout("o_pfv", [L, T, 256])
    o_pflf = dout("o_pflf", [L, T, 4])
    o_pdk = dout("o_pdk", [L, T, 256])
    o_pdv = dout("o_pdv", [L, T, 256])
    o_plh = dout("o_plh", [L, 256])
    o_plc = dout("o_plc", [L, 3, 256])
    o_psc = dout("o_psc", [L, 2, 256])
    o_sfk = dout("o_sfk", [L, NTS, 256])
    o_sfv = dout("o_sfv", [L, NTS, 256])
    o_sflf = dout("o_sflf", [L, NTS, 4])
    o_sdk = dout("o_sdk", [L, NTS, 256])
    o_sdv = dout("o_sdv", [L, NTS, 256])
    o_slh = dout("o_slh", [L, NS, 256])
    o_slc = dout("o_slc", [L, NS, 3, 256])
    o_ssc = dout("o_ssc", [L, NS, 2, 256])

    wb_in = dscr("wb_in", [L, D, NIN], BF16)
    wb_out = dscr("wb_out", [L, D, D], BF16)
    wb_gate = dscr("wb_gate", [L, D, DFF], BF16)
    wb_up = dscr("wb_up", [L, D, DFF], BF16)
    wb_down = dscr("wb_down", [L, DFF, D], BF16)
    xres = dscr("xres", [NT, D], F32)
    zfm = dscr("zfm", [1280, NT], BF16)
    fqa = dscr("fqa", [4, 67, NT], BF16)
    fka = dscr("fka", [4, 67, NT], BF16)
    dqs = dscr("dqs", [4, 64, NT], BF16)
    dks = dscr("dks", [4, 64, NT], BF16)
    flT = dscr("flT", [4, NT], F32)
    vsf = dscr("vsf", [NT, 256], BF16)
    vsd = dscr("vsd", [NT, 256], BF16)
    lfs = dscr("lfs", [NT, 4], F32)
    mixT = dscr("mixT", [D, NT], BF16)
    aT = dscr("aT", [DFF, NT], BF16)
    t5s = dscr("t5s", [4, 384], BF16)
    t5b = dscr("t5b", [4, 128, 384], BF16)

    tiles = [(i * 512, 512) for i in range(T // 512)] + [(T, NTS)]

    with ExitStack() as st:
        st.enter_context(nc.allow_non_contiguous_dma(reason="layout transforms"))
        st.enter_context(nc.allow_low_precision(reason="bf16 matmul operands per problem tolerance"))
        AW = 42000
        arena_t = st.enter_context(nc.sbuf_tensor("arena", [128, AW], F32))
        A = Arena(arena_t, AW)
        cst_t = st.enter_context(nc.sbuf_tensor("cst", [128, 2600], F32))
        C = Arena(cst_t, 2600)
        ident = C.alloc([128, 128], BF16)
        tri = C.alloc([128, 256], BF16)
        ones_b = C.alloc([128, 128], BF16)
        ones_f = C.alloc([128, 64], F32)
        gb = C.alloc([128, 8, 128], BF16)
        gcol = C.alloc([128, 8], F32)
        small = C.alloc([128, 256], F32)
        eb = C.alloc([128, 4, 256], BF16)
        psb = [st.enter_context(nc.psum_tensor("ps%d" % i, [128, 512], F32))[:, :] for i in range(7)]
        pst = st.enter_context(nc.psum_tensor("pst", [128, 1024], BF16))[:, :]

        def dma(q, out, in_, reads=(), writes=()):
            return P.emit(q, lambda e: e.dma_start(out=out, in_=in_), reads, writes, dma=True)

        def mm(out, lhsT, rhs, start, stop, reads, writes):
            return P.emit("pe", lambda e: e.matmul(out, lhsT=lhsT, rhs=rhs, start=start, stop=stop), reads, writes)

        def act(out, in_, func, reads, writes, bias=None, scale=None, accum_out=None):
            kw = {}
            if bias is not None:
                kw["bias"] = bias
            if scale is not None:
                kw["scale"] = scale
            if accum_out is not None:
                kw["accum_out"] = accum_out
            return P.emit("act", lambda e: e.activation(out=out, in_=in_, func=func, **kw), reads, writes)

        def ts(eng, out, in0, s1, s2, op0, op1, reads, writes):
            if s2 is None:
                return P.emit(eng, lambda e: e.tensor_scalar(out=out, in0=in0, scalar1=s1, scalar2=None, op0=op0),
                              reads, writes)
            return P.emit(eng, lambda e: e.tensor_scalar(out=out, in0=in0, scalar1=s1, scalar2=s2, op0=op0, op1=op1),
                          reads, writes)

        def tt(eng, out, in0, in1, op, reads, writes):
            return P.emit(eng, lambda e: e.tensor_tensor(out=out, in0=in0, in1=in1, op=op), reads, writes)

        def stt(out, in0, scalar, in1, op0, op1, reads, writes):
            return P.emit("dve", lambda e: e.scalar_tensor_tensor(out=out, in0=in0, scalar=scalar, in1=in1,
                                                                   op0=op0, op1=op1), reads, writes)

        def cp(eng, out, in_, reads, writes):
            if eng == "act":
                return P.emit("act", lambda e: e.copy(out=out, in_=in_), reads, writes)
            return P.emit(eng, lambda e: e.tensor_copy(out=out, in_=in_), reads, writes)

        def memset(eng, ap, val, writes):
            return P.emit(eng, lambda e: e.memset(ap, val), (), writes)

        memset("pool", ones_b, 1.0, ["ones_b"])
        memset("pool", ones_f, 1.0, ["ones_f"])
        memset("pool", ident, 1.0, ["ident"])
        P.emit("pool", lambda e: e.affine_select(out=ident, in_=ident, pattern=[[-1, 128]], compare_op=ALU.is_equal,
                                                 fill=0.0, base=0, channel_multiplier=1), ["ident"], ["ident"])
        memset("pool", tri, 1.0, ["tri"])
        P.emit("pool", lambda e: e.affine_select(out=tri[:, 0:128], in_=tri[:, 0:128], pattern=[[1, 128]],
                                                 compare_op=ALU.is_ge, fill=0.0, base=0, channel_multiplier=-1),
               ["tri"], ["tri"])
        for l in range(L if not dbg_only_sample else 0):
            for (src, dst, rows) in ((w_in, wb_in, D), (w_out, wb_out, D), (w_gate, wb_gate, D), (w_up, wb_up, D),
                                     (w_down, wb_down, DFF)):
                nsp = 4
                rr_ = rows // nsp
                for i in range(nsp):
                    dma("pool", dst[l, i * rr_:(i + 1) * rr_, :], src[l, i * rr_:(i + 1) * rr_, :],
                        writes=[("wb", id(dst), l)])
        cr = A.alloc([4, NT], BF16)
        memset("dve", cr, 1.0, ["cr"])
        dma("sp", fka[:, 64, :], cr, ["cr"], [])
        cr2 = A.alloc([4, NT], BF16)
        memset("dve", cr2, -8.0, ["cr2"])
        dma("sp", fqa[:, 65, :], cr2, ["cr2"], [])
        dma("sp", fqa[:, 66, :], cr2, ["cr2"], [])
        rb = A.alloc([32, 4], F32)
        oh = A.alloc([32, 384], F32)
        dma("sp", rb, rel_bias, (), ["rb"])
        dma("sp", oh, c_t5, (), ["oh"])
        mm(psb[0][0:4, 0:384], rb, oh, True, True, ["rb", "oh"], ["ps0"])
        gex = A.alloc([4, 384], BF16)
        act(gex, psb[0][0:4, 0:384], AF.Exp, ["ps0"], ["gex"])
        memset("dve", gex[:, 0:127], 0.0, ["gex"])
        dma("sp", t5s, gex, ["gex"], ["t5s"])
        for h in range(4):
            bsrc = bass.AP(tensor=t5s.tensor, offset=t5s.offset + h * 384, ap=[[0, 128], [1, 384]])
            dma("sp", t5b[h], bsrc, ["t5s"], ["t5b"])
            src = bass.AP(tensor=t5b.tensor, offset=t5b.offset + h * 128 * 384 + 127, ap=[[383, 128], [1, 256]])
            dma("sp", eb[:, h, :], src, ["t5b"], ["eb"])
        P.barrier()

        def load_gb(gsrc):
            dma("sp", gcol, gsrc.rearrange("(c p) -> p c", p=128), (), ["gcol"])
            for c in range(8):
                ts("dve", gb[:, c, :], ones_b, gcol[:, c:c + 1], None, ALU.mult, None, ["gcol", "ones_b"], ["gb"])

        def rmsnorm_T(xt, nsub, xnT, W, xk="xt"):
            for s in range(nsub):
                act(W["xsb"][:, s, :], xt[:, s, :], AF.Square, [xk], ["xsb", "ssq"], accum_out=W["ssq"][:, s:s + 1])
            ts("dve", W["rstd"][:, 0:nsub], W["ssq"][:, 0:nsub], 1.0 / D, EPS, ALU.mult, ALU.add, ["ssq"], ["rstd"])
            act(W["rstd"][:, 0:nsub], W["rstd"][:, 0:nsub], AF.Ln, ["rstd"], ["rstd"])
            act(W["rstd"][:, 0:nsub], W["rstd"][:, 0:nsub], AF.Exp, ["rstd"], ["rstd"], scale=-0.5)
            for s in range(nsub):
                ts("dve", W["xsb"][:, s, :], xt[:, s, :], W["rstd"][:, s:s + 1], None,
                   ALU.mult, None, [xk, "rstd"], ["xsb"])
            for s in range(nsub):
                for c in range(8):
                    P.emit("pe", lambda e, s=s, c=c: e.transpose(out=pst[:, c * 128:(c + 1) * 128],
                                                                  in_=W["xsb"][:, s, c * 128:(c + 1) * 128],
                                                                  identity=ident), ["xsb", "ident"], ["pst"])
                tt("dve", xnT[:, :, s * 128:(s + 1) * 128], pst.rearrange("p (c t) -> p c t", t=128), gb, ALU.mult,
                   ["pst", "gb"], ["xnT"])

        def load_w(dst, src, key):
            K = src.shape[0] // 128
            for k in range(K):
                dma("sp", dst[:, k, :], src[k * 128:(k + 1) * 128, :], [key[0]], [key[1]])

        evac_rr = [0]

        def evac(out, in_, reads, writes):
            evac_rr[0] ^= 1
            return cp("act" if evac_rr[0] else "dve", out, in_, reads, writes)

        def phase1(l):
            A.reset()
            wA = A.alloc([128, 8, NIN], BF16)
            W = dict(ssq=A.alloc([128, 4], F32), rstd=A.alloc([128, 4], F32),
                     xsb=A.alloc([128, 4, 1024], BF16))
            xts = [A.alloc([128, 4, 1024], F32) for _ in range(2)]
            xnT = A.alloc([128, 8, 512], BF16)
            zst = [A.alloc([128, 512], BF16) for _ in range(4)]
            zsf = A.alloc([4, 512], F32)
            ztm = A.alloc([128, 4, 1028], F32)
            vst = A.alloc([128, 4, 512], BF16)
            bfb = A.alloc([128, 4], F32)
            lt = A.alloc([128, 4, 4], F32)
            load_w(wA, wb_in[l], (("wb", id(wb_in), l), "wA"))
            load_gb(norm_mix_g[l])
            dma("sp", bfb, fox_b_f[l].partition_broadcast(128), (), ["bfb"])
            groups = [(c * 128, 128, zfm[c * 128:(c + 1) * 128, :]) for c in range(10)]
            for h in range(4):
                groups.append((1280 + 64 * h, 64, fqa[h, 0:64, :]))
                groups.append((1536 + 64 * h, 64, fka[h, 0:64, :]))
                groups.append((2052 + 64 * h, 64, dqs[h, 0:64, :]))
                groups.append((2308 + 64 * h, 64, dks[h, 0:64, :]))
            def load_x(ti):
                t0, n = tiles[ti]
                if l == 0:
                    src = xp[t0:t0 + n, :] if t0 < T else xs
                else:
                    src = xres[t0:t0 + n, :]
                dma("sp", xts[ti % 2][:, 0:n // 128, :], src.rearrange("(s p) d -> p s d", p=128), (), ["xt%d" % (ti % 2)])

            load_x(0)
            for ti, (t0, n) in enumerate(tiles):
                nsub = n // 128
                if ti + 1 < len(tiles):
                    load_x(ti + 1)
                xt = xts[ti % 2]
                rmsnorm_T(xt, nsub, xnT, W, "xt%d" % (ti % 2))
                for gi, (c0, M, dst) in enumerate(groups):
                    ps = psb[gi % 4]
                    pk = "ps%d" % (gi % 4)
                    for k in range(8):
                        mm(ps[0:M, 0:n], wA[:, k, c0:c0 + M], xnT[:, k, 0:n], k == 0, k == 7, ["wA", "xnT"], [pk])
                    zk = "zst%d" % (gi % 4)
                    evac(zst[gi % 4][0:M, 0:n], ps[0:M, 0:n], [pk], [zk])
                    dma("pool" if gi % 2 else "sp", dst[:, t0:t0 + n], zst[gi % 4][0:M, 0:n], [zk], [])
                for k in range(8):
                    mm(psb[0][0:4, 0:n], wA[:, k, 2048:2052], xnT[:, k, 0:n], k == 0, k == 7, ["wA", "xnT"], ["ps0"])
                cp("dve", zsf[:, 0:n], psb[0][0:4, 0:n], ["ps0"], ["zsf"])
                dma("sp", flT[:, t0:t0 + n], zsf[:, 0:n], ["zsf"], [])
                for s in range(nsub):
                    for (pi, c0, ncol) in ((4, 1536, 512), (5, 2308, 512), (6, 2048, 4)):
                        for k in range(8):
                            mm(psb[pi][:, 0:ncol], xnT[:, k, s * 128:(s + 1) * 128], wA[:, k, c0:c0 + ncol],
                               k == 0, k == 7, ["wA", "xnT"], ["ps%d" % pi])
                    cp("act", ztm[:, s, 0:512], psb[4], ["ps4"], ["ztm"])
                    cp("dve", ztm[:, s, 512:1024], psb[5], ["ps5"], ["ztm"])
                    tt("dve", lt[:, s, :], psb[6][:, 0:4], bfb, ALU.add, ["ps6", "bfb"], ["lt"])
                act(lt[:, 0:nsub, :], lt[:, 0:nsub, :], AF.Exp, ["lt"], ["lt"], scale=-1.0)
                act(lt[:, 0:nsub, :], lt[:, 0:nsub, :], AF.Ln, ["lt"], ["lt"], bias=1.0)
                ts("dve", ztm[:, 0:nsub, 1024:1028], lt[:, 0:nsub, :], -1.0, None, ALU.mult, None, ["lt"], ["ztm"])
                cp("dve", vst[:, 0:nsub, 0:256], ztm[:, 0:nsub, 256:512], ["ztm"], ["vst"])
                cp("act", vst[:, 0:nsub, 256:512], ztm[:, 0:nsub, 768:1024], ["ztm"], ["vst"])
                if t0 < T:
                    outs = (o_pfk, o_pfv, o_pdk, o_pdv, o_pflf)
                    r0 = t0
                else:
                    outs = (o_sfk, o_sfv, o_sdk, o_sdv, o_sflf)
                    r0 = 0
                for oi, o in enumerate(outs):
                    c0, cn = (oi * 256, 256) if oi < 4 else (1024, 4)
                    dma("pool", o[l, r0:r0 + n, :].rearrange("(s p) c -> p s c", p=128), ztm[:, 0:nsub, c0:c0 + cn],
                        ["ztm"], [])
                dma("sp", lfs[t0:t0 + n, :].rearrange("(s p) c -> p s c", p=128), ztm[:, 0:nsub, 1024:1028],
                    ["ztm"], [])
                dma("sp", vsf[t0:t0 + n, :].rearrange("(s p) c -> p s c", p=128), vst[:, 0:nsub, 0:256],
                    ["vst"], [])
                dma("sp", vsd[t0:t0 + n, :].rearrange("(s p) c -> p s c", p=128), vst[:, 0:nsub, 256:512],
                    ["vst"], [])
            P.barrier()

        def phase2_frows(l):
            A.reset()
            fl = A.alloc([4, T], F32)
            Fc = A.alloc([4, T], F32)
            fhi = A.alloc([4, T], BF16)
            flo = A.alloc([4, T], BF16)
            f8 = A.alloc([4, T], BF16)
            bfc = A.alloc([4, 1], F32)
            onesT = A.alloc([4, T], F32)
            dma("sp", fl, flT[:, 0:T], (), ["fl"])
            dma("sp", bfc, fox_b_f[l].rearrange("(h o) -> h o", o=1), (), ["bfc"])
            memset("pool", onesT, 1.0, ["onesT"])
            ts("dve", fl, fl, bfc[:, 0:1], -1.0, ALU.add, ALU.mult, ["fl", "bfc"], ["fl"])
            act(fl, fl, AF.Exp, ["fl"], ["fl"])
            act(fl, fl, AF.Ln, ["fl"], ["fl"], bias=1.0)
            ts("dve", fl, fl, -1.0, None, ALU.mult, None, ["fl"], ["fl"])
            P.emit("dve", lambda e: e.tensor_tensor_scan(out=Fc, data0=onesT, data1=fl, initial=0.0, op0=ALU.mult,
                                                         op1=ALU.add), ["fl", "onesT"], ["Fc"])
            cp("dve", fhi, Fc, ["Fc"], ["fhi"])
            tt("dve", flo, Fc, fhi, ALU.subtract, ["Fc", "fhi"], ["flo"])
            ts("dve", f8, fhi, 8.0, None, ALU.mult, None, ["fhi"], ["f8"])
            dma("sp", fka[:, 65, 0:T], fhi, ["fhi"], [])
            dma("sp", fka[:, 66, 0:T], flo, ["flo"], [])
            dma("sp", fqa[:, 64, 0:T], f8, ["f8"], [])
            P.barrier()

        def phase2_rec(l, sample):
            A.reset()
            S, TT, nt = (NS, TS, 1) if sample else (1, 1024, T // 1024)
            tb = T if sample else 0
            NW = S * TT
            cw = A.alloc([128, 2, 4], F32)
            cbias = A.alloc([128, 2], F32)
            ba = A.alloc([128, 2], F32)
            bx = A.alloc([128, 2], F32)
            lam = A.alloc([128, 2], F32)
            cl = A.alloc([128, 2], F32)
            scw = A.alloc([128, 2, 3], F32)
            wa_f = A.alloc([128, 2, 128], F32)
            wx_f = A.alloc([128, 2, 128], F32)
            wa_b = A.alloc([128, 2, 128], BF16)
            wx_b = A.alloc([128, 2, 128], BF16)
            for c_ in range(2):
                dma("sp", cw[:, c_, :], lru_conv_w[l, :, c_ * 128:(c_ + 1) * 128].rearrange("k p -> p k"), (), ["cw"])
            dma("sp", cbias, lru_conv_b[l].rearrange("(c p) -> p c", p=128), (), ["cw"])
            dma("sp", ba, lru_b_a[l].rearrange("(c p) -> p c", p=128), (), ["cw"])
            dma("sp", bx, lru_b_x[l].rearrange("(c p) -> p c", p=128), (), ["cw"])
            dma("sp", lam, lru_lambda[l].rearrange("(c p) -> p c", p=128), (), ["lam"])
            for c_ in range(2):
                dma("sp", scw[:, c_, :], sc_conv_w[l, :, c_ * 128:(c_ + 1) * 128].rearrange("k p -> p k"), (), ["cw"])
            memset("dve", wa_f, 0.0, ["wa_f"])
            memset("dve", wx_f, 0.0, ["wx_f"])
            for c in range(2):
                for b in range(2):
                    dma("sp", wa_f[b * 64:(b + 1) * 64, c, b * 64:(b + 1) * 64], lru_w_a[l, 2 * c + b], (), ["wa_f"])
                    dma("sp", wx_f[b * 64:(b + 1) * 64, c, b * 64:(b + 1) * 64], lru_w_x[l, 2 * c + b], (), ["wx_f"])
            cp("dve", wa_b, wa_f, ["wa_f"], ["wa_b"])
            cp("dve", wx_b, wx_f, ["wx_f"], ["wx_b"])
            act(cl, lam, AF.Exp, ["lam"], ["cl"], scale=-1.0)
            act(cl, cl, AF.Ln, ["cl"], ["cl"], bias=1.0)
            ts("dve", cl, cl, -8.0, None, ALU.mult, None, ["cl"], ["cl"])

            zb = A.alloc([128, NW], BF16)
            xl = A.alloc([128, S, 3 + TT], F32)
            xc = A.alloc([128, S, TT], F32)
            xcb = A.alloc([128, NW], BF16)
            rg = A.alloc([128, NW], F32)
            ig = A.alloc([128, NW], F32)
            av = A.alloc([128, NW], F32)
            uv = A.alloc([128, NW], F32)
            hv = A.alloc([128, NW], F32)
            gt = A.alloc([128, NW], BF16)
            g2 = A.alloc([128, NW], F32)
            yb = A.alloc([128, NW], BF16)
            h0 = A.alloc([128, S], F32)
            hist = A.alloc([128, S, 3], F32)
            cx = A.alloc([128, S, 2 + TT], F32)
            z2 = A.alloc([128, NW], BF16)
            z3 = A.alloc([128, NW], BF16)
            v3 = lambda ap: ap.rearrange("p (s t) -> p s t", t=TT)
            for c in range(2):
                r_ = slice(c * 128, (c + 1) * 128)
                for j in range(nt):
                    t0 = tb + j * NW
                    dma("sp", zb, zfm[c * 128:(c + 1) * 128, t0:t0 + NW], (), ["zb"])
                    if j > 0:
                        cp("dve", hist, xl[:, :, TT:TT + 3], ["xl"], ["hist"])
                    cp("dve", xl[:, :, 3:3 + TT], v3(zb), ["zb"], ["xl"])
                    if j > 0:
                        cp("dve", xl[:, :, 0:3], hist, ["hist"], ["xl"])
                    elif sample:
                        for k_ in range(3):
                            dma("sp", xl[:, :, k_], st_lru_conv[l, :, k_, r_].rearrange("s p -> p s"), (), ["xl"])
                        dma("sp", h0, st_lru_h[l, :, r_].rearrange("s p -> p s"), (), ["h0"])
                    else:
                        memset("dve", xl[:, :, 0:3], 0.0, ["xl"])
                    ts("dve", xc, xl[:, :, 0:TT], cw[:, c, 0:1], cbias[:, c:c + 1], ALU.mult, ALU.add, ["xl", "cw"], ["xc"])
                    for k in range(1, 4):
                        stt(xc, xl[:, :, k:k + TT], cw[:, c, k:k + 1], xc, ALU.mult, ALU.add, ["xl", "cw", "xc"], ["xc"])
                    cp("act", v3(xcb), xc, ["xc"], ["xcb"])
                    for hf in range(0, NW, 512):
                        n = min(512, NW - hf)
                        mm(psb[0][:, 0:n], wa_b[:, c, :], xcb[:, hf:hf + n], True, True, ["wa_b", "xcb"], ["ps0"])
                        act(rg[:, hf:hf + n], psb[0][:, 0:n], AF.Sigmoid, ["ps0", "cw"], ["rg"], bias=ba[:, c:c + 1])
                        mm(psb[1][:, 0:n], wx_b[:, c, :], xcb[:, hf:hf + n], True, True, ["wx_b", "xcb"], ["ps1"])
                        act(ig[:, hf:hf + n], psb[1][:, 0:n], AF.Sigmoid, ["ps1", "cw"], ["ig"], bias=bx[:, c:c + 1])
                    act(av, rg, AF.Exp, ["rg", "cl"], ["av"], scale=cl[:, c:c + 1])
                    tt("dve", uv, av, av, ALU.mult, ["av"], ["uv"])
                    ts("dve", uv, uv, -1.0, 1.0, ALU.mult, ALU.add, ["uv"], ["uv"])
                    ts("dve", uv, uv, 0.0, None, ALU.max, None, ["uv"], ["uv"])
                    act(uv, uv, AF.Sqrt, ["uv"], ["uv"])
                    tt("dve", uv, uv, ig, ALU.mult, ["uv", "ig"], ["uv"])
                    tt("dve", v3(uv), v3(uv), xc, ALU.mult, ["uv", "xc"], ["uv"])
                    if j > 0:
                        cp("dve", h0[:, 0:1], hv[:, NW - 1:NW], ["hv"], ["h0"])
                    for s in range(S):
                        init = 0.0 if (not sample and j == 0) else h0[:, s:s + 1]
                        P.emit("dve", lambda e, s=s, init=init: e.tensor_tensor_scan(
                            out=hv[:, s * TT:(s + 1) * TT], data0=av[:, s * TT:(s + 1) * TT],
                            data1=uv[:, s * TT:(s + 1) * TT], initial=init, op0=ALU.mult, op1=ALU.add),
                            ["av", "uv", "h0"], ["hv"])
                    dma("sp", gt, zfm[256 + c * 128:256 + (c + 1) * 128, t0:t0 + NW], (), ["gt"])
                    tt("dve", g2, gt, gt, ALU.mult, ["gt"], ["g2"])
                    ts("dve", g2, g2, 0.044715 * 0.7978845608028654, 0.7978845608028654, ALU.mult, ALU.add, ["g2"], ["g2"])
                    tt("dve", g2, g2, gt, ALU.mult, ["g2", "gt"], ["g2"])
                    act(g2, g2, AF.Tanh, ["g2"], ["g2"])
                    stt(g2, g2, 1.0, gt, ALU.add, ALU.mult, ["g2", "gt"], ["g2"])
                    stt(yb, g2, 0.5, hv, ALU.mult, ALU.mult, ["g2", "hv"], ["yb"])
                    dma("pool", mixT[c * 128:(c + 1) * 128, t0:t0 + NW], yb, ["yb"], [])
                if sample:
                    dma("pool", o_slh[l, :, r_].rearrange("s p -> p s"), v3(hv)[:, :, TT - 1], ["hv"], [])
                    for k_ in range(3):
                        dma("pool", o_slc[l, :, k_, r_].rearrange("s p -> p s"), xl[:, :, TT + k_], ["xl"], [])
                else:
                    dma("pool", o_plh[l, r_].rearrange("(p o) -> p o", o=1), hv[:, NW - 1:NW], ["hv"], [])
                    dma("pool", o_plc[l, :, r_].rearrange("k p -> p k"), xl[:, 0, TT:TT + 3], ["xl"], [])
                for j in range(nt):
                    t0 = tb + j * NW
                    dma("sp", zb, zfm[512 + c * 128:512 + (c + 1) * 128, t0:t0 + NW], (), ["zb"])
                    dma("sp", z2, zfm[768 + c * 128:768 + (c + 1) * 128, t0:t0 + NW], (), ["z2"])
                    dma("sp", z3, zfm[1024 + c * 128:1024 + (c + 1) * 128, t0:t0 + NW], (), ["z3"])
                    if j > 0:
                        cp("dve", hist[:, :, 0:2], cx[:, :, TT:TT + 2], ["cx"], ["hist"])
                    tt("dve", cx[:, :, 2:2 + TT], v3(z2), v3(z3), ALU.mult, ["z2", "z3"], ["cx"])
                    if j > 0:
                        cp("dve", cx[:, :, 0:2], hist[:, :, 0:2], ["hist"], ["cx"])
                    elif sample:
                        for k_ in range(2):
                            dma("sp", cx[:, :, k_], st_sconv[l, :, k_, r_].rearrange("s p -> p s"), (), ["cx"])
                    else:
                        memset("dve", cx[:, :, 0:2], 0.0, ["cx"])
                    ts("dve", xc, cx[:, :, 0:TT], scw[:, c, 0:1], None, ALU.mult, None, ["cx", "cw"], ["xc"])
                    for k in range(1, 3):
                        stt(xc, cx[:, :, k:k + TT], scw[:, c, k:k + 1], xc, ALU.mult, ALU.add, ["cx", "cw", "xc"], ["xc"])
                    tt("dve", v3(yb), xc, v3(zb), ALU.mult, ["xc", "zb"], ["yb"])
                    dma("pool", mixT[256 + c * 128:256 + (c + 1) * 128, t0:t0 + NW], yb, ["yb"], [])
                if sample:
                    for k_ in range(2):
                        dma("pool", o_ssc[l, :, k_, r_].rearrange("s p -> p s"), cx[:, :, TT + k_], ["cx"], [])
                else:
                    dma("pool", o_psc[l, :, r_].rearrange("k p -> p k"), cx[:, 0, TT:TT + 2], ["cx"], [])
            P.barrier()

        def lam_col(l, dst):
            lp = A.alloc([128, 128], F32)
            pr = A.alloc([128, 64], F32)
            sm = A.alloc([128, 2], F32)
            dma("sp", lp, diff_lambda[l].partition_broadcast(128), (), ["lp"])
            tt("dve", pr[:, 0:32], lp[:, 0:32], lp[:, 32:64], ALU.mult, ["lp"], ["pr"])
            tt("dve", pr[:, 32:64], lp[:, 64:96], lp[:, 96:128], ALU.mult, ["lp"], ["pr"])
            P.emit("dve", lambda e: e.reduce_sum(out=sm, in_=pr.rearrange("p (a b) -> p a b", b=32),
                                                 axis=mybir.AxisListType.X), ["pr"], ["sm"])
            act(sm, sm, AF.Exp, ["sm"], ["sm"])
            tt("dve", dst, sm[:, 0:1], sm[:, 1:2], ALU.subtract, ["sm"], ["lamc"])
            ts("dve", dst, dst, 0.8 - 0.6 * math.exp(-0.3 * l), None, ALU.add, None, ["lamc"], ["lamc"])

        def phase2_attn_prompt(l):
            A.reset()
            lam_init = 0.8 - 0.6 * math.exp(-0.3 * l)
            Qa = A.alloc([67, T], BF16)
            Ka = A.alloc([67, T], BF16)
            Va = A.alloc([128, T // 128, 128], BF16)
            pts = [A.alloc([128, 512], BF16) for _ in range(3)]
            rs = A.alloc([64, 512], F32)
            o1 = A.alloc([64, 512], F32)
            o2 = A.alloc([64, 512], F32)
            sq = A.alloc([64, 512], BF16)
            ys = [A.alloc([64, 512], BF16) for _ in range(2)]
            lamc = A.alloc([128, 1], F32)
            dg = A.alloc([64, 1], F32)
            o64 = A.alloc([64, 64], BF16)
            lam_col(l, lamc)
            dma("sp", dg, diff_norm_g[l].rearrange("(p o) -> p o", o=1), (), ["dg"])
            ts("dve", dg, dg, 1.0 - lam_init, None, ALU.mult, None, ["dg"], ["dg"])
            memset("pool", o64, 1.0 / 64.0, ["o64"])
            memset("pool", Va[:, :, 64:128], 1.0, ["Va1"])
            NQ = T // 512
            yi = [0]

            def run_map(qt, kq, kk, krows, scale, band, acc, acck, sti):
                q0 = qt * 512
                nkc = 4 * qt + 4
                steps = []
                for kc in range(nkc):
                    j = kc - 4 * qt
                    n0 = 128 * max(0, j)
                    steps.append((kc, j, n0, 512 - n0))

                def qk(i):
                    kc, j, n0, N = steps[i]
                    b = sti + (i % 2)
                    mm(psb[b][:, 0:N], kk[krows, kc * 128:(kc + 1) * 128], kq[krows, q0 + n0:q0 + 512], True, True,
                       ["Ka", "Qa"], ["ps%d" % b])

                qk(0)
                for i, (kc, j, n0, N) in enumerate(steps):
                    if i + 1 < len(steps):
                        qk(i + 1)
                    b = sti + (i % 2)
                    pt = pts[i % 3]
                    pk = "pt%d" % (i % 3)
                    act(pt[:, 0:N], psb[b][:, 0:N], AF.Exp, ["ps%d" % b], [pk], scale=scale)
                    if band is not None:
                        if j >= 0:
                            w = 256 if j <= 2 else 128
                            tt("dve", pt[:, 0:w], pt[:, 0:w], band[:, 0:w], ALU.mult, [pk, "eb", "tri"], [pk])
                        elif j == -1 and band is not tri:
                            tt("dve", pt[:, 0:128], pt[:, 0:128], band[:, 128:256], ALU.mult, [pk, "eb"], [pk])
                    mm(acc[:, n0:512], Va[:, kc, :], pt[:, 0:N], i == 0, i == len(steps) - 1, ["Va", "Va1", pk], [acck])

            for typ in ("fox", "diff"):
                for h in range(4):
                    if typ == "fox":
                        dma("sp", Qa, fqa[h, :, 0:T], (), ["Qa"])
                        dma("sp", Ka, fka[h, :, 0:T], (), ["Ka"])
                        vsrc = vsf
                    else:
                        dma("sp", Qa[0:64, :], dqs[h, :, 0:T], (), ["Qa"])
                        dma("sp", Ka[0:64, :], dks[h, :, 0:T], (), ["Ka"])
                        vsrc = vsd
                    dma("sp", Va[:, :, 0:64], vsrc[0:T, h * 64:(h + 1) * 64].rearrange("(c p) e -> p c e", p=128),
                        (), ["Va"])
                    for qt in range(NQ):
                        y = ys[yi[0] % 2]
                        yk = "ys%d" % (yi[0] % 2)
                        yi[0] += 1
                        if typ == "fox":
                            run_map(qt, Qa, Ka, slice(0, 67), 0.125, tri, psb[4], "ps4", 0)
                            P.emit("dve", lambda e: e.reciprocal(out=rs, in_=psb[4][64:128, :]), ["ps4"], ["rs"])
                            tt("dve", y, psb[4][0:64, :], rs, ALU.mult, ["ps4", "rs"], [yk])
                            dma("pool", mixT[512 + h * 64:512 + (h + 1) * 64, qt * 512:(qt + 1) * 512], y, [yk], [])
                        else:
                            sc_ = 32 ** -0.5
                            run_map(qt, Qa, Ka, slice(0, 32), sc_, eb[:, h, :], psb[4], "ps4", 0)
                            run_map(qt, Qa, Ka, slice(32, 64), sc_, eb[:, h, :], psb[5], "ps5", 2)
                            P.emit("dve", lambda e: e.reciprocal(out=rs, in_=psb[4][64:128, :]), ["ps4"], ["rs"])
                            tt("dve", o1, psb[4][0:64, :], rs, ALU.mult, ["ps4", "rs"], ["o1"])
                            P.emit("dve", lambda e: e.reciprocal(out=rs, in_=psb[5][64:128, :]), ["ps5"], ["rs"])
                            tt("dve", o2, psb[5][0:64, :], rs, ALU.mult, ["ps5", "rs"], ["o2"])
                            stt(o1, o2, lamc[0:64, 0:1], o1, ALU.mult, ALU.subtract, ["o1", "o2", "lamc"], ["o1"])
                            tt("dve", sq, o1, o1, ALU.mult, ["o1"], ["sq"])
                            mm(psb[6][0:64, :], o64, sq, True, True, ["o64", "sq"], ["ps6"])
                            ts("dve", rs, psb[6][0:64, :], EPS, None, ALU.add, None, ["ps6"], ["rs"])
                            act(rs, rs, AF.Ln, ["rs"], ["rs"])
                            act(rs, rs, AF.Exp, ["rs"], ["rs"], scale=-0.5)
                            tt("dve", o1, o1, rs, ALU.mult, ["o1", "rs"], ["o1"])
                            ts("dve", y, o1, dg[:, 0:1], -1.0, ALU.mult, ALU.mult, ["o1", "dg"], [yk])
                            dma("pool", mixT[768 + h * 64:768 + (h + 1) * 64, qt * 512:(qt + 1) * 512], y, [yk], [])
            P.barrier()


        def phase2_attn_sample(l):
            A.reset()
            lam_init = 0.8 - 0.6 * math.exp(-0.3 * l)
            triF = A.alloc([128, 128], F32)
            onesF = A.alloc([128, 128], F32)
            memset("dve", onesF, 1.0, ["onesF"])
            memset("pool", triF, 1.0, ["triF"])
            P.emit("pool", lambda e: e.affine_select(out=triF, in_=triF, pattern=[[1, 128]], compare_op=ALU.is_ge,
                                                     fill=0.0, base=0, channel_multiplier=-1), ["triF"], ["triF"])
            pti = A.alloc([128, NS * NPG], I32)
            idxf = A.alloc([128, NS * NPG], F32)
            idxi = A.alloc([128, NS * NPG], I32)
            iopi = A.alloc([128, 1], I32)
            iop = A.alloc([128, 1], F32)
            dma("sp", pti, ptab.partition_broadcast(128), (), ["pti"])
            P.emit("pool", lambda e: e.iota(iopi, pattern=[[0, 1]], base=0, channel_multiplier=1), (), ["iopi"])
            cp("dve", iop, iopi, ["iopi"], ["iop"])
            cp("dve", idxf, pti, ["pti"], ["idxf"])
            ts("dve", idxf, idxf, 128.0, iop[:, 0:1], ALU.mult, ALU.add, ["idxf", "iop"], ["idxf"])
            ts("dve", idxf, idxf, float(l * NPOOL * 128), None, ALU.add, None, ["idxf"], ["idxf"])
            cp("dve", idxi, idxf, ["idxf"], ["idxi"])
            qs = A.alloc([64, 4, NTS], BF16)
            ks = A.alloc([64, 4, NTS], BF16)
            qd = A.alloc([64, 4, NTS], BF16)
            kd = A.alloc([64, 4, NTS], BF16)
            for h in range(4):
                dma("sp", qs[:, h, :], fqa[h, 0:64, T:NT], (), ["qs"])
                dma("sp", ks[:, h, :], fka[h, 0:64, T:NT], (), ["qs"])
                dma("sp", qd[:, h, :], dqs[h, :, T:NT], (), ["qs"])
                dma("sp", kd[:, h, :], dks[h, :, T:NT], (), ["qs"])
            lamc = A.alloc([128, 1], F32)
            lam_col(l, lamc)
            dg2 = A.alloc([128, 1], F32)
            for b in range(2):
                dma("sp", dg2[b * 64:(b + 1) * 64, :], diff_norm_g[l].rearrange("(p o) -> p o", o=1), (), ["dg2"])
            ts("dve", dg2, dg2, -(1.0 - lam_init), None, ALU.mult, None, ["dg2"], ["dg2"])
            o64 = A.alloc([128, 128], BF16)
            memset("pool", o64, 0.0, ["o64"])
            memset("pool", o64[0:64, 0:64], 1.0 / 64.0, ["o64"])
            memset("pool", o64[64:128, 64:128], 1.0 / 64.0, ["o64"])
            gk = A.alloc([128, NPG, 256], F32)
            gv = A.alloc([128, NPG, 256], F32)
            glf = A.alloc([128, NPG * 4], F32)
            bk = A.alloc([128, NPG, 256], BF16)
            bv = A.alloc([128, NPG, 256], BF16)
            ktT = A.alloc([64, NPG, 512], BF16)
            p_all = A.alloc([128, NPG * 64], BF16)
            pn = A.alloc([8, 64], BF16)
            tmpS = A.alloc([128, 512], F32)
            bias = A.alloc([128, NPG * 4], F32)
            wth = A.alloc([128, NPG * 4], F32)
            tot = A.alloc([128, NPG * 4], F32)
            inc = A.alloc([128, NPG * 4], F32)
            ones16 = A.alloc([128, NPG], F32)
            memset("dve", ones16, 1.0, ["ones16"])
            lfn = A.alloc([8, 4], F32)
            bnew = A.alloc([8, 4], F32)
            vnew = A.alloc([8, 256], BF16)
            rsum = A.alloc([128, 64], F32)
            yf = A.alloc([128, 2, NTS], BF16)
            od = A.alloc([128, 2, 2, NTS], F32)
            odn = A.alloc([128, 2, NTS], F32)
            sqb = A.alloc([128, 2, NTS], BF16)
            rst = A.alloc([128, 2 * NTS], F32)
            yd = A.alloc([128, 2, NTS], BF16)
            pst2 = psb[3].bitcast(BF16)
            ti = [0]

            def gather(dst, cache, s, key):
                for j in range(NPG):
                    col = s * NPG + j
                    P.emit("pool", lambda e, j=j, col=col: e.indirect_dma_start(
                        out=dst[:, j, :] if len(dst.shape) == 3 else dst[:, j * 4:(j + 1) * 4], out_offset=None,
                        in_=cache, in_offset=bass.IndirectOffsetOnAxis(ap=idxi[:, col:col + 1], axis=0)),
                        ["idxi"], [key], dma=True)

            for s in range(dbg_only_sample or NS):
                c0 = s * TS
                for typ in ("fox", "diff"):
                    fox = typ == "fox"
                    it = ti[0]
                    ti[0] += 1
                    gather(gk, c_fk if fox else c_dk, s, "gk")
                    gather(gv, c_fv if fox else c_dv, s, "gv")
                    cp("dve", bk, gk, ["gk"], ["bk"])
                    cp("act", bv, gv, ["gv"], ["bv"])
                    dma("sp", vnew, (vsf if fox else vsd)[T + c0:T + c0 + TS, :], (), ["vnew"])
                    if fox:
                        gather(glf, c_flf, s, "glf")
                        mm(psb[2][:, 0:64], triF, glf, True, True, ["triF", "glf"], ["ps2"])
                        mm(psb[2][:, 64:128], onesF, glf, True, True, ["onesF", "glf"], ["ps2"])
                        cp("dve", wth, psb[2][:, 0:64], ["ps2"], ["wth"])
                        cp("dve", tot, psb[2][:, 64:128], ["ps2"], ["tot"])
                        for h in range(4):
                            P.emit("dve", lambda e, h=h: e.tensor_tensor_scan(
                                out=inc.rearrange("p (j h) -> p j h", h=4)[:, :, h], data0=ones16,
                                data1=tot.rearrange("p (j h) -> p j h", h=4)[:, :, h], initial=0.0, op0=ALU.mult,
                                op1=ALU.add), ["tot", "ones16"], ["inc"])
                        tt("dve", bias, tot, inc, ALU.subtract, ["tot", "inc"], ["bias"])
                        tt("dve", bias, bias, wth, ALU.subtract, ["bias", "wth"], ["bias"])
                        for h in range(4):
                            bh = bias.rearrange("p (j h) -> p j h", h=4)[:, :, h]
                            ts("dve", bh, bh, inc[:, 60 + h:61 + h], 8.0, ALU.add, ALU.mult, ["bias", "inc"], ["bias"])
                        dma("sp", lfn, lfs[T + c0:T + c0 + TS, :], (), ["lfn"])
                        mm(psb[2][0:8, 128:132], triF[0:8, 0:8], lfn, True, True, ["triF", "lfn"], ["ps2"])
                        ts("dve", bnew, psb[2][0:8, 128:132], -1.0, None, ALU.mult, None, ["ps2"], ["bnew"])
                    ncol = 32 if fox else 64
                    qq, kk = (qs, ks) if fox else (qd, kd)
                    for jp in range(NPG // 2):
                        tb_ = pst
                        tk_ = "pst"
                        for jj in range(2):
                            j = 2 * jp + jj
                            for h in range(4):
                                P.emit("pe", lambda e, h=h, j=j, jj=jj, tb_=tb_: e.transpose(
                                    out=tb_[0:64, (jj * 4 + h) * 128:(jj * 4 + h + 1) * 128],
                                    in_=bk[:, j, h * 64:(h + 1) * 64], identity=ident), ["bk", "ident"], [tk_])
                        evac(ktT[:, 2 * jp:2 * jp + 2, :], tb_[0:64, :].rearrange("p (a b) -> p a b", b=512), [tk_], ["ktT"])
                    for j in range(NPG):
                        for h in range(4):
                            for m in range(1 if fox else 2):
                                rows = slice(0, 64) if fox else slice(32 * m, 32 * m + 32)
                                cc = j * 32 + h * 8
                                dst_ = psb[m][:, cc:cc + 8]
                                dk_ = "ps%d" % m
                                mm(dst_, ktT[rows, j, h * 128:(h + 1) * 128], qq[rows, h, c0:c0 + TS], True, True,
                                   ["ktT", "qs"], [dk_])
                    for h in range(4):
                        for m in range(1 if fox else 2):
                            rows = slice(0, 64) if fox else slice(32 * m, 32 * m + 32)
                            nb_ = psb[2][0:8, 192 + h * 8:192 + (h + 1) * 8] if m == 0 else psb[3][0:8, h * 8:(h + 1) * 8]
                            mm(nb_, kk[rows, h, c0:c0 + TS], qq[rows, h, c0:c0 + TS], True, True,
                               ["qs"], ["ps2" if m == 0 else "ps3"])
                    if fox:
                        tt("dve", tmpS.rearrange("p (a q) -> p a q", q=8), psb[0].rearrange("p (a q) -> p a q", q=8),
                           bias.unsqueeze(2).broadcast_to([128, NPG * 4, 8]), ALU.add, ["ps0", "bias"], ["tmpS"])
                        act(p_all[:, 0:512], tmpS, AF.Exp, ["tmpS"], ["p_all"], scale=0.125)
                        for h in range(4):
                            act(pn[:, h * 8:(h + 1) * 8], psb[2][0:8, 192 + h * 8:192 + (h + 1) * 8], AF.Exp,
                                ["ps2", "bnew"], ["pn"], bias=bnew[:, h:h + 1], scale=0.125)
                        tt("dve", pn[:, 0:32].rearrange("p (h q) -> p h q", q=8),
                           pn[:, 0:32].rearrange("p (h q) -> p h q", q=8),
                           tri[0:8, 0:8].unsqueeze(1).broadcast_to([8, 4, 8]), ALU.mult, ["pn", "tri"], ["pn"])
                    else:
                        sc_ = 32 ** -0.5
                        for m in range(2):
                            act(p_all.rearrange("p (j h m q) -> p j h m q", h=4, m=2, q=8)[:, :, :, m, :],
                                psb[m].rearrange("p (j h q) -> p j h q", h=4, q=8), AF.Exp, ["ps%d" % m], ["p_all"],
                                scale=sc_)
                        act(pn.rearrange("p (h m q) -> p h m q", m=2, q=8)[:, :, 0, :],
                            psb[2][0:8, 192:224].rearrange("p (h q) -> p h q", q=8), AF.Exp, ["ps2"], ["pn"], scale=sc_)
                        act(pn.rearrange("p (h m q) -> p h m q", m=2, q=8)[:, :, 1, :],
                            psb[3][0:8, 0:32].rearrange("p (h q) -> p h q", q=8), AF.Exp, ["ps3"], ["pn"], scale=sc_)
                        for h in range(4):
                            v_ = p_all[:, 15 * 64 + h * 16:15 * 64 + (h + 1) * 16].rearrange("p (m q) -> p m q", q=8)
                            tt("dve", v_, v_, eb[:, h, 128:136].unsqueeze(1).broadcast_to([128, 2, 8]), ALU.mult,
                               ["p_all", "eb"], ["p_all"])
                            v2 = pn[:, h * 16:(h + 1) * 16].rearrange("p (m q) -> p m q", q=8)
                            tt("dve", v2, v2, eb[0:8, h, 0:8].unsqueeze(1).broadcast_to([8, 2, 8]), ALU.mult,
                               ["pn", "eb"], ["pn"])
                    for j in range(NPG + 1):
                        new = j == NPG
                        nk = 8 if new else 128
                        rhs_ = pn[0:8, 0:ncol] if new else p_all[:, j * ncol:(j + 1) * ncol]
                        rk = "pn" if new else "p_all"
                        for hp in range(2):
                            lhs = vnew[:, hp * 128:(hp + 1) * 128] if new else bv[:, j, hp * 128:(hp + 1) * 128]
                            mm(psb[4 + hp][:, 0:ncol], lhs, rhs_, j == 0, new,
                               (["vnew"] if new else ["bv"]) + [rk], ["ps%d" % (4 + hp)])
                        mm(psb[6][:, 0:ncol], ones_b[0:nk, :], rhs_, j == 0, new, ["ones_b", rk], ["ps6"])
                    P.emit("dve", lambda e, ncol=ncol: e.reciprocal(out=rsum[:, 0:ncol], in_=psb[6][:, 0:ncol]),
                           ["ps6"], ["rsum"])
                    for h in range(4):
                        pr = slice((h % 2) * 64, (h % 2) * 64 + 64)
                        ab = psb[4 + h // 2]
                        ak = "ps%d" % (4 + h // 2)
                        if fox:
                            tt("dve", yf[pr, h // 2, c0:c0 + TS], ab[pr, h * 8:(h + 1) * 8],
                               rsum[pr, h * 8:(h + 1) * 8], ALU.mult, [ak, "rsum"], ["yf"])
                        else:
                            for m in range(2):
                                cc = (h * 2 + m) * 8
                                tt("dve", od[pr, m, h // 2, c0:c0 + TS], ab[pr, cc:cc + 8], rsum[pr, cc:cc + 8],
                                   ALU.mult, [ak, "rsum"], ["od"])
            stt(odn, od[:, 1], lamc[:, 0:1], od[:, 0], ALU.mult, ALU.subtract, ["od", "lamc"], ["odn"])
            tt("dve", sqb, odn, odn, ALU.mult, ["odn"], ["sqb"])
            mm(psb[0][:, 0:2 * NTS], o64, sqb.rearrange("p a t -> p (a t)"), True, True, ["o64", "sqb"], ["ps0"])
            ts("dve", rst, psb[0][:, 0:2 * NTS], EPS, None, ALU.add, None, ["ps0"], ["rst"])
            act(rst, rst, AF.Sqrt, ["rst"], ["rst"])
            P.emit("dve", lambda e: e.reciprocal(out=rst, in_=rst), ["rst"], ["rst"])
            tt("dve", odn, odn, rst.rearrange("p (a t) -> p a t", t=NTS), ALU.mult, ["odn", "rst"], ["odn"])
            ts("dve", yd, odn, dg2[:, 0:1], None, ALU.mult, None, ["odn", "dg2"], ["yd"])
            for c in range(2):
                dma("pool", mixT[512 + c * 128:512 + (c + 1) * 128, T:NT], yf[:, c, :], ["yf"], [])
                dma("pool", mixT[768 + c * 128:768 + (c + 1) * 128, T:NT], yd[:, c, :], ["yd"], [])
            P.barrier()

        def phase3(l):
            A.reset()
            wA = A.alloc([128, 8, DFF], BF16)
            wB = A.alloc([128, 8, DFF], BF16)
            wC = A.alloc([128, 8, D], BF16)
            W = dict(ssq=A.alloc([128, 4], F32), rstd=A.alloc([128, 4], F32),
                     xsb=A.alloc([128, 4, 1024], BF16))
            xt = A.alloc([128, 4, 1024], F32)
            xnT = A.alloc([128, 8, 512], BF16)
            mts = [A.alloc([128, 8, 512], BF16) for _ in range(2)]
            sg = [A.alloc([128, 512], F32) for _ in range(2)]
            ast = [A.alloc([128, 512], BF16) for _ in range(3)]
            load_w(wC, wb_out[l], (("wb", id(wb_out), l), "wC"))
            load_w(wA, wb_gate[l], (("wb", id(wb_gate), l), "wA"))
            load_w(wB, wb_up[l], (("wb", id(wb_up), l), "wB"))
            load_gb(norm_ffn_g[l])

            def load_m(ti):
                t0, n = tiles[ti]
                dma("sp", mts[ti % 2][:, :, 0:n], mixT[:, t0:t0 + n].rearrange("(k p) t -> p k t", p=128), (),
                    ["mt%d" % (ti % 2)])

            load_m(0)
            for ti, (t0, n) in enumerate(tiles):
                nsub = n // 128
                if l == 0:
                    src = xp[t0:t0 + n, :] if t0 < T else xs
                else:
                    src = xres[t0:t0 + n, :]
                dma("sp", xt[:, 0:nsub, :], src.rearrange("(s p) d -> p s d", p=128), ["xres"], ["xt"])
                if ti + 1 < len(tiles):
                    load_m(ti + 1)
                mt = mts[ti % 2]
                mk_ = "mt%d" % (ti % 2)
                for s in range(nsub):
                    for hf in range(2):
                        b = (2 * s + hf) % 4
                        for k in range(8):
                            mm(psb[b], mt[:, k, s * 128:(s + 1) * 128], wC[:, k, hf * 512:(hf + 1) * 512], k == 0, k == 7,
                               [mk_, "wC"], ["ps%d" % b])
                        tt("dve", xt[:, s, hf * 512:(hf + 1) * 512], xt[:, s, hf * 512:(hf + 1) * 512], psb[b], ALU.add,
                           ["xt", "ps%d" % b], ["xt"])
                dma("pool", xres[t0:t0 + n, :].rearrange("(s p) d -> p s d", p=128), xt[:, 0:nsub, :], ["xt"], ["xres"])
                rmsnorm_T(xt, nsub, xnT, W)
                for fc in range(NFC):
                    bg = (2 * fc) % 4
                    bu = bg + 1
                    for k in range(8):
                        mm(psb[bg][:, 0:n], wA[:, k, fc * 128:(fc + 1) * 128], xnT[:, k, 0:n], k == 0, k == 7,
                           ["wA", "xnT"], ["ps%d" % bg])
                    for k in range(8):
                        mm(psb[bu][:, 0:n], wB[:, k, fc * 128:(fc + 1) * 128], xnT[:, k, 0:n], k == 0, k == 7,
                           ["wB", "xnT"], ["ps%d" % bu])
                    sgt = sg[fc % 2]
                    sgk = "sg%d" % (fc % 2)
                    act(sgt[:, 0:n], psb[bg][:, 0:n], AF.Silu, ["ps%d" % bg], [sgk])
                    a_ = ast[fc % 3]
                    ak = "ast%d" % (fc % 3)
                    tt("dve", a_[:, 0:n], sgt[:, 0:n], psb[bu][:, 0:n], ALU.mult, [sgk, "ps%d" % bu], [ak])
                    dma("pool" if fc % 2 else "sp", aT[fc * 128:(fc + 1) * 128, t0:t0 + n], a_[:, 0:n], [ak], [])
            P.barrier()

        def phase4(l):
            A.reset()
            wA = A.alloc([128, NFC, D], BF16)
            xts = [A.alloc([128, 4, 1024], F32) for _ in range(2)]
            ats = [A.alloc([128, NFC, 512], BF16) for _ in range(2)]
            W = dict(junk=A.alloc([128, 1024], BF16), ssq=A.alloc([128, 4], F32), rstd=A.alloc([128, 4], F32))
            gfb = A.alloc([128, 1024], F32)
            load_w(wA, wb_down[l], (("wb", id(wb_down), l), "wA"))
            last = (l == L - 1)
            if last:
                dma("sp", gfb, norm_final_g.partition_broadcast(128), (), ["gfb"])
            def load_xa(ti):
                t0, n = tiles[ti]
                dma("sp", xts[ti % 2][:, 0:n // 128, :], xres[t0:t0 + n, :].rearrange("(s p) d -> p s d", p=128), (),
                    ["xt%d" % (ti % 2)])
                dma("sp", ats[ti % 2][:, :, 0:n], aT[:, t0:t0 + n].rearrange("(k p) t -> p k t", p=128), (),
                    ["at%d" % (ti % 2)])

            load_xa(0)
            for ti, (t0, n) in enumerate(tiles):
                nsub = n // 128
                if ti + 1 < len(tiles):
                    load_xa(ti + 1)
                xt = xts[ti % 2]
                at = ats[ti % 2]
                xk = "xt%d" % (ti % 2)
                ak_ = "at%d" % (ti % 2)
                for s in range(nsub):
                    for hf in range(2):
                        b = (2 * s + hf) % 4
                        for k in range(NFC):
                            mm(psb[b], at[:, k, s * 128:(s + 1) * 128], wA[:, k, hf * 512:(hf + 1) * 512], k == 0,
                               k == NFC - 1, [ak_, "wA"], ["ps%d" % b])
                        tt("dve", xt[:, s, hf * 512:(hf + 1) * 512], xt[:, s, hf * 512:(hf + 1) * 512], psb[b], ALU.add,
                           [xk, "ps%d" % b], [xk])
                if not last:
                    dma("pool", xres[t0:t0 + n, :].rearrange("(s p) d -> p s d", p=128), xt[:, 0:nsub, :], [xk], [])
                else:
                    for s in range(nsub):
                        act(W["junk"], xt[:, s, :], AF.Square, [xk], ["junk", "ssq"], accum_out=W["ssq"][:, s:s + 1])
                    ts("dve", W["rstd"][:, 0:nsub], W["ssq"][:, 0:nsub], 1.0 / D, EPS, ALU.mult, ALU.add, ["ssq"], ["rstd"])
                    act(W["rstd"][:, 0:nsub], W["rstd"][:, 0:nsub], AF.Ln, ["rstd"], ["rstd"])
                    act(W["rstd"][:, 0:nsub], W["rstd"][:, 0:nsub], AF.Exp, ["rstd"], ["rstd"], scale=-0.5)
                    for s in range(nsub):
                        stt(xt[:, s, :], xt[:, s, :], W["rstd"][:, s:s + 1], gfb, ALU.mult, ALU.mult,
                            [xk, "rstd", "gfb"], [xk])
                    dst = o_yp[t0:t0 + n, :] if t0 < T else o_ys
                    dma("pool", dst.rearrange("(s p) d -> p s d", p=128), xt[:, 0:nsub, :], [xk], [])
            P.barrier()

        if dbg_only_sample:
            phase2_attn_sample(0)
        for l in range(L if not dbg_only_sample else 0):
            phase1(l)
            phase2_frows(l)
            phase2_rec(l, False)
            phase2_rec(l, True)
            phase2_attn_prompt(l)
            if sample_attn:
                phase2_attn_sample(l)
            phase3(l)
            phase4(l)

        P.replay(nc, st)
    return nc


_IN_NAMES = ["w_in", "w_out", "w_gate", "w_up", "w_down", "norm_mix_g", "norm_ffn_g", "norm_final_g", "lru_conv_w",
             "lru_conv_b", "lru_w_a", "lru_b_a", "lru_w_x", "lru_b_x", "lru_lambda", "sc_conv_w", "fox_b_f",
             "diff_norm_g", "rel_bias"]


def make_in_maps(inp, cores, sample_attn=True):
    f = lambda a: np.ascontiguousarray(np.asarray(a, dtype=np.float32))
    shared = {k: f(inp[k]) for k in _IN_NAMES}
    shared["diff_lambda"] = f(inp["diff_lambda"]).reshape(L, 128)
    shared["c_t5"] = t5_onehot()
    if sample_attn:
        shared["c_fk"] = f(inp["cache_fox_k"]).reshape(L * NPOOL * 128, 256)
        shared["c_fv"] = f(inp["cache_fox_v"]).reshape(L * NPOOL * 128, 256)
        shared["c_flf"] = f(inp["cache_fox_logf"]).reshape(L * NPOOL * 128, 4)
        shared["c_dk"] = f(inp["cache_diff_k"]).reshape(L * NPOOL * 128, 256)
        shared["c_dv"] = f(inp["cache_diff_v"]).reshape(L * NPOOL * 128, 256)
    maps = []
    for c in cores:
        m = dict(shared)
        sl = slice(c * NS, (c + 1) * NS)
        m["xp"] = f(inp["x_prompt"][c % 4])
        m["xs"] = f(inp["x_sample"][sl]).reshape(NTS, D)
        m["st_lru_h"] = f(np.asarray(inp["state_lru_h"])[:, sl])
        m["st_lru_conv"] = f(np.asarray(inp["state_lru_conv"])[:, sl])
        m["st_sconv"] = f(np.asarray(inp["state_sconv"])[:, sl])
        if sample_attn:
            m["ptab"] = np.ascontiguousarray(np.asarray(inp["page_table"], dtype=np.int32)[sl]).reshape(NS * NPG)
        maps.append(m)
    return maps


def assemble(results, cores):
    B = 4
    G = 128
    out = {}
    yp = np.zeros((B, T, D), np.float32)
    ys = np.zeros((G, TS, D), np.float32)
    pk = {n: np.zeros((L, B, T, 256 if n != "flf" else 4), np.float32) for n in ("fk", "fv", "flf", "dk", "dv")}
    plh = np.zeros((L, B, 256), np.float32)
    plc = np.zeros((L, B, 3, 256), np.float32)
    psc = np.zeros((L, B, 2, 256), np.float32)
    sk = {n: np.zeros((L, G, TS, 256 if n != "flf" else 4), np.float32) for n in ("fk", "fv", "flf", "dk", "dv")}
    slh = np.zeros((L, G, 256), np.float32)
    slc = np.zeros((L, G, 3, 256), np.float32)
    ssc = np.zeros((L, G, 2, 256), np.float32)
    for r, c in zip(results, cores):
        sl = slice(c * NS, (c + 1) * NS)
        if c < 4:
            yp[c] = r["o_yp"]
            for n in pk:
                pk[n][:, c] = r["o_p" + n]
            plh[:, c] = r["o_plh"]
            plc[:, c] = r["o_plc"]
            psc[:, c] = r["o_psc"]
        ys[sl] = r["o_ys"].reshape(NS, TS, D)
        for n in sk:
            sk[n][:, sl] = r["o_s" + n].reshape(L, NS, TS, -1)
        slh[:, sl] = r["o_slh"]
        slc[:, sl] = r["o_slc"]
        ssc[:, sl] = r["o_ssc"]
    return (yp, ys,
            pk["fk"].reshape(L, B, T, 4, 64), pk["fv"].reshape(L, B, T, 4, 64), pk["flf"],
            pk["dk"].reshape(L, B, T, 4, 64), pk["dv"].reshape(L, B, T, 4, 64), plh, plc, psc,
            sk["fk"].reshape(L, G, TS, 4, 64), sk["fv"].reshape(L, G, TS, 4, 64), sk["flf"],
            sk["dk"].reshape(L, G, TS, 4, 64), sk["dv"].reshape(L, G, TS, 4, 64), slh, slc, ssc)


def kernel(**inputs):
    cores = list(range(N_CORES))
    nc = build_nc(sample_attn=True)
    in_maps = make_in_maps(inputs, cores, sample_attn=True)
    res = run_bass_kernel_spmd(nc, in_maps, core_ids=cores)
    return assemble(res.results, cores)
```

```python
import math
from contextlib import ExitStack

import numpy as np
import concourse.bass as bass
import concourse.mybir as mybir
from concourse.bass_utils import run_bass_kernel_spmd

F32 = mybir.dt.float32
BF16 = mybir.dt.bfloat16
I32 = mybir.dt.int32
U32 = mybir.dt.uint32
AF = mybir.ActivationFunctionType
ALU = mybir.AluOpType

ENGS = ("pe", "act", "dve", "pool", "sp")
N_DMA_SEMS = 6

L = 2
D = 1024
T = 4096
NS = 16
TS = 8
NTS = NS * TS
NT = T + NTS
NIN = 2820
DFF = 2816
NFC = DFF // 128
PAST = 2048
NPG = 16
NPOOL = 2560
EPS = 1e-6
N_CORES = 8


class Prog:
    def __init__(self):
        self.ops = {e: [] for e in ENGS}
        self.cnt = {}
        self.last_w = {}
        self.readers = {}
        self.waited = {e: {} for e in ENGS}
        self.rr = {e: 0 for e in ENGS}
        self.pending = {e: [] for e in ENGS}
        self.part_w = {}
        self.batch_war = {}

    def barrier(self):
        self.part_w = {}
        self.batch_war = {}
        cur = [(sk, v) for sk, v in self.cnt.items() if v > 0]
        for e in ENGS:
            self.pending[e] = list(cur)
        self.last_w = {}
        self.readers = {}

    def emit(self, eng, fn, reads=(), writes=(), dma=False, pwrites=()):
        psr = [k for k in reads if isinstance(k, str) and k.startswith("ps") and k not in writes]
        if psr:
            writes = list(writes) + psr
        deps = list(self.pending[eng])
        self.pending[eng] = []
        for k in reads:
            t = self.last_w.get(k)
            if t is not None:
                deps.append(t)
            deps.extend(self.part_w.get(k, ()))
        for k in writes:
            t = self.last_w.get(k)
            if t is not None:
                deps.append(t)
            deps.extend(self.readers.get(k, ()))
            deps.extend(self.part_w.get(k, ()))
        for k in pwrites:
            if self.readers.get(k) or k not in self.part_w:
                war = list(self.readers.get(k, ()))
                if k in self.last_w:
                    war.append(self.last_w[k])
                self.batch_war[k] = war
                self.part_w[k] = []
                self.readers[k] = []
            deps.extend(self.batch_war[k])
        if dma:
            slot = self.rr[eng]
            self.rr[eng] = (slot + 1) % N_DMA_SEMS
            sk = ("dma", eng, slot)
            inc = 16
            if self.cnt.get(sk, 0) > 0:
                deps.append((sk, self.cnt[sk]))
        else:
            sk = ("eng", eng)
            inc = 1
        val = self.cnt.get(sk, 0) + inc
        self.cnt[sk] = val
        tok = (sk, val)
        waits = []
        wd = self.waited[eng]
        need = {}
        for (dk, dv) in deps:
            if eng == "pe" and dk == ("eng", "pe"):
                continue
            if dv > need.get(dk, 0):
                need[dk] = dv
        for dk, dv in need.items():
            if wd.get(dk, 0) < dv:
                wd[dk] = dv
                waits.append((dk, dv))
        self.ops[eng].append((fn, waits, sk, inc))
        for k in writes:
            self.last_w[k] = tok
            self.readers[k] = []
            self.part_w.pop(k, None)
            self.batch_war.pop(k, None)
        for k in pwrites:
            self.part_w[k].append(tok)
        for k in reads:
            self.readers.setdefault(k, []).append(tok)
        return tok

    def replay(self, nc, stack):
        sems = {}
        for sk in self.cnt:
            sems[sk] = stack.enter_context(nc.semaphore("s_" + "_".join(str(x) for x in sk)))
        block = stack.enter_context(nc.Block())
        finals = [(sk, v) for sk, v in self.cnt.items()]
        sig = {}
        for en in ENGS:
            for (fn, waits, sk, inc) in self.ops[en]:
                for (wk, wv) in waits:
                    if wk[0] == "eng":
                        sig.setdefault(wk, set()).add(wv)
        for (fk, fv) in finals:
            if fk[0] == "eng":
                sig.setdefault(fk, set()).add(fv)
        rank = {k: {v: i + 1 for i, v in enumerate(sorted(vs))} for k, vs in sig.items()}

        def wval(wk, wv):
            return rank[wk][wv] if wk[0] == "eng" else wv

        def mk(eng_name):
            def body(e):
                idx = {}
                for (fn, waits, sk, inc) in self.ops[eng_name]:
                    for (wk, wv) in waits:
                        e.wait_ge(sems[wk], wval(wk, wv))
                    ins = fn(e)
                    if sk[0] == "eng":
                        idx[sk] = idx.get(sk, 0) + 1
                        if idx[sk] in sig.get(sk, ()):
                            ins.then_inc(sems[sk], 1)
                    else:
                        ins.then_inc(sems[sk], inc)
                if eng_name == "sp":
                    for (fk, fv) in finals:
                        e.wait_ge(sems[fk], wval(fk, fv))
            return body

        block.tensor(mk("pe"))
        block.scalar(mk("act"))
        block.vector(mk("dve"))
        block.gpsimd(mk("pool"))
        block.sync(mk("sp"))


class Arena:
    def __init__(self, ap, words):
        self.ap = ap
        self.words = words
        self.off = 0

    def reset(self):
        self.off = 0

    def alloc(self, shape, dtype, parts=128):
        n = 1
        for s in shape[1:]:
            n *= s
        w = n if dtype in (F32, I32, U32) else (n + 1) // 2
        w = (w + 7) // 8 * 8
        assert self.off + w <= self.words, ("arena overflow", self.off, w, self.words)
        v = self.ap[0:shape[0], self.off:self.off + w]
        self.off += w
        if dtype != F32:
            v = v.bitcast(dtype)
        v = v[:, 0:n]
        if len(shape) == 3:
            v = v.rearrange("p (a b) -> p a b", b=shape[2])
        elif len(shape) == 4:
            v = v.rearrange("p (a b c) -> p a b c", b=shape[2], c=shape[3])
        return v


def t5_onehot():
    oh = np.zeros((32, 384), np.float32)
    for j in range(383):
        rel = j - 127
        if rel < 0:
            continue
        if rel < 16:
            b = rel
        else:
            b = 16 + int(np.float32(np.log(np.float32(rel) / np.float32(16.0)) / np.float32(math.log(8.0))
                                    * np.float32(16.0)))
            b = min(b, 31)
        oh[b, j] += 1.0
        oh[31, j] -= 1.0
    return oh


def build_nc(sample_attn=True, dbg_only_sample=0):
    nc = bass.Bass("TRN2", target_bir_lowering=False)
    P = Prog()

    def din(name, shape, dt=F32):
        return nc.dram_tensor(name, list(shape), dt, kind="ExternalInput").ap()

    def dout(name, shape, dt=F32):
        return nc.dram_tensor(name, list(shape), dt, kind="ExternalOutput").ap()

    def dscr(name, shape, dt):
        return nc.dram_tensor(name, list(shape), dt, kind="Internal").ap()

    xp = din("xp", [T, D])
    xs = din("xs", [NTS, D])
    w_in = din("w_in", [L, D, NIN])
    w_out = din("w_out", [L, D, D])
    w_gate = din("w_gate", [L, D, DFF])
    w_up = din("w_up", [L, D, DFF])
    w_down = din("w_down", [L, DFF, D])
    norm_mix_g = din("norm_mix_g", [L, D])
    norm_ffn_g = din("norm_ffn_g", [L, D])
    norm_final_g = din("norm_final_g", [D])
    lru_conv_w = din("lru_conv_w", [L, 4, 256])
    lru_conv_b = din("lru_conv_b", [L, 256])
    lru_w_a = din("lru_w_a", [L, 4, 64, 64])
    lru_b_a = din("lru_b_a", [L, 256])
    lru_w_x = din("lru_w_x", [L, 4, 64, 64])
    lru_b_x = din("lru_b_x", [L, 256])
    lru_lambda = din("lru_lambda", [L, 256])
    sc_conv_w = din("sc_conv_w", [L, 3, 256])
    fox_b_f = din("fox_b_f", [L, 4])
    diff_lambda = din("diff_lambda", [L, 128])
    diff_norm_g = din("diff_norm_g", [L, 64])
    rel_bias = din("rel_bias", [32, 4])
    c_t5 = din("c_t5", [32, 384])
    st_lru_h = din("st_lru_h", [L, NS, 256])
    st_lru_conv = din("st_lru_conv", [L, NS, 3, 256])
    st_sconv = din("st_sconv", [L, NS, 2, 256])
    if sample_attn:
        c_fk = din("c_fk", [L * NPOOL * 128, 256])
        c_fv = din("c_fv", [L * NPOOL * 128, 256])
        c_flf = din("c_flf", [L * NPOOL * 128, 4])
        c_dk = din("c_dk", [L * NPOOL * 128, 256])
        c_dv = din("c_dv", [L * NPOOL * 128, 256])
        ptab = din("ptab", [NS * NPG], I32)

    o_yp = dout("o_yp", [T, D])
    o_ys = dout("o_ys", [NTS, D])
    o_pfk = dout("o_pfk", [L, T, 256])
    o_pfv = dout("o_pfv", [L, T, 256])
    o_pflf = dout("o_pflf", [L, T, 4])
    o_pdk = dout("o_pdk", [L, T, 256])
    o_pdv = dout("o_pdv", [L, T, 256])
    o_plh = dout("o_plh", [L, 256])
    o_plc = dout("o_plc", [L, 3, 256])
    o_psc = dout("o_psc", [L, 2, 256])
    o_sfk = dout("o_sfk", [L, NTS, 256])
    o_sfv = dout("o_sfv", [L, NTS, 256])
    o_sflf = dout("o_sflf", [L, NTS, 4])
    o_sdk = dout("o_sdk", [L, NTS, 256])
    o_sdv = dout("o_sdv", [L, NTS, 256])
    o_slh = dout("o_slh", [L, NS, 256])
    o_slc = dout("o_slc", [L, NS, 3, 256])
    o_ssc = dout("o_ssc", [L, NS, 2, 256])

    wb_in = dscr("wb_in", [L, D, NIN], BF16)
    wb_out = dscr("wb_out", [L, D, D], BF16)
    wb_gate = dscr("wb_gate", [L, D, DFF], BF16)
    wb_up = dscr("wb_up", [L, D, DFF], BF16)
    wb_down = dscr("wb_down", [L, DFF, D], BF16)
    xres = dscr("xres", [NT, D], F32)
    zfm = dscr("zfm", [1280, NT], BF16)
    fqa = dscr("fqa", [4, 67, NT], BF16)
    fka = dscr("fka", [4, 67, NT], BF16)
    dqs = dscr("dqs", [4, 64, NT], BF16)
    dks = dscr("dks", [4, 64, NT], BF16)
    flT = dscr("flT", [4, NT], F32)
    vsf = dscr("vsf", [NT, 256], BF16)
    vsd = dscr("vsd", [NT, 256], BF16)
    lfs = dscr("lfs", [NT, 4], F32)
    mixT = dscr("mixT", [D, NT], BF16)
    aT = dscr("aT", [DFF, NT], BF16)
    t5s = dscr("t5s", [4, 384], BF16)
    t5b = dscr("t5b", [4, 128, 384], BF16)

    tiles = [(i * 512, 512) for i in range(T // 512)] + [(T, NTS)]

    with ExitStack() as st:
        st.enter_context(nc.allow_non_contiguous_dma(reason="layout transforms"))
        st.enter_context(nc.allow_low_precision(reason="bf16 matmul operands per problem tolerance"))
        AW = 42000
        arena_t = st.enter_context(nc.sbuf_tensor("arena", [128, AW], F32))
        A = Arena(arena_t, AW)
        cst_t = st.enter_context(nc.sbuf_tensor("cst", [128, 2600], F32))
        C = Arena(cst_t, 2600)
        ident = C.alloc([128, 128], BF16)
        tri = C.alloc([128, 256], BF16)
        ones_b = C.alloc([128, 128], BF16)
        ones_f = C.alloc([128, 64], F32)
        gb = C.alloc([128, 8, 128], BF16)
        gcol = C.alloc([128, 8], F32)
        small = C.alloc([128, 256], F32)
        eb = C.alloc([128, 4, 256], BF16)
        psb = [st.enter_context(nc.psum_tensor("ps%d" % i, [128, 512], F32))[:, :] for i in range(7)]
        pst = st.enter_context(nc.psum_tensor("pst", [128, 1024], BF16))[:, :]

        def dma(q, out, in_, reads=(), writes=(), pwrites=()):
            return P.emit(q, lambda e: e.dma_start(out=out, in_=in_), reads, writes, dma=True, pwrites=pwrites)

        def mm(out, lhsT, rhs, start, stop, reads, writes):
            return P.emit("pe", lambda e: e.matmul(out, lhsT=lhsT, rhs=rhs, start=start, stop=stop), reads, writes)

        def act(out, in_, func, reads, writes, bias=None, scale=None, accum_out=None):
            kw = {}
            if bias is not None:
                kw["bias"] = bias
            if scale is not None:
                kw["scale"] = scale
            if accum_out is not None:
                kw["accum_out"] = accum_out
            return P.emit("act", lambda e: e.activation(out=out, in_=in_, func=func, **kw), reads, writes)

        def ts(eng, out, in0, s1, s2, op0, op1, reads, writes):
            if s2 is None:
                return P.emit(eng, lambda e: e.tensor_scalar(out=out, in0=in0, scalar1=s1, scalar2=None, op0=op0),
                              reads, writes)
            return P.emit(eng, lambda e: e.tensor_scalar(out=out, in0=in0, scalar1=s1, scalar2=s2, op0=op0, op1=op1),
                          reads, writes)

        def tt(eng, out, in0, in1, op, reads, writes):
            return P.emit(eng, lambda e: e.tensor_tensor(out=out, in0=in0, in1=in1, op=op), reads, writes)

        def stt(out, in0, scalar, in1, op0, op1, reads, writes):
            return P.emit("dve", lambda e: e.scalar_tensor_tensor(out=out, in0=in0, scalar=scalar, in1=in1,
                                                                   op0=op0, op1=op1), reads, writes)

        def cp(eng, out, in_, reads, writes):
            if eng == "act":
                return P.emit("act", lambda e: e.copy(out=out, in_=in_), reads, writes)
            return P.emit(eng, lambda e: e.tensor_copy(out=out, in_=in_), reads, writes)

        def memset(eng, ap, val, writes):
            return P.emit(eng, lambda e: e.memset(ap, val), (), writes)

        memset("pool", ones_b, 1.0, ["ones_b"])
        memset("pool", ones_f, 1.0, ["ones_f"])
        memset("pool", ident, 1.0, ["ident"])
        P.emit("pool", lambda e: e.affine_select(out=ident, in_=ident, pattern=[[-1, 128]], compare_op=ALU.is_equal,
                                                 fill=0.0, base=0, channel_multiplier=1), ["ident"], ["ident"])
        memset("pool", tri, 1.0, ["tri"])
        P.emit("pool", lambda e: e.affine_select(out=tri[:, 0:128], in_=tri[:, 0:128], pattern=[[1, 128]],
                                                 compare_op=ALU.is_ge, fill=0.0, base=0, channel_multiplier=-1),
               ["tri"], ["tri"])
        for l in range(L if not dbg_only_sample else 0):
            for (src, dst, rows) in ((w_in, wb_in, D), (w_out, wb_out, D), (w_gate, wb_gate, D), (w_up, wb_up, D),
                                     (w_down, wb_down, DFF)):
                nsp = 4
                rr_ = rows // nsp
                for i in range(nsp):
                    dma("pool", dst[l, i * rr_:(i + 1) * rr_, :], src[l, i * rr_:(i + 1) * rr_, :],
                        writes=[("wb", id(dst), l)])
        cr = A.alloc([4, NT], BF16)
        memset("dve", cr, 1.0, ["cr"])
        dma("sp", fka[:, 64, :], cr, ["cr"], [])
        cr2 = A.alloc([4, NT], BF16)
        memset("dve", cr2, -8.0, ["cr2"])
        dma("sp", fqa[:, 65, :], cr2, ["cr2"], [])
        dma("sp", fqa[:, 66, :], cr2, ["cr2"], [])
        rb = A.alloc([32, 4], F32)
        oh = A.alloc([32, 384], F32)
        dma("sp", rb, rel_bias, (), ["rb"])
        dma("sp", oh, c_t5, (), ["oh"])
        mm(psb[0][0:4, 0:384], rb, oh, True, True, ["rb", "oh"], ["ps0"])
        gex = A.alloc([4, 384], BF16)
        act(gex, psb[0][0:4, 0:384], AF.Exp, ["ps0"], ["gex"])
        memset("dve", gex[:, 0:127], 0.0, ["gex"])
        dma("sp", t5s, gex, ["gex"], ["t5s"])
        for h in range(4):
            bsrc = bass.AP(tensor=t5s.tensor, offset=t5s.offset + h * 384, ap=[[0, 128], [1, 384]])
            dma("sp", t5b[h], bsrc, ["t5s"], ["t5b"])
            src = bass.AP(tensor=t5b.tensor, offset=t5b.offset + h * 128 * 384 + 127, ap=[[383, 128], [1, 256]])
            dma("sp", eb[:, h, :], src, ["t5b"], ["eb"])
        P.barrier()

        def load_gb(gsrc):
            dma("sp", gcol, gsrc.rearrange("(c p) -> p c", p=128), (), ["gcol"])
            for c in range(8):
                ts("dve", gb[:, c, :], ones_b, gcol[:, c:c + 1], None, ALU.mult, None, ["gcol", "ones_b"], ["gb"])

        def rmsnorm_T(xt, nsub, xnT, W, xk="xt"):
            for s in range(nsub):
                act(W["xsb"][:, s, :], xt[:, s, :], AF.Square, [xk], ["xsb", "ssq"], accum_out=W["ssq"][:, s:s + 1])
            ts("dve", W["rstd"][:, 0:nsub], W["ssq"][:, 0:nsub], 1.0 / D, EPS, ALU.mult, ALU.add, ["ssq"], ["rstd"])
            act(W["rstd"][:, 0:nsub], W["rstd"][:, 0:nsub], AF.Ln, ["rstd"], ["rstd"])
            act(W["rstd"][:, 0:nsub], W["rstd"][:, 0:nsub], AF.Exp, ["rstd"], ["rstd"], scale=-0.5)
            for s in range(nsub):
                ts("dve", W["xsb"][:, s, :], xt[:, s, :], W["rstd"][:, s:s + 1], None,
                   ALU.mult, None, [xk, "rstd"], ["xsb"])
            for s in range(nsub):
                for c in range(8):
                    P.emit("pe", lambda e, s=s, c=c: e.transpose(out=pst[:, c * 128:(c + 1) * 128],
                                                                  in_=W["xsb"][:, s, c * 128:(c + 1) * 128],
                                                                  identity=ident), ["xsb", "ident"], ["pst"])
                tt("dve", xnT[:, :, s * 128:(s + 1) * 128], pst.rearrange("p (c t) -> p c t", t=128), gb, ALU.mult,
                   ["pst", "gb"], ["xnT"])

        def load_w(dst, src, key):
            K = src.shape[0] // 128
            for k in range(K):
                dma("sp", dst[:, k, :], src[k * 128:(k + 1) * 128, :], [key[0]], (), pwrites=[key[1]])

        evac_rr = [0]

        def evac(out, in_, reads, writes):
            evac_rr[0] ^= 1
            return cp("act" if evac_rr[0] else "dve", out, in_, reads, writes)

        def phase1(l):
            A.reset()
            wA = A.alloc([128, 8, NIN], BF16)
            W = dict(ssq=A.alloc([128, 4], F32), rstd=A.alloc([128, 4], F32),
                     xsb=A.alloc([128, 4, 1024], BF16))
            xts = [A.alloc([128, 4, 1024], F32) for _ in range(2)]
            xnT = A.alloc([128, 8, 512], BF16)
            zst = [A.alloc([128, 512], BF16) for _ in range(4)]
            zsf = A.alloc([4, 512], F32)
            ztm = A.alloc([128, 4, 1028], F32)
            vst = A.alloc([128, 4, 512], BF16)
            bfb = A.alloc([128, 4], F32)
            lt = A.alloc([128, 4, 4], F32)
            load_w(wA, wb_in[l], (("wb", id(wb_in), l), "wA"))
            load_gb(norm_mix_g[l])
            dma("sp", bfb, fox_b_f[l].partition_broadcast(128), (), ["bfb"])
            groups = [(c * 128, 128, zfm[c * 128:(c + 1) * 128, :]) for c in range(10)]
            for h in range(4):
                groups.append((1280 + 64 * h, 64, fqa[h, 0:64, :]))
                groups.append((1536 + 64 * h, 64, fka[h, 0:64, :]))
                groups.append((2052 + 64 * h, 64, dqs[h, 0:64, :]))
                groups.append((2308 + 64 * h, 64, dks[h, 0:64, :]))
            def load_x(ti):
                t0, n = tiles[ti]
                if l == 0:
                    src = xp[t0:t0 + n, :] if t0 < T else xs
                else:
                    src = xres[t0:t0 + n, :]
                dma("sp", xts[ti % 2][:, 0:n // 128, :], src.rearrange("(s p) d -> p s d", p=128), (), ["xt%d" % (ti % 2)])

            load_x(0)
            for ti, (t0, n) in enumerate(tiles):
                nsub = n // 128
                if ti + 1 < len(tiles):
                    load_x(ti + 1)
                xt = xts[ti % 2]
                rmsnorm_T(xt, nsub, xnT, W, "xt%d" % (ti % 2))
                for gi, (c0, M, dst) in enumerate(groups):
                    ps = psb[gi % 4]
                    pk = "ps%d" % (gi % 4)
                    for k in range(8):
                        mm(ps[0:M, 0:n], wA[:, k, c0:c0 + M], xnT[:, k, 0:n], k == 0, k == 7, ["wA", "xnT"], [pk])
                    zk = "zst%d" % (gi % 4)
                    evac(zst[gi % 4][0:M, 0:n], ps[0:M, 0:n], [pk], [zk])
                    dma("pool" if gi % 2 else "sp", dst[:, t0:t0 + n], zst[gi % 4][0:M, 0:n], [zk], [])
                for k in range(8):
                    mm(psb[0][0:4, 0:n], wA[:, k, 2048:2052], xnT[:, k, 0:n], k == 0, k == 7, ["wA", "xnT"], ["ps0"])
                cp("dve", zsf[:, 0:n], psb[0][0:4, 0:n], ["ps0"], ["zsf"])
                dma("sp", flT[:, t0:t0 + n], zsf[:, 0:n], ["zsf"], [])
                for s in range(nsub):
                    for (pi, c0, ncol) in ((4, 1536, 512), (5, 2308, 512), (6, 2048, 4)):
                        for k in range(8):
                            mm(psb[pi][:, 0:ncol], xnT[:, k, s * 128:(s + 1) * 128], wA[:, k, c0:c0 + ncol],
                               k == 0, k == 7, ["wA", "xnT"], ["ps%d" % pi])
                    cp("act", ztm[:, s, 0:512], psb[4], ["ps4"], ["ztm"])
                    cp("dve", ztm[:, s, 512:1024], psb[5], ["ps5"], ["ztm"])
                    tt("dve", lt[:, s, :], psb[6][:, 0:4], bfb, ALU.add, ["ps6", "bfb"], ["lt"])
                act(lt[:, 0:nsub, :], lt[:, 0:nsub, :], AF.Exp, ["lt"], ["lt"], scale=-1.0)
                act(lt[:, 0:nsub, :], lt[:, 0:nsub, :], AF.Ln, ["lt"], ["lt"], bias=1.0)
                ts("dve", ztm[:, 0:nsub, 1024:1028], lt[:, 0:nsub, :], -1.0, None, ALU.mult, None, ["lt"], ["ztm"])
                cp("dve", vst[:, 0:nsub, 0:256], ztm[:, 0:nsub, 256:512], ["ztm"], ["vst"])
                cp("act", vst[:, 0:nsub, 256:512], ztm[:, 0:nsub, 768:1024], ["ztm"], ["vst"])
                if t0 < T:
                    outs = (o_pfk, o_pfv, o_pdk, o_pdv, o_pflf)
                    r0 = t0
                else:
                    outs = (o_sfk, o_sfv, o_sdk, o_sdv, o_sflf)
                    r0 = 0
                for oi, o in enumerate(outs):
                    c0, cn = (oi * 256, 256) if oi < 4 else (1024, 4)
                    dma("pool", o[l, r0:r0 + n, :].rearrange("(s p) c -> p s c", p=128), ztm[:, 0:nsub, c0:c0 + cn],
                        ["ztm"], [])
                dma("sp", lfs[t0:t0 + n, :].rearrange("(s p) c -> p s c", p=128), ztm[:, 0:nsub, 1024:1028],
                    ["ztm"], [])
                dma("sp", vsf[t0:t0 + n, :].rearrange("(s p) c -> p s c", p=128), vst[:, 0:nsub, 0:256],
                    ["vst"], [])
                dma("sp", vsd[t0:t0 + n, :].rearrange("(s p) c -> p s c", p=128), vst[:, 0:nsub, 256:512],
                    ["vst"], [])
            P.barrier()

        def phase2_frows(l):
            A.reset()
            fl = A.alloc([4, T], F32)
            Fc = A.alloc([4, T], F32)
            fhi = A.alloc([4, T], BF16)
            flo = A.alloc([4, T], BF16)
            f8 = A.alloc([4, T], BF16)
            bfc = A.alloc([4, 1], F32)
            onesT = A.alloc([4, T], F32)
            dma("sp", fl, flT[:, 0:T], (), ["fl"])
            dma("sp", bfc, fox_b_f[l].rearrange("(h o) -> h o", o=1), (), ["bfc"])
            memset("pool", onesT, 1.0, ["onesT"])
            ts("dve", fl, fl, bfc[:, 0:1], -1.0, ALU.add, ALU.mult, ["fl", "bfc"], ["fl"])
            act(fl, fl, AF.Exp, ["fl"], ["fl"])
            act(fl, fl, AF.Ln, ["fl"], ["fl"], bias=1.0)
            ts("dve", fl, fl, -1.0, None, ALU.mult, None, ["fl"], ["fl"])
            P.emit("dve", lambda e: e.tensor_tensor_scan(out=Fc, data0=onesT, data1=fl, initial=0.0, op0=ALU.mult,
                                                         op1=ALU.add), ["fl", "onesT"], ["Fc"])
            cp("dve", fhi, Fc, ["Fc"], ["fhi"])
            tt("dve", flo, Fc, fhi, ALU.subtract, ["Fc", "fhi"], ["flo"])
            ts("dve", f8, fhi, 8.0, None, ALU.mult, None, ["fhi"], ["f8"])
            dma("sp", fka[:, 65, 0:T], fhi, ["fhi"], [])
            dma("sp", fka[:, 66, 0:T], flo, ["flo"], [])
            dma("sp", fqa[:, 64, 0:T], f8, ["f8"], [])
            P.barrier()

        def phase2_rec(l, sample):
            A.reset()
            S, TT, nt = (NS, TS, 1) if sample else (1, 1024, T // 1024)
            tb = T if sample else 0
            NW = S * TT
            cw = A.alloc([128, 2, 4], F32)
            cbias = A.alloc([128, 2], F32)
            ba = A.alloc([128, 2], F32)
            bx = A.alloc([128, 2], F32)
            lam = A.alloc([128, 2], F32)
            cl = A.alloc([128, 2], F32)
            scw = A.alloc([128, 2, 3], F32)
            wa_f = A.alloc([128, 2, 128], F32)
            wx_f = A.alloc([128, 2, 128], F32)
            wa_b = A.alloc([128, 2, 128], BF16)
            wx_b = A.alloc([128, 2, 128], BF16)
            for c_ in range(2):
                dma("sp", cw[:, c_, :], lru_conv_w[l, :, c_ * 128:(c_ + 1) * 128].rearrange("k p -> p k"), (), ["cw"])
            dma("sp", cbias, lru_conv_b[l].rearrange("(c p) -> p c", p=128), (), ["cw"])
            dma("sp", ba, lru_b_a[l].rearrange("(c p) -> p c", p=128), (), ["cw"])
            dma("sp", bx, lru_b_x[l].rearrange("(c p) -> p c", p=128), (), ["cw"])
            dma("sp", lam, lru_lambda[l].rearrange("(c p) -> p c", p=128), (), ["lam"])
            for c_ in range(2):
                dma("sp", scw[:, c_, :], sc_conv_w[l, :, c_ * 128:(c_ + 1) * 128].rearrange("k p -> p k"), (), ["cw"])
            memset("dve", wa_f, 0.0, ["wa_f"])
            memset("dve", wx_f, 0.0, ["wx_f"])
            for c in range(2):
                for b in range(2):
                    dma("sp", wa_f[b * 64:(b + 1) * 64, c, b * 64:(b + 1) * 64], lru_w_a[l, 2 * c + b], (), ["wa_f"])
                    dma("sp", wx_f[b * 64:(b + 1) * 64, c, b * 64:(b + 1) * 64], lru_w_x[l, 2 * c + b], (), ["wx_f"])
            cp("dve", wa_b, wa_f, ["wa_f"], ["wa_b"])
            cp("dve", wx_b, wx_f, ["wx_f"], ["wx_b"])
            act(cl, lam, AF.Exp, ["lam"], ["cl"], scale=-1.0)
            act(cl, cl, AF.Ln, ["cl"], ["cl"], bias=1.0)
            ts("dve", cl, cl, -8.0, None, ALU.mult, None, ["cl"], ["cl"])

            zb = A.alloc([128, NW], BF16)
            xl = A.alloc([128, S, 3 + TT], F32)
            xc = A.alloc([128, S, TT], F32)
            xcb = A.alloc([128, NW], BF16)
            rg = A.alloc([128, NW], F32)
            ig = A.alloc([128, NW], F32)
            av = A.alloc([128, NW], F32)
            uv = A.alloc([128, NW], F32)
            hv = A.alloc([128, NW], F32)
            gt = A.alloc([128, NW], BF16)
            g2 = A.alloc([128, NW], F32)
            yb = A.alloc([128, NW], BF16)
            h0 = A.alloc([128, S], F32)
            hist = A.alloc([128, S, 3], F32)
            cx = A.alloc([128, S, 2 + TT], F32)
            z2 = A.alloc([128, NW], BF16)
            z3 = A.alloc([128, NW], BF16)
            v3 = lambda ap: ap.rearrange("p (s t) -> p s t", t=TT)
            for c in range(2):
                r_ = slice(c * 128, (c + 1) * 128)
                for j in range(nt):
                    t0 = tb + j * NW
                    dma("sp", zb, zfm[c * 128:(c + 1) * 128, t0:t0 + NW], (), ["zb"])
                    if j > 0:
                        cp("dve", hist, xl[:, :, TT:TT + 3], ["xl"], ["hist"])
                    cp("dve", xl[:, :, 3:3 + TT], v3(zb), ["zb"], ["xl"])
                    if j > 0:
                        cp("dve", xl[:, :, 0:3], hist, ["hist"], ["xl"])
                    elif sample:
                        for k_ in range(3):
                            dma("sp", xl[:, :, k_], st_lru_conv[l, :, k_, r_].rearrange("s p -> p s"), (), ["xl"])
                        dma("sp", h0, st_lru_h[l, :, r_].rearrange("s p -> p s"), (), ["h0"])
                    else:
                        memset("dve", xl[:, :, 0:3], 0.0, ["xl"])
                    ts("dve", xc, xl[:, :, 0:TT], cw[:, c, 0:1], cbias[:, c:c + 1], ALU.mult, ALU.add, ["xl", "cw"], ["xc"])
                    for k in range(1, 4):
                        stt(xc, xl[:, :, k:k + TT], cw[:, c, k:k + 1], xc, ALU.mult, ALU.add, ["xl", "cw", "xc"], ["xc"])
                    cp("act", v3(xcb), xc, ["xc"], ["xcb"])
                    for hf in range(0, NW, 512):
                        n = min(512, NW - hf)
                        mm(psb[0][:, 0:n], wa_b[:, c, :], xcb[:, hf:hf + n], True, True, ["wa_b", "xcb"], ["ps0"])
                        act(rg[:, hf:hf + n], psb[0][:, 0:n], AF.Sigmoid, ["ps0", "cw"], ["rg"], bias=ba[:, c:c + 1])
                        mm(psb[1][:, 0:n], wx_b[:, c, :], xcb[:, hf:hf + n], True, True, ["wx_b", "xcb"], ["ps1"])
                        act(ig[:, hf:hf + n], psb[1][:, 0:n], AF.Sigmoid, ["ps1", "cw"], ["ig"], bias=bx[:, c:c + 1])
                    act(av, rg, AF.Exp, ["rg", "cl"], ["av"], scale=cl[:, c:c + 1])
                    tt("dve", uv, av, av, ALU.mult, ["av"], ["uv"])
                    ts("dve", uv, uv, -1.0, 1.0, ALU.mult, ALU.add, ["uv"], ["uv"])
                    ts("dve", uv, uv, 0.0, None, ALU.max, None, ["uv"], ["uv"])
                    act(uv, uv, AF.Sqrt, ["uv"], ["uv"])
                    tt("dve", uv, uv, ig, ALU.mult, ["uv", "ig"], ["uv"])
                    tt("dve", v3(uv), v3(uv), xc, ALU.mult, ["uv", "xc"], ["uv"])
                    if j > 0:
                        cp("dve", h0[:, 0:1], hv[:, NW - 1:NW], ["hv"], ["h0"])
                    for s in range(S):
                        init = 0.0 if (not sample and j == 0) else h0[:, s:s + 1]
                        P.emit("dve", lambda e, s=s, init=init: e.tensor_tensor_scan(
                            out=hv[:, s * TT:(s + 1) * TT], data0=av[:, s * TT:(s + 1) * TT],
                            data1=uv[:, s * TT:(s + 1) * TT], initial=init, op0=ALU.mult, op1=ALU.add),
                            ["av", "uv", "h0"], ["hv"])
                    dma("sp", gt, zfm[256 + c * 128:256 + (c + 1) * 128, t0:t0 + NW], (), ["gt"])
                    tt("dve", g2, gt, gt, ALU.mult, ["gt"], ["g2"])
                    ts("dve", g2, g2, 0.044715 * 0.7978845608028654, 0.7978845608028654, ALU.mult, ALU.add, ["g2"], ["g2"])
                    tt("dve", g2, g2, gt, ALU.mult, ["g2", "gt"], ["g2"])
                    act(g2, g2, AF.Tanh, ["g2"], ["g2"])
                    stt(g2, g2, 1.0, gt, ALU.add, ALU.mult, ["g2", "gt"], ["g2"])
                    stt(yb, g2, 0.5, hv, ALU.mult, ALU.mult, ["g2", "hv"], ["yb"])
                    dma("pool", mixT[c * 128:(c + 1) * 128, t0:t0 + NW], yb, ["yb"], [])
                if sample:
                    dma("pool", o_slh[l, :, r_].rearrange("s p -> p s"), v3(hv)[:, :, TT - 1], ["hv"], [])
                    for k_ in range(3):
                        dma("pool", o_slc[l, :, k_, r_].rearrange("s p -> p s"), xl[:, :, TT + k_], ["xl"], [])
                else:
                    dma("pool", o_plh[l, r_].rearrange("(p o) -> p o", o=1), hv[:, NW - 1:NW], ["hv"], [])
                    dma("pool", o_plc[l, :, r_].rearrange("k p -> p k"), xl[:, 0, TT:TT + 3], ["xl"], [])
                for j in range(nt):
                    t0 = tb + j * NW
                    dma("sp", zb, zfm[512 + c * 128:512 + (c + 1) * 128, t0:t0 + NW], (), ["zb"])
                    dma("sp", z2, zfm[768 + c * 128:768 + (c + 1) * 128, t0:t0 + NW], (), ["z2"])
                    dma("sp", z3, zfm[1024 + c * 128:1024 + (c + 1) * 128, t0:t0 + NW], (), ["z3"])
                    if j > 0:
                        cp("dve", hist[:, :, 0:2], cx[:, :, TT:TT + 2], ["cx"], ["hist"])
                    tt("dve", cx[:, :, 2:2 + TT], v3(z2), v3(z3), ALU.mult, ["z2", "z3"], ["cx"])
                    if j > 0:
                        cp("dve", cx[:, :, 0:2], hist[:, :, 0:2], ["hist"], ["cx"])
                    elif sample:
                        for k_ in range(2):
                            dma("sp", cx[:, :, k_], st_sconv[l, :, k_, r_].rearrange("s p -> p s"), (), ["cx"])
                    else:
                        memset("dve", cx[:, :, 0:2], 0.0, ["cx"])
                    ts("dve", xc, cx[:, :, 0:TT], scw[:, c, 0:1], None, ALU.mult, None, ["cx", "cw"], ["xc"])
                    for k in range(1, 3):
                        stt(xc, cx[:, :, k:k + TT], scw[:, c, k:k + 1], xc, ALU.mult, ALU.add, ["cx", "cw", "xc"], ["xc"])
                    tt("dve", v3(yb), xc, v3(zb), ALU.mult, ["xc", "zb"], ["yb"])
                    dma("pool", mixT[256 + c * 128:256 + (c + 1) * 128, t0:t0 + NW], yb, ["yb"], [])
                if sample:
                    for k_ in range(2):
                        dma("pool", o_ssc[l, :, k_, r_].rearrange("s p -> p s"), cx[:, :, TT + k_], ["cx"], [])
                else:
                    dma("pool", o_psc[l, :, r_].rearrange("k p -> p k"), cx[:, 0, TT:TT + 2], ["cx"], [])
            P.barrier()

        def lam_col(l, dst):
            lp = A.alloc([128, 128], F32)
            pr = A.alloc([128, 64], F32)
            sm = A.alloc([128, 2], F32)
            dma("sp", lp, diff_lambda[l].partition_broadcast(128), (), ["lp"])
            tt("dve", pr[:, 0:32], lp[:, 0:32], lp[:, 32:64], ALU.mult, ["lp"], ["pr"])
            tt("dve", pr[:, 32:64], lp[:, 64:96], lp[:, 96:128], ALU.mult, ["lp"], ["pr"])
            P.emit("dve", lambda e: e.reduce_sum(out=sm, in_=pr.rearrange("p (a b) -> p a b", b=32),
                                                 axis=mybir.AxisListType.X), ["pr"], ["sm"])
            act(sm, sm, AF.Exp, ["sm"], ["sm"])
            tt("dve", dst, sm[:, 0:1], sm[:, 1:2], ALU.subtract, ["sm"], ["lamc"])
            ts("dve", dst, dst, 0.8 - 0.6 * math.exp(-0.3 * l), None, ALU.add, None, ["lamc"], ["lamc"])

        def phase2_attn_prompt(l):
            A.reset()
            lam_init = 0.8 - 0.6 * math.exp(-0.3 * l)
            Qa = A.alloc([67, T], BF16)
            Ka = A.alloc([67, T], BF16)
            Va = A.alloc([128, T // 128, 128], BF16)
            pts = [A.alloc([128, 512], BF16) for _ in range(3)]
            rs = A.alloc([64, 512], F32)
            o1 = A.alloc([64, 512], F32)
            o2 = A.alloc([64, 512], F32)
            sq = A.alloc([64, 512], BF16)
            ys = [A.alloc([64, 512], BF16) for _ in range(2)]
            lamc = A.alloc([128, 1], F32)
            dg = A.alloc([64, 1], F32)
            o64 = A.alloc([64, 64], BF16)
            lam_col(l, lamc)
            dma("sp", dg, diff_norm_g[l].rearrange("(p o) -> p o", o=1), (), ["dg"])
            ts("dve", dg, dg, 1.0 - lam_init, None, ALU.mult, None, ["dg"], ["dg"])
            memset("pool", o64, 1.0 / 64.0, ["o64"])
            memset("pool", Va[:, :, 64:128], 1.0, ["Va1"])
            NQ = T // 512
            yi = [0]

            def run_map(qt, kq, kk, krows, scale, band, acc, acck, sti):
                q0 = qt * 512
                nkc = 4 * qt + 4
                steps = []
                for kc in range(nkc):
                    j = kc - 4 * qt
                    n0 = 128 * max(0, j)
                    steps.append((kc, j, n0, 512 - n0))

                def qk(i):
                    kc, j, n0, N = steps[i]
                    b = sti + (i % 2)
                    mm(psb[b][:, 0:N], kk[krows, kc * 128:(kc + 1) * 128], kq[krows, q0 + n0:q0 + 512], True, True,
                       ["Ka", "Qa"], ["ps%d" % b])

                qk(0)
                for i, (kc, j, n0, N) in enumerate(steps):
                    if i + 1 < len(steps):
                        qk(i + 1)
                    b = sti + (i % 2)
                    pt = pts[i % 3]
                    pk = "pt%d" % (i % 3)
                    act(pt[:, 0:N], psb[b][:, 0:N], AF.Exp, ["ps%d" % b], [pk], scale=scale)
                    if band is not None:
                        if j >= 0:
                            w = 256 if j <= 2 else 128
                            tt("dve", pt[:, 0:w], pt[:, 0:w], band[:, 0:w], ALU.mult, [pk, "eb", "tri"], [pk])
                        elif j == -1 and band is not tri:
                            tt("dve", pt[:, 0:128], pt[:, 0:128], band[:, 128:256], ALU.mult, [pk, "eb"], [pk])
                    mm(acc[:, n0:512], Va[:, kc, :], pt[:, 0:N], i == 0, i == len(steps) - 1, ["Va", "Va1", pk], [acck])

            for typ in ("fox", "diff"):
                for h in range(4):
                    if typ == "fox":
                        dma("sp", Qa, fqa[h, :, 0:T], (), ["Qa"])
                        dma("sp", Ka, fka[h, :, 0:T], (), ["Ka"])
                        vsrc = vsf
                    else:
                        dma("sp", Qa[0:64, :], dqs[h, :, 0:T], (), ["Qa"])
                        dma("sp", Ka[0:64, :], dks[h, :, 0:T], (), ["Ka"])
                        vsrc = vsd
                    dma("sp", Va[:, :, 0:64], vsrc[0:T, h * 64:(h + 1) * 64].rearrange("(c p) e -> p c e", p=128),
                        (), ["Va"])
                    for qt in range(NQ):
                        y = ys[yi[0] % 2]
                        yk = "ys%d" % (yi[0] % 2)
                        yi[0] += 1
                        if typ == "fox":
                            run_map(qt, Qa, Ka, slice(0, 67), 0.125, tri, psb[4], "ps4", 0)
                            P.emit("dve", lambda e: e.reciprocal(out=rs, in_=psb[4][64:128, :]), ["ps4"], ["rs"])
                            tt("dve", y, psb[4][0:64, :], rs, ALU.mult, ["ps4", "rs"], [yk])
                            dma("pool", mixT[512 + h * 64:512 + (h + 1) * 64, qt * 512:(qt + 1) * 512], y, [yk], [])
                        else:
                            sc_ = 32 ** -0.5
                            run_map(qt, Qa, Ka, slice(0, 32), sc_, eb[:, h, :], psb[4], "ps4", 0)
                            run_map(qt, Qa, Ka, slice(32, 64), sc_, eb[:, h, :], psb[5], "ps5", 2)
                            P.emit("dve", lambda e: e.reciprocal(out=rs, in_=psb[4][64:128, :]), ["ps4"], ["rs"])
                            tt("dve", o1, psb[4][0:64, :], rs, ALU.mult, ["ps4", "rs"], ["o1"])
                            P.emit("dve", lambda e: e.reciprocal(out=rs, in_=psb[5][64:128, :]), ["ps5"], ["rs"])
                            tt("dve", o2, psb[5][0:64, :], rs, ALU.mult, ["ps5", "rs"], ["o2"])
                            stt(o1, o2, lamc[0:64, 0:1], o1, ALU.mult, ALU.subtract, ["o1", "o2", "lamc"], ["o1"])
                            tt("dve", sq, o1, o1, ALU.mult, ["o1"], ["sq"])
                            mm(psb[6][0:64, :], o64, sq, True, True, ["o64", "sq"], ["ps6"])
                            ts("dve", rs, psb[6][0:64, :], EPS, None, ALU.add, None, ["ps6"], ["rs"])
                            act(rs, rs, AF.Ln, ["rs"], ["rs"])
                            act(rs, rs, AF.Exp, ["rs"], ["rs"], scale=-0.5)
                            tt("dve", o1, o1, rs, ALU.mult, ["o1", "rs"], ["o1"])
                            ts("dve", y, o1, dg[:, 0:1], -1.0, ALU.mult, ALU.mult, ["o1", "dg"], [yk])
                            dma("pool", mixT[768 + h * 64:768 + (h + 1) * 64, qt * 512:(qt + 1) * 512], y, [yk], [])
            P.barrier()


        def phase2_attn_sample(l):
            A.reset()
            lam_init = 0.8 - 0.6 * math.exp(-0.3 * l)
            triF = A.alloc([128, 128], F32)
            onesF = A.alloc([128, 128], F32)
            memset("dve", onesF, 1.0, ["onesF"])
            memset("pool", triF, 1.0, ["triF"])
            P.emit("pool", lambda e: e.affine_select(out=triF, in_=triF, pattern=[[1, 128]], compare_op=ALU.is_ge,
                                                     fill=0.0, base=0, channel_multiplier=-1), ["triF"], ["triF"])
            pti = A.alloc([128, NS * NPG], I32)
            idxf = A.alloc([128, NS * NPG], F32)
            idxi = A.alloc([128, NS * NPG], I32)
            iopi = A.alloc([128, 1], I32)
            iop = A.alloc([128, 1], F32)
            dma("sp", pti, ptab.partition_broadcast(128), (), ["pti"])
            P.emit("pool", lambda e: e.iota(iopi, pattern=[[0, 1]], base=0, channel_multiplier=1), (), ["iopi"])
            cp("dve", iop, iopi, ["iopi"], ["iop"])
            cp("dve", idxf, pti, ["pti"], ["idxf"])
            ts("dve", idxf, idxf, 128.0, iop[:, 0:1], ALU.mult, ALU.add, ["idxf", "iop"], ["idxf"])
            ts("dve", idxf, idxf, float(l * NPOOL * 128), None, ALU.add, None, ["idxf"], ["idxf"])
            cp("dve", idxi, idxf, ["idxf"], ["idxi"])
            qs = A.alloc([64, 4, NTS], BF16)
            ks = A.alloc([64, 4, NTS], BF16)
            qd = A.alloc([64, 4, NTS], BF16)
            kd = A.alloc([64, 4, NTS], BF16)
            for h in range(4):
                dma("sp", qs[:, h, :], fqa[h, 0:64, T:NT], (), (), pwrites=["qs"])
                dma("sp", ks[:, h, :], fka[h, 0:64, T:NT], (), (), pwrites=["qs"])
                dma("sp", qd[:, h, :], dqs[h, :, T:NT], (), (), pwrites=["qs"])
                dma("sp", kd[:, h, :], dks[h, :, T:NT], (), (), pwrites=["qs"])
            lamc = A.alloc([128, 1], F32)
            lam_col(l, lamc)
            dg2 = A.alloc([128, 1], F32)
            for b in range(2):
                dma("sp", dg2[b * 64:(b + 1) * 64, :], diff_norm_g[l].rearrange("(p o) -> p o", o=1), (), ["dg2"])
            ts("dve", dg2, dg2, -(1.0 - lam_init), None, ALU.mult, None, ["dg2"], ["dg2"])
            o64 = A.alloc([128, 128], BF16)
            memset("pool", o64, 0.0, ["o64"])
            memset("pool", o64[0:64, 0:64], 1.0 / 64.0, ["o64"])
            memset("pool", o64[64:128, 64:128], 1.0 / 64.0, ["o64"])
            gk = A.alloc([128, NPG, 256], F32)
            gv = A.alloc([128, NPG, 256], F32)
            glf = A.alloc([128, NPG * 4], F32)
            bk = A.alloc([128, NPG, 256], BF16)
            bv = A.alloc([128, NPG, 256], BF16)
            ktT = A.alloc([64, NPG, 512], BF16)
            p_all = A.alloc([128, NPG * 64], BF16)
            pn = A.alloc([8, 64], BF16)
            tmpS = A.alloc([128, 512], F32)
            bias = A.alloc([128, NPG * 4], F32)
            wth = A.alloc([128, NPG * 4], F32)
            tot = A.alloc([128, NPG * 4], F32)
            inc = A.alloc([128, NPG * 4], F32)
            ones16 = A.alloc([128, NPG], F32)
            memset("dve", ones16, 1.0, ["ones16"])
            lfn = A.alloc([8, 4], F32)
            bnew = A.alloc([8, 4], F32)
            vnew = A.alloc([8, 256], BF16)
            rsum = A.alloc([128, 64], F32)
            yf = A.alloc([128, 2, NTS], BF16)
            od = A.alloc([128, 2, 2, NTS], F32)
            odn = A.alloc([128, 2, NTS], F32)
            sqb = A.alloc([128, 2, NTS], BF16)
            rst = A.alloc([128, 2 * NTS], F32)
            yd = A.alloc([128, 2, NTS], BF16)
            pst2 = psb[3].bitcast(BF16)
            ti = [0]

            def gather(dst, cache, s, key):
                for j in range(NPG):
                    col = s * NPG + j
                    P.emit("pool", lambda e, j=j, col=col: e.indirect_dma_start(
                        out=dst[:, j, :] if len(dst.shape) == 3 else dst[:, j * 4:(j + 1) * 4], out_offset=None,
                        in_=cache, in_offset=bass.IndirectOffsetOnAxis(ap=idxi[:, col:col + 1], axis=0)),
                        ["idxi"], (), dma=True, pwrites=[key])

            for s in range(dbg_only_sample or NS):
                c0 = s * TS
                for typ in ("fox", "diff"):
                    fox = typ == "fox"
                    it = ti[0]
                    ti[0] += 1
                    gather(gk, c_fk if fox else c_dk, s, "gk")
                    gather(gv, c_fv if fox else c_dv, s, "gv")
                    cp("dve", bk, gk, ["gk"], ["bk"])
                    cp("act", bv, gv, ["gv"], ["bv"])
                    dma("sp", vnew, (vsf if fox else vsd)[T + c0:T + c0 + TS, :], (), ["vnew"])
                    if fox:
                        gather(glf, c_flf, s, "glf")
                        mm(psb[2][:, 0:64], triF, glf, True, True, ["triF", "glf"], ["ps2"])
                        mm(psb[2][:, 64:128], onesF, glf, True, True, ["onesF", "glf"], ["ps2"])
                        cp("dve", wth, psb[2][:, 0:64], ["ps2"], ["wth"])
                        cp("dve", tot, psb[2][:, 64:128], ["ps2"], ["tot"])
                        for h in range(4):
                            P.emit("dve", lambda e, h=h: e.tensor_tensor_scan(
                                out=inc.rearrange("p (j h) -> p j h", h=4)[:, :, h], data0=ones16,
                                data1=tot.rearrange("p (j h) -> p j h", h=4)[:, :, h], initial=0.0, op0=ALU.mult,
                                op1=ALU.add), ["tot", "ones16"], ["inc"])
                        tt("dve", bias, tot, inc, ALU.subtract, ["tot", "inc"], ["bias"])
                        tt("dve", bias, bias, wth, ALU.subtract, ["bias", "wth"], ["bias"])
                        for h in range(4):
                            bh = bias.rearrange("p (j h) -> p j h", h=4)[:, :, h]
                            ts("dve", bh, bh, inc[:, 60 + h:61 + h], 8.0, ALU.add, ALU.mult, ["bias", "inc"], ["bias"])
                        dma("sp", lfn, lfs[T + c0:T + c0 + TS, :], (), ["lfn"])
                        mm(psb[2][0:8, 128:132], triF[0:8, 0:8], lfn, True, True, ["triF", "lfn"], ["ps2"])
                        ts("dve", bnew, psb[2][0:8, 128:132], -1.0, None, ALU.mult, None, ["ps2"], ["bnew"])
                    ncol = 32 if fox else 64
                    qq, kk = (qs, ks) if fox else (qd, kd)
                    for jp in range(NPG // 2):
                        tb_ = pst
                        tk_ = "pst"
                        for jj in range(2):
                            j = 2 * jp + jj
                            for h in range(4):
                                P.emit("pe", lambda e, h=h, j=j, jj=jj, tb_=tb_: e.transpose(
                                    out=tb_[0:64, (jj * 4 + h) * 128:(jj * 4 + h + 1) * 128],
                                    in_=bk[:, j, h * 64:(h + 1) * 64], identity=ident), ["bk", "ident"], [tk_])
                        evac(ktT[:, 2 * jp:2 * jp + 2, :], tb_[0:64, :].rearrange("p (a b) -> p a b", b=512), [tk_], ["ktT"])
                    for j in range(NPG):
                        for h in range(4):
                            for m in range(1 if fox else 2):
                                rows = slice(0, 64) if fox else slice(32 * m, 32 * m + 32)
                                cc = j * 32 + h * 8
                                dst_ = psb[m][:, cc:cc + 8]
                                dk_ = "ps%d" % m
                                mm(dst_, ktT[rows, j, h * 128:(h + 1) * 128], qq[rows, h, c0:c0 + TS], True, True,
                                   ["ktT", "qs"], [dk_])
                    for h in range(4):
                        for m in range(1 if fox else 2):
                            rows = slice(0, 64) if fox else slice(32 * m, 32 * m + 32)
                            nb_ = psb[2][0:8, 192 + h * 8:192 + (h + 1) * 8] if m == 0 else psb[3][0:8, h * 8:(h + 1) * 8]
                            mm(nb_, kk[rows, h, c0:c0 + TS], qq[rows, h, c0:c0 + TS], True, True,
                               ["qs"], ["ps2" if m == 0 else "ps3"])
                    if fox:
                        tt("dve", tmpS.rearrange("p (a q) -> p a q", q=8), psb[0].rearrange("p (a q) -> p a q", q=8),
                           bias.unsqueeze(2).broadcast_to([128, NPG * 4, 8]), ALU.add, ["ps0", "bias"], ["tmpS"])
                        act(p_all[:, 0:512], tmpS, AF.Exp, ["tmpS"], ["p_all"], scale=0.125)
                        for h in range(4):
                            act(pn[:, h * 8:(h + 1) * 8], psb[2][0:8, 192 + h * 8:192 + (h + 1) * 8], AF.Exp,
                                ["ps2", "bnew"], ["pn"], bias=bnew[:, h:h + 1], scale=0.125)
                        tt("dve", pn[:, 0:32].rearrange("p (h q) -> p h q", q=8),
                           pn[:, 0:32].rearrange("p (h q) -> p h q", q=8),
                           tri[0:8, 0:8].unsqueeze(1).broadcast_to([8, 4, 8]), ALU.mult, ["pn", "tri"], ["pn"])
                    else:
                        sc_ = 32 ** -0.5
                        for m in range(2):
                            act(p_all.rearrange("p (j h m q) -> p j h m q", h=4, m=2, q=8)[:, :, :, m, :],
                                psb[m].rearrange("p (j h q) -> p j h q", h=4, q=8), AF.Exp, ["ps%d" % m], ["p_all"],
                                scale=sc_)
                        act(pn.rearrange("p (h m q) -> p h m q", m=2, q=8)[:, :, 0, :],
                            psb[2][0:8, 192:224].rearrange("p (h q) -> p h q", q=8), AF.Exp, ["ps2"], ["pn"], scale=sc_)
                        act(pn.rearrange("p (h m q) -> p h m q", m=2, q=8)[:, :, 1, :],
                            psb[3][0:8, 0:32].rearrange("p (h q) -> p h q", q=8), AF.Exp, ["ps3"], ["pn"], scale=sc_)
                        for h in range(4):
                            v_ = p_all[:, 15 * 64 + h * 16:15 * 64 + (h + 1) * 16].rearrange("p (m q) -> p m q", q=8)
                            tt("dve", v_, v_, eb[:, h, 128:136].unsqueeze(1).broadcast_to([128, 2, 8]), ALU.mult,
                               ["p_all", "eb"], ["p_all"])
                            v2 = pn[:, h * 16:(h + 1) * 16].rearrange("p (m q) -> p m q", q=8)
                            tt("dve", v2, v2, eb[0:8, h, 0:8].unsqueeze(1).broadcast_to([8, 2, 8]), ALU.mult,
                               ["pn", "eb"], ["pn"])
                    for j in range(NPG + 1):
                        new = j == NPG
                        nk = 8 if new else 128
                        rhs_ = pn[0:8, 0:ncol] if new else p_all[:, j * ncol:(j + 1) * ncol]
                        rk = "pn" if new else "p_all"
                        for hp in range(2):
                            lhs = vnew[:, hp * 128:(hp + 1) * 128] if new else bv[:, j, hp * 128:(hp + 1) * 128]
                            mm(psb[4 + hp][:, 0:ncol], lhs, rhs_, j == 0, new,
                               (["vnew"] if new else ["bv"]) + [rk], ["ps%d" % (4 + hp)])
                        mm(psb[6][:, 0:ncol], ones_b[0:nk, :], rhs_, j == 0, new, ["ones_b", rk], ["ps6"])
                    P.emit("dve", lambda e, ncol=ncol: e.reciprocal(out=rsum[:, 0:ncol], in_=psb[6][:, 0:ncol]),
                           ["ps6"], ["rsum"])
                    for h in range(4):
                        pr = slice((h % 2) * 64, (h % 2) * 64 + 64)
                        ab = psb[4 + h // 2]
                        ak = "ps%d" % (4 + h // 2)
                        if fox:
                            tt("dve", yf[pr, h // 2, c0:c0 + TS], ab[pr, h * 8:(h + 1) * 8],
                               rsum[pr, h * 8:(h + 1) * 8], ALU.mult, [ak, "rsum"], ["yf"])
                        else:
                            for m in range(2):
                                cc = (h * 2 + m) * 8
                                tt("dve", od[pr, m, h // 2, c0:c0 + TS], ab[pr, cc:cc + 8], rsum[pr, cc:cc + 8],
                                   ALU.mult, [ak, "rsum"], ["od"])
            stt(odn, od[:, 1], lamc[:, 0:1], od[:, 0], ALU.mult, ALU.subtract, ["od", "lamc"], ["odn"])
            tt("dve", sqb, odn, odn, ALU.mult, ["odn"], ["sqb"])
            mm(psb[0][:, 0:2 * NTS], o64, sqb.rearrange("p a t -> p (a t)"), True, True, ["o64", "sqb"], ["ps0"])
            ts("dve", rst, psb[0][:, 0:2 * NTS], EPS, None, ALU.add, None, ["ps0"], ["rst"])
            act(rst, rst, AF.Sqrt, ["rst"], ["rst"])
            P.emit("dve", lambda e: e.reciprocal(out=rst, in_=rst), ["rst"], ["rst"])
            tt("dve", odn, odn, rst.rearrange("p (a t) -> p a t", t=NTS), ALU.mult, ["odn", "rst"], ["odn"])
            ts("dve", yd, odn, dg2[:, 0:1], None, ALU.mult, None, ["odn", "dg2"], ["yd"])
            for c in range(2):
                dma("pool", mixT[512 + c * 128:512 + (c + 1) * 128, T:NT], yf[:, c, :], ["yf"], [])
                dma("pool", mixT[768 + c * 128:768 + (c + 1) * 128, T:NT], yd[:, c, :], ["yd"], [])
            P.barrier()

        def phase3(l):
            A.reset()
            wA = A.alloc([128, 8, DFF], BF16)
            wB = A.alloc([128, 8, DFF], BF16)
            wC = A.alloc([128, 8, D], BF16)
            W = dict(ssq=A.alloc([128, 4], F32), rstd=A.alloc([128, 4], F32),
                     xsb=A.alloc([128, 4, 1024], BF16))
            xt = A.alloc([128, 4, 1024], F32)
            xnT = A.alloc([128, 8, 512], BF16)
            mts = [A.alloc([128, 8, 512], BF16) for _ in range(2)]
            sg = [A.alloc([128, 512], F32) for _ in range(2)]
            ast = [A.alloc([128, 512], BF16) for _ in range(3)]
            load_w(wC, wb_out[l], (("wb", id(wb_out), l), "wC"))
            load_w(wA, wb_gate[l], (("wb", id(wb_gate), l), "wA"))
            load_w(wB, wb_up[l], (("wb", id(wb_up), l), "wB"))
            load_gb(norm_ffn_g[l])

            def load_m(ti):
                t0, n = tiles[ti]
                dma("sp", mts[ti % 2][:, :, 0:n], mixT[:, t0:t0 + n].rearrange("(k p) t -> p k t", p=128), (),
                    ["mt%d" % (ti % 2)])

            load_m(0)
            for ti, (t0, n) in enumerate(tiles):
                nsub = n // 128
                if l == 0:
                    src = xp[t0:t0 + n, :] if t0 < T else xs
                else:
                    src = xres[t0:t0 + n, :]
                dma("sp", xt[:, 0:nsub, :], src.rearrange("(s p) d -> p s d", p=128), ["xres"], ["xt"])
                if ti + 1 < len(tiles):
                    load_m(ti + 1)
                mt = mts[ti % 2]
                mk_ = "mt%d" % (ti % 2)
                for s in range(nsub):
                    for hf in range(2):
                        b = (2 * s + hf) % 4
                        for k in range(8):
                            mm(psb[b], mt[:, k, s * 128:(s + 1) * 128], wC[:, k, hf * 512:(hf + 1) * 512], k == 0, k == 7,
                               [mk_, "wC"], ["ps%d" % b])
                        tt("dve", xt[:, s, hf * 512:(hf + 1) * 512], xt[:, s, hf * 512:(hf + 1) * 512], psb[b], ALU.add,
                           ["xt", "ps%d" % b], ["xt"])
                dma("pool", xres[t0:t0 + n, :].rearrange("(s p) d -> p s d", p=128), xt[:, 0:nsub, :], ["xt"], ["xres"])
                rmsnorm_T(xt, nsub, xnT, W)
                for fc in range(NFC):
                    bg = (2 * fc) % 4
                    bu = bg + 1
                    for k in range(8):
                        mm(psb[bg][:, 0:n], wA[:, k, fc * 128:(fc + 1) * 128], xnT[:, k, 0:n], k == 0, k == 7,
                           ["wA", "xnT"], ["ps%d" % bg])
                    for k in range(8):
                        mm(psb[bu][:, 0:n], wB[:, k, fc * 128:(fc + 1) * 128], xnT[:, k, 0:n], k == 0, k == 7,
                           ["wB", "xnT"], ["ps%d" % bu])
                    sgt = sg[fc % 2]
                    sgk = "sg%d" % (fc % 2)
                    act(sgt[:, 0:n], psb[bg][:, 0:n], AF.Silu, ["ps%d" % bg], [sgk])
                    a_ = ast[fc % 3]
                    ak = "ast%d" % (fc % 3)
                    tt("dve", a_[:, 0:n], sgt[:, 0:n], psb[bu][:, 0:n], ALU.mult, [sgk, "ps%d" % bu], [ak])
                    dma("pool" if fc % 2 else "sp", aT[fc * 128:(fc + 1) * 128, t0:t0 + n], a_[:, 0:n], [ak], [])
            P.barrier()

        def phase4(l):
            A.reset()
            wA = A.alloc([128, NFC, D], BF16)
            xts = [A.alloc([128, 4, 1024], F32) for _ in range(2)]
            ats = [A.alloc([128, NFC, 512], BF16) for _ in range(2)]
            W = dict(junk=A.alloc([128, 1024], BF16), ssq=A.alloc([128, 4], F32), rstd=A.alloc([128, 4], F32))
            gfb = A.alloc([128, 1024], F32)
            load_w(wA, wb_down[l], (("wb", id(wb_down), l), "wA"))
            last = (l == L - 1)
            if last:
                dma("sp", gfb, norm_final_g.partition_broadcast(128), (), ["gfb"])
            def load_xa(ti):
                t0, n = tiles[ti]
                dma("sp", xts[ti % 2][:, 0:n // 128, :], xres[t0:t0 + n, :].rearrange("(s p) d -> p s d", p=128), (),
                    ["xt%d" % (ti % 2)])
                dma("sp", ats[ti % 2][:, :, 0:n], aT[:, t0:t0 + n].rearrange("(k p) t -> p k t", p=128), (),
                    ["at%d" % (ti % 2)])

            load_xa(0)
            for ti, (t0, n) in enumerate(tiles):
                nsub = n // 128
                if ti + 1 < len(tiles):
                    load_xa(ti + 1)
                xt = xts[ti % 2]
                at = ats[ti % 2]
                xk = "xt%d" % (ti % 2)
                ak_ = "at%d" % (ti % 2)
                for s in range(nsub):
                    for hf in range(2):
                        b = (2 * s + hf) % 4
                        for k in range(NFC):
                            mm(psb[b], at[:, k, s * 128:(s + 1) * 128], wA[:, k, hf * 512:(hf + 1) * 512], k == 0,
                               k == NFC - 1, [ak_, "wA"], ["ps%d" % b])
                        tt("dve", xt[:, s, hf * 512:(hf + 1) * 512], xt[:, s, hf * 512:(hf + 1) * 512], psb[b], ALU.add,
                           [xk, "ps%d" % b], [xk])
                if not last:
                    dma("pool", xres[t0:t0 + n, :].rearrange("(s p) d -> p s d", p=128), xt[:, 0:nsub, :], [xk], [])
                else:
                    for s in range(nsub):
                        act(W["junk"], xt[:, s, :], AF.Square, [xk], ["junk", "ssq"], accum_out=W["ssq"][:, s:s + 1])
                    ts("dve", W["rstd"][:, 0:nsub], W["ssq"][:, 0:nsub], 1.0 / D, EPS, ALU.mult, ALU.add, ["ssq"], ["rstd"])
                    act(W["rstd"][:, 0:nsub], W["rstd"][:, 0:nsub], AF.Ln, ["rstd"], ["rstd"])
                    act(W["rstd"][:, 0:nsub], W["rstd"][:, 0:nsub], AF.Exp, ["rstd"], ["rstd"], scale=-0.5)
                    for s in range(nsub):
                        stt(xt[:, s, :], xt[:, s, :], W["rstd"][:, s:s + 1], gfb, ALU.mult, ALU.mult,
                            [xk, "rstd", "gfb"], [xk])
                    dst = o_yp[t0:t0 + n, :] if t0 < T else o_ys
                    dma("pool", dst.rearrange("(s p) d -> p s d", p=128), xt[:, 0:nsub, :], [xk], [])
            P.barrier()

        if dbg_only_sample:
            phase2_attn_sample(0)
        for l in range(L if not dbg_only_sample else 0):
            phase1(l)
            phase2_frows(l)
            phase2_rec(l, False)
            phase2_rec(l, True)
            phase2_attn_prompt(l)
            if sample_attn:
                phase2_attn_sample(l)
            phase3(l)
            phase4(l)

        P.replay(nc, st)
    return nc


_IN_NAMES = ["w_in", "w_out", "w_gate", "w_up", "w_down", "norm_mix_g", "norm_ffn_g", "norm_final_g", "lru_conv_w",
             "lru_conv_b", "lru_w_a", "lru_b_a", "lru_w_x", "lru_b_x", "lru_lambda", "sc_conv_w", "fox_b_f",
             "diff_norm_g", "rel_bias"]


def make_in_maps(inp, cores, sample_attn=True):
    f = lambda a: np.ascontiguousarray(np.asarray(a, dtype=np.float32))
    shared = {k: f(inp[k]) for k in _IN_NAMES}
    shared["diff_lambda"] = f(inp["diff_lambda"]).reshape(L, 128)
    shared["c_t5"] = t5_onehot()
    if sample_attn:
        shared["c_fk"] = f(inp["cache_fox_k"]).reshape(L * NPOOL * 128, 256)
        shared["c_fv"] = f(inp["cache_fox_v"]).reshape(L * NPOOL * 128, 256)
        shared["c_flf"] = f(inp["cache_fox_logf"]).reshape(L * NPOOL * 128, 4)
        shared["c_dk"] = f(inp["cache_diff_k"]).reshape(L * NPOOL * 128, 256)
        shared["c_dv"] = f(inp["cache_diff_v"]).reshape(L * NPOOL * 128, 256)
    maps = []
    for c in cores:
        m = dict(shared)
        sl = slice(c * NS, (c + 1) * NS)
        m["xp"] = f(inp["x_prompt"][c % 4])
        m["xs"] = f(inp["x_sample"][sl]).reshape(NTS, D)
        m["st_lru_h"] = f(np.asarray(inp["state_lru_h"])[:, sl])
        m["st_lru_conv"] = f(np.asarray(inp["state_lru_conv"])[:, sl])
        m["st_sconv"] = f(np.asarray(inp["state_sconv"])[:, sl])
        if sample_attn:
            m["ptab"] = np.ascontiguousarray(np.asarray(inp["page_table"], dtype=np.int32)[sl]).reshape(NS * NPG)
        maps.append(m)
    return maps


def assemble(results, cores):
    B = 4
    G = 128
    out = {}
    yp = np.zeros((B, T, D), np.float32)
    ys = np.zeros((G, TS, D), np.float32)
    pk = {n: np.zeros((L, B, T, 256 if n != "flf" else 4), np.float32) for n in ("fk", "fv", "flf", "dk", "dv")}
    plh = np.zeros((L, B, 256), np.float32)
    plc = np.zeros((L, B, 3, 256), np.float32)
    psc = np.zeros((L, B, 2, 256), np.float32)
    sk = {n: np.zeros((L, G, TS, 256 if n != "flf" else 4), np.float32) for n in ("fk", "fv", "flf", "dk", "dv")}
    slh = np.zeros((L, G, 256), np.float32)
    slc = np.zeros((L, G, 3, 256), np.float32)
    ssc = np.zeros((L, G, 2, 256), np.float32)
    for r, c in zip(results, cores):
        sl = slice(c * NS, (c + 1) * NS)
        if c < 4:
            yp[c] = r["o_yp"]
            for n in pk:
                pk[n][:, c] = r["o_p" + n]
            plh[:, c] = r["o_plh"]
            plc[:, c] = r["o_plc"]
            psc[:, c] = r["o_psc"]
        ys[sl] = r["o_ys"].reshape(NS, TS, D)
        for n in sk:
            sk[n][:, sl] = r["o_s" + n].reshape(L, NS, TS, -1)
        slh[:, sl] = r["o_slh"]
        slc[:, sl] = r["o_slc"]
        ssc[:, sl] = r["o_ssc"]
    return (yp, ys,
            pk["fk"].reshape(L, B, T, 4, 64), pk["fv"].reshape(L, B, T, 4, 64), pk["flf"],
            pk["dk"].reshape(L, B, T, 4, 64), pk["dv"].reshape(L, B, T, 4, 64), plh, plc, psc,
            sk["fk"].reshape(L, G, TS, 4, 64), sk["fv"].reshape(L, G, TS, 4, 64), sk["flf"],
            sk["dk"].reshape(L, G, TS, 4, 64), sk["dv"].reshape(L, G, TS, 4, 64), slh, slc, ssc)


def kernel(**inputs):
    cores = list(range(N_CORES))
    nc = build_nc(sample_attn=True)
    in_maps = make_in_maps(inputs, cores, sample_attn=True)
    res = run_bass_kernel_spmd(nc, in_maps, core_ids=cores)
    return assemble(res.results, cores)
```

```python
import math
from contextlib import ExitStack

import numpy as np
import concourse.bass as bass
import concourse.mybir as mybir
from concourse.bass_utils import run_bass_kernel_spmd

F32 = mybir.dt.float32
BF16 = mybir.dt.bfloat16
I32 = mybir.dt.int32
U32 = mybir.dt.uint32
AF = mybir.ActivationFunctionType
ALU = mybir.AluOpType

ENGS = ("pe", "act", "dve", "pool", "sp")
N_DMA_SEMS = 10

L = 2
D = 1024
T = 4096
NS = 16
TS = 8
NTS = NS * TS
NT = T + NTS
NIN = 2820
DFF = 2816
NFC = DFF // 128
PAST = 2048
NPG = 16
NPOOL = 2560
EPS = 1e-6
N_CORES = 8


class Prog:
    def __init__(self):
        self.ops = {e: [] for e in ENGS}
        self.cnt = {}
        self.last_w = {}
        self.readers = {}
        self.waited = {e: {} for e in ENGS}
        self.rr = {e: 0 for e in ENGS}
        self.pending = {e: [] for e in ENGS}
        self.part_w = {}
        self.batch_war = {}

    def barrier(self):
        self.part_w = {}
        self.batch_war = {}
        cur = [(sk, v) for sk, v in self.cnt.items() if v > 0]
        for e in ENGS:
            self.pending[e] = list(cur)
        self.last_w = {}
        self.readers = {}

    def emit(self, eng, fn, reads=(), writes=(), dma=False, pwrites=()):
        psr = [k for k in reads if isinstance(k, str) and k.startswith("ps") and k not in writes]
        if psr:
            writes = list(writes) + psr
        deps = list(self.pending[eng])
        self.pending[eng] = []
        for k in reads:
            t = self.last_w.get(k)
            if t is not None:
                deps.append(t)
            deps.extend(self.part_w.get(k, ()))
        for k in writes:
            t = self.last_w.get(k)
            if t is not None:
                deps.append(t)
            deps.extend(self.readers.get(k, ()))
            deps.extend(self.part_w.get(k, ()))
        for k in pwrites:
            if self.readers.get(k) or k not in self.part_w:
                war = list(self.readers.get(k, ()))
                if k in self.last_w:
                    war.append(self.last_w[k])
                self.batch_war[k] = war
                self.part_w[k] = []
                self.readers[k] = []
            deps.extend(self.batch_war[k])
        if dma:
            slot = self.rr[eng]
            self.rr[eng] = (slot + 1) % N_DMA_SEMS
            sk = ("dma", eng, slot)
            inc = 16
            if self.cnt.get(sk, 0) > 0:
                deps.append((sk, self.cnt[sk]))
        else:
            sk = ("eng", eng)
            inc = 1
        val = self.cnt.get(sk, 0) + inc
        self.cnt[sk] = val
        tok = (sk, val)
        waits = []
        wd = self.waited[eng]
        need = {}
        for (dk, dv) in deps:
            if eng == "pe" and dk == ("eng", "pe"):
                continue
            if dv > need.get(dk, 0):
                need[dk] = dv
        for dk, dv in need.items():
            if wd.get(dk, 0) < dv:
                wd[dk] = dv
                waits.append((dk, dv))
        self.ops[eng].append((fn, waits, sk, inc))
        for k in writes:
            self.last_w[k] = tok
            self.readers[k] = []
            self.part_w.pop(k, None)
            self.batch_war.pop(k, None)
        for k in pwrites:
            self.part_w[k].append(tok)
        for k in reads:
            self.readers.setdefault(k, []).append(tok)
        return tok

    def replay(self, nc, stack):
        sems = {}
        for sk in self.cnt:
            sems[sk] = stack.enter_context(nc.semaphore("s_" + "_".join(str(x) for x in sk)))
        block = stack.enter_context(nc.Block())
        finals = [(sk, v) for sk, v in self.cnt.items()]
        sig = {}
        for en in ENGS:
            for (fn, waits, sk, inc) in self.ops[en]:
                for (wk, wv) in waits:
                    if wk[0] == "eng":
                        sig.setdefault(wk, set()).add(wv)
        for (fk, fv) in finals:
            if fk[0] == "eng":
                sig.setdefault(fk, set()).add(fv)
        rank = {k: {v: i + 1 for i, v in enumerate(sorted(vs))} for k, vs in sig.items()}

        def wval(wk, wv):
            return rank[wk][wv] if wk[0] == "eng" else wv

        def mk(eng_name):
            def body(e):
                idx = {}
                for (fn, waits, sk, inc) in self.ops[eng_name]:
                    for (wk, wv) in waits:
                        e.wait_ge(sems[wk], wval(wk, wv))
                    ins = fn(e)
                    if sk[0] == "eng":
                        idx[sk] = idx.get(sk, 0) + 1
                        if idx[sk] in sig.get(sk, ()):
                            ins.then_inc(sems[sk], 1)
                    else:
                        ins.then_inc(sems[sk], inc)
                if eng_name == "sp":
                    for (fk, fv) in finals:
                        e.wait_ge(sems[fk], wval(fk, fv))
            return body

        block.tensor(mk("pe"))
        block.scalar(mk("act"))
        block.vector(mk("dve"))
        block.gpsimd(mk("pool"))
        block.sync(mk("sp"))


class Arena:
    def __init__(self, ap, words):
        self.ap = ap
        self.words = words
        self.off = 0

    def reset(self):
        self.off = 0

    def alloc(self, shape, dtype, parts=128):
        n = 1
        for s in shape[1:]:
            n *= s
        w = n if dtype in (F32, I32, U32) else (n + 1) // 2
        w = (w + 7) // 8 * 8
        assert self.off + w <= self.words, ("arena overflow", self.off, w, self.words)
        v = self.ap[0:shape[0], self.off:self.off + w]
        self.off += w
        if dtype != F32:
            v = v.bitcast(dtype)
        v = v[:, 0:n]
        if len(shape) == 3:
            v = v.rearrange("p (a b) -> p a b", b=shape[2])
        elif len(shape) == 4:
            v = v.rearrange("p (a b c) -> p a b c", b=shape[2], c=shape[3])
        return v


def t5_onehot():
    oh = np.zeros((32, 384), np.float32)
    for j in range(383):
        rel = j - 127
        if rel < 0:
            continue
        if rel < 16:
            b = rel
        else:
            b = 16 + int(np.float32(np.log(np.float32(rel) / np.float32(16.0)) / np.float32(math.log(8.0))
                                    * np.float32(16.0)))
            b = min(b, 31)
        oh[b, j] += 1.0
        oh[31, j] -= 1.0
    return oh


def build_nc(sample_attn=True, dbg_only_sample=0):
    nc = bass.Bass("TRN2", target_bir_lowering=False)
    P = Prog()

    def din(name, shape, dt=F32):
        return nc.dram_tensor(name, list(shape), dt, kind="ExternalInput").ap()

    def dout(name, shape, dt=F32):
        return nc.dram_tensor(name, list(shape), dt, kind="ExternalOutput").ap()

    def dscr(name, shape, dt):
        return nc.dram_tensor(name, list(shape), dt, kind="Internal").ap()

    xp = din("xp", [T, D])
    xs = din("xs", [NTS, D])
    w_in = din("w_in", [L, D, NIN])
    w_out = din("w_out", [L, D, D])
    w_gate = din("w_gate", [L, D, DFF])
    w_up = din("w_up", [L, D, DFF])
    w_down = din("w_down", [L, DFF, D])
    norm_mix_g = din("norm_mix_g", [L, D])
    norm_ffn_g = din("norm_ffn_g", [L, D])
    norm_final_g = din("norm_final_g", [D])
    lru_conv_w = din("lru_conv_w", [L, 4, 256])
    lru_conv_b = din("lru_conv_b", [L, 256])
    lru_w_a = din("lru_w_a", [L, 4, 64, 64])
    lru_b_a = din("lru_b_a", [L, 256])
    lru_w_x = din("lru_w_x", [L, 4, 64, 64])
    lru_b_x = din("lru_b_x", [L, 256])
    lru_lambda = din("lru_lambda", [L, 256])
    sc_conv_w = din("sc_conv_w", [L, 3, 256])
    fox_b_f = din("fox_b_f", [L, 4])
    diff_lambda = din("diff_lambda", [L, 128])
    diff_norm_g = din("diff_norm_g", [L, 64])
    rel_bias = din("rel_bias", [32, 4])
    c_t5 = din("c_t5", [32, 384])
    st_lru_h = din("st_lru_h", [L, NS, 256])
    st_lru_conv = din("st_lru_conv", [L, NS, 3, 256])
    st_sconv = din("st_sconv", [L, NS, 2, 256])
    if sample_attn:
        c_all = din("c_all", [L * NPOOL * 128, 1028])
        ptab = din("ptab", [NS * NPG], I32)

    o_yp = dout("o_yp", [T, D])
    o_ys = dout("o_ys", [NTS, D])
    o_pfk = dout("o_pfk", [L, T, 256])
    o_pfv = dout("o_pfv", [L, T, 256])
    o_pflf = dout("o_pflf", [L, T, 4])
    o_pdk = dout("o_pdk", [L, T, 256])
    o_pdv = dout("o_pdv", [L, T, 256])
    o_plh = dout("o_plh", [L, 256])
    o_plc = dout("o_plc", [L, 3, 256])
    o_psc = dout("o_psc", [L, 2, 256])
    o_sfk = dout("o_sfk", [L, NTS, 256])
    o_sfv = dout("o_sfv", [L, NTS, 256])
    o_sflf = dout("o_sflf", [L, NTS, 4])
    o_sdk = dout("o_sdk", [L, NTS, 256])
    o_sdv = dout("o_sdv", [L, NTS, 256])
    o_slh = dout("o_slh", [L, NS, 256])
    o_slc = dout("o_slc", [L, NS, 3, 256])
    o_ssc = dout("o_ssc", [L, NS, 2, 256])

    wb_in = dscr("wb_in", [L, D, NIN], BF16)
    wb_out = dscr("wb_out", [L, D, D], BF16)
    wb_gate = dscr("wb_gate", [L, D, DFF], BF16)
    wb_up = dscr("wb_up", [L, D, DFF], BF16)
    wb_down = dscr("wb_down", [L, DFF, D], BF16)
    xres = dscr("xres", [NT, D], F32)
    zfm = dscr("zfm", [1280, NT], BF16)
    fqa = dscr("fqa", [4, 67, NT], BF16)
    fka = dscr("fka", [4, 67, NT], BF16)
    dqs = dscr("dqs", [4, 64, NT], BF16)
    dks = dscr("dks", [4, 64, NT], BF16)
    flT = dscr("flT", [4, NT], F32)
    vsf = dscr("vsf", [NT, 256], BF16)
    vsd = dscr("vsd", [NT, 256], BF16)
    lfs = dscr("lfs", [NT, 4], F32)
    mixT = dscr("mixT", [D, NT], BF16)
    aT = dscr("aT", [DFF, NT], BF16)
    t5s = dscr("t5s", [4, 384], BF16)
    t5b = dscr("t5b", [4, 128, 384], BF16)

    tiles = [(i * 512, 512) for i in range(T // 512)] + [(T, NTS)]

    with ExitStack() as st:
        st.enter_context(nc.allow_non_contiguous_dma(reason="layout transforms"))
        st.enter_context(nc.allow_low_precision(reason="bf16 matmul operands per problem tolerance"))
        AW = 42000
        arena_t = st.enter_context(nc.sbuf_tensor("arena", [128, AW], F32))
        A = Arena(arena_t, AW)
        cst_t = st.enter_context(nc.sbuf_tensor("cst", [128, 2600], F32))
        C = Arena(cst_t, 2600)
        ident = C.alloc([128, 128], BF16)
        tri = C.alloc([128, 256], BF16)
        ones_b = C.alloc([128, 128], BF16)
        ones_f = C.alloc([128, 64], F32)
        gb = C.alloc([128, 8, 128], BF16)
        gcol = C.alloc([128, 8], F32)
        small = C.alloc([128, 256], F32)
        eb = C.alloc([128, 4, 256], BF16)
        psb = [st.enter_context(nc.psum_tensor("ps%d" % i, [128, 512], F32))[:, :] for i in range(7)]
        pst = st.enter_context(nc.psum_tensor("pst", [128, 1024], BF16))[:, :]

        def dma(q, out, in_, reads=(), writes=(), pwrites=()):
            return P.emit(q, lambda e: e.dma_start(out=out, in_=in_), reads, writes, dma=True, pwrites=pwrites)

        def mm(out, lhsT, rhs, start, stop, reads, writes):
            return P.emit("pe", lambda e: e.matmul(out, lhsT=lhsT, rhs=rhs, start=start, stop=stop), reads, writes)

        def act(out, in_, func, reads, writes, bias=None, scale=None, accum_out=None):
            kw = {}
            if bias is not None:
                kw["bias"] = bias
            if scale is not None:
                kw["scale"] = scale
            if accum_out is not None:
                kw["accum_out"] = accum_out
            return P.emit("act", lambda e: e.activation(out=out, in_=in_, func=func, **kw), reads, writes)

        def ts(eng, out, in0, s1, s2, op0, op1, reads, writes):
            if s2 is None:
                return P.emit(eng, lambda e: e.tensor_scalar(out=out, in0=in0, scalar1=s1, scalar2=None, op0=op0),
                              reads, writes)
            return P.emit(eng, lambda e: e.tensor_scalar(out=out, in0=in0, scalar1=s1, scalar2=s2, op0=op0, op1=op1),
                          reads, writes)

        def tt(eng, out, in0, in1, op, reads, writes):
            return P.emit(eng, lambda e: e.tensor_tensor(out=out, in0=in0, in1=in1, op=op), reads, writes)

        def stt(out, in0, scalar, in1, op0, op1, reads, writes):
            return P.emit("dve", lambda e: e.scalar_tensor_tensor(out=out, in0=in0, scalar=scalar, in1=in1,
                                                                   op0=op0, op1=op1), reads, writes)

        def cp(eng, out, in_, reads, writes):
            if eng == "act":
                return P.emit("act", lambda e: e.copy(out=out, in_=in_), reads, writes)
            return P.emit(eng, lambda e: e.tensor_copy(out=out, in_=in_), reads, writes)

        def memset(eng, ap, val, writes):
            return P.emit(eng, lambda e: e.memset(ap, val), (), writes)

        memset("pool", ones_b, 1.0, ["ones_b"])
        memset("pool", ones_f, 1.0, ["ones_f"])
        memset("pool", ident, 1.0, ["ident"])
        P.emit("pool", lambda e: e.affine_select(out=ident, in_=ident, pattern=[[-1, 128]], compare_op=ALU.is_equal,
                                                 fill=0.0, base=0, channel_multiplier=1), ["ident"], ["ident"])
        memset("pool", tri, 1.0, ["tri"])
        P.emit("pool", lambda e: e.affine_select(out=tri[:, 0:128], in_=tri[:, 0:128], pattern=[[1, 128]],
                                                 compare_op=ALU.is_ge, fill=0.0, base=0, channel_multiplier=-1),
               ["tri"], ["tri"])
        for l in range(L if not dbg_only_sample else 0):
            for (src, dst, rows) in ((w_in, wb_in, D), (w_out, wb_out, D), (w_gate, wb_gate, D), (w_up, wb_up, D),
                                     (w_down, wb_down, DFF)):
                nsp = 4
                rr_ = rows // nsp
                for i in range(nsp):
                    dma("pool", dst[l, i * rr_:(i + 1) * rr_, :], src[l, i * rr_:(i + 1) * rr_, :],
                        writes=[("wb", id(dst), l)])
        cr = A.alloc([4, NT], BF16)
        memset("dve", cr, 1.0, ["cr"])
        dma("sp", fka[:, 64, :], cr, ["cr"], [])
        cr2 = A.alloc([4, NT], BF16)
        memset("dve", cr2, -8.0, ["cr2"])
        dma("sp", fqa[:, 65, :], cr2, ["cr2"], [])
        dma("sp", fqa[:, 66, :], cr2, ["cr2"], [])
        rb = A.alloc([32, 4], F32)
        oh = A.alloc([32, 384], F32)
        dma("sp", rb, rel_bias, (), ["rb"])
        dma("sp", oh, c_t5, (), ["oh"])
        mm(psb[0][0:4, 0:384], rb, oh, True, True, ["rb", "oh"], ["ps0"])
        gex = A.alloc([4, 384], BF16)
        act(gex, psb[0][0:4, 0:384], AF.Exp, ["ps0"], ["gex"])
        memset("dve", gex[:, 0:127], 0.0, ["gex"])
        dma("sp", t5s, gex, ["gex"], ["t5s"])
        for h in range(4):
            bsrc = bass.AP(tensor=t5s.tensor, offset=t5s.offset + h * 384, ap=[[0, 128], [1, 384]])
            dma("sp", t5b[h], bsrc, ["t5s"], ["t5b"])
            src = bass.AP(tensor=t5b.tensor, offset=t5b.offset + h * 128 * 384 + 127, ap=[[383, 128], [1, 256]])
            dma("sp", eb[:, h, :], src, ["t5b"], ["eb"])
        P.barrier()

        def load_gb(gsrc):
            dma("sp", gcol, gsrc.rearrange("(c p) -> p c", p=128), (), ["gcol"])
            for c in range(8):
                ts("dve", gb[:, c, :], ones_b, gcol[:, c:c + 1], None, ALU.mult, None, ["gcol", "ones_b"], ["gb"])

        def rmsnorm_T(xt, nsub, xnT, W, xk="xt"):
            for s in range(nsub):
                act(W["xsb"][:, s, :], xt[:, s, :], AF.Square, [xk], ["xsb", "ssq"], accum_out=W["ssq"][:, s:s + 1])
            ts("dve", W["rstd"][:, 0:nsub], W["ssq"][:, 0:nsub], 1.0 / D, EPS, ALU.mult, ALU.add, ["ssq"], ["rstd"])
            act(W["rstd"][:, 0:nsub], W["rstd"][:, 0:nsub], AF.Ln, ["rstd"], ["rstd"])
            act(W["rstd"][:, 0:nsub], W["rstd"][:, 0:nsub], AF.Exp, ["rstd"], ["rstd"], scale=-0.5)
            for s in range(nsub):
                ts("dve", W["xsb"][:, s, :], xt[:, s, :], W["rstd"][:, s:s + 1], None,
                   ALU.mult, None, [xk, "rstd"], ["xsb"])
            for s in range(nsub):
                for c in range(8):
                    P.emit("pe", lambda e, s=s, c=c: e.transpose(out=pst[:, c * 128:(c + 1) * 128],
                                                                  in_=W["xsb"][:, s, c * 128:(c + 1) * 128],
                                                                  identity=ident), ["xsb", "ident"], ["pst"])
                tt("dve", xnT[:, :, s * 128:(s + 1) * 128], pst.rearrange("p (c t) -> p c t", t=128), gb, ALU.mult,
                   ["pst", "gb"], ["xnT"])

        def load_w(dst, src, key):
            K = src.shape[0] // 128
            for k in range(K):
                dma("sp", dst[:, k, :], src[k * 128:(k + 1) * 128, :], [key[0]], (), pwrites=[key[1]])

        evac_rr = [0]

        def evac(out, in_, reads, writes):
            evac_rr[0] ^= 1
            return cp("act" if evac_rr[0] else "dve", out, in_, reads, writes)

        def phase1(l):
            A.reset()
            wA = A.alloc([128, 8, NIN], BF16)
            W = dict(ssq=A.alloc([128, 4], F32), rstd=A.alloc([128, 4], F32),
                     xsb=A.alloc([128, 4, 1024], BF16))
            xts = [A.alloc([128, 4, 1024], F32) for _ in range(2)]
            xnT = A.alloc([128, 8, 512], BF16)
            zst = [A.alloc([128, 512], BF16) for _ in range(4)]
            zsf = A.alloc([4, 512], F32)
            ztm = A.alloc([128, 4, 1028], F32)
            vst = A.alloc([128, 4, 512], BF16)
            bfb = A.alloc([128, 4], F32)
            lt = A.alloc([128, 4, 4], F32)
            load_w(wA, wb_in[l], (("wb", id(wb_in), l), "wA"))
            load_gb(norm_mix_g[l])
            dma("sp", bfb, fox_b_f[l].partition_broadcast(128), (), ["bfb"])
            groups = [(c * 128, 128, zfm[c * 128:(c + 1) * 128, :]) for c in range(10)]
            for h in range(4):
                groups.append((1280 + 64 * h, 64, fqa[h, 0:64, :]))
                groups.append((1536 + 64 * h, 64, fka[h, 0:64, :]))
                groups.append((2052 + 64 * h, 64, dqs[h, 0:64, :]))
                groups.append((2308 + 64 * h, 64, dks[h, 0:64, :]))
            def load_x(ti):
                t0, n = tiles[ti]
                if l == 0:
                    src = xp[t0:t0 + n, :] if t0 < T else xs
                else:
                    src = xres[t0:t0 + n, :]
                dma("sp", xts[ti % 2][:, 0:n // 128, :], src.rearrange("(s p) d -> p s d", p=128), (), ["xt%d" % (ti % 2)])

            load_x(0)
            for ti, (t0, n) in enumerate(tiles):
                nsub = n // 128
                if ti + 1 < len(tiles):
                    load_x(ti + 1)
                xt = xts[ti % 2]
                rmsnorm_T(xt, nsub, xnT, W, "xt%d" % (ti % 2))
                for gi, (c0, M, dst) in enumerate(groups):
                    ps = psb[gi % 4]
                    pk = "ps%d" % (gi % 4)
                    for k in range(8):
                        mm(ps[0:M, 0:n], wA[:, k, c0:c0 + M], xnT[:, k, 0:n], k == 0, k == 7, ["wA", "xnT"], [pk])
                    zk = "zst%d" % (gi % 4)
                    evac(zst[gi % 4][0:M, 0:n], ps[0:M, 0:n], [pk], [zk])
                    dma("pool" if gi % 2 else "sp", dst[:, t0:t0 + n], zst[gi % 4][0:M, 0:n], [zk], [])
                for k in range(8):
                    mm(psb[0][0:4, 0:n], wA[:, k, 2048:2052], xnT[:, k, 0:n], k == 0, k == 7, ["wA", "xnT"], ["ps0"])
                cp("dve", zsf[:, 0:n], psb[0][0:4, 0:n], ["ps0"], ["zsf"])
                dma("sp", flT[:, t0:t0 + n], zsf[:, 0:n], ["zsf"], [])
                for s in range(nsub):
                    for (pi, c0, ncol) in ((4, 1536, 512), (5, 2308, 512), (6, 2048, 4)):
                        for k in range(8):
                            mm(psb[pi][:, 0:ncol], xnT[:, k, s * 128:(s + 1) * 128], wA[:, k, c0:c0 + ncol],
                               k == 0, k == 7, ["wA", "xnT"], ["ps%d" % pi])
                    cp("act", ztm[:, s, 0:512], psb[4], ["ps4"], ["ztm"])
                    cp("dve", ztm[:, s, 512:1024], psb[5], ["ps5"], ["ztm"])
                    tt("dve", lt[:, s, :], psb[6][:, 0:4], bfb, ALU.add, ["ps6", "bfb"], ["lt"])
                act(lt[:, 0:nsub, :], lt[:, 0:nsub, :], AF.Exp, ["lt"], ["lt"], scale=-1.0)
                act(lt[:, 0:nsub, :], lt[:, 0:nsub, :], AF.Ln, ["lt"], ["lt"], bias=1.0)
                ts("dve", ztm[:, 0:nsub, 1024:1028], lt[:, 0:nsub, :], -1.0, None, ALU.mult, None, ["lt"], ["ztm"])
                cp("dve", vst[:, 0:nsub, 0:256], ztm[:, 0:nsub, 256:512], ["ztm"], ["vst"])
                cp("act", vst[:, 0:nsub, 256:512], ztm[:, 0:nsub, 768:1024], ["ztm"], ["vst"])
                if t0 < T:
                    outs = (o_pfk, o_pfv, o_pdk, o_pdv, o_pflf)
                    r0 = t0
                else:
                    outs = (o_sfk, o_sfv, o_sdk, o_sdv, o_sflf)
                    r0 = 0
                for oi, o in enumerate(outs):
                    c0, cn = (oi * 256, 256) if oi < 4 else (1024, 4)
                    dma("pool", o[l, r0:r0 + n, :].rearrange("(s p) c -> p s c", p=128), ztm[:, 0:nsub, c0:c0 + cn],
                        ["ztm"], [])
                dma("sp", lfs[t0:t0 + n, :].rearrange("(s p) c -> p s c", p=128), ztm[:, 0:nsub, 1024:1028],
                    ["ztm"], [])
                dma("sp", vsf[t0:t0 + n, :].rearrange("(s p) c -> p s c", p=128), vst[:, 0:nsub, 0:256],
                    ["vst"], [])
                dma("sp", vsd[t0:t0 + n, :].rearrange("(s p) c -> p s c", p=128), vst[:, 0:nsub, 256:512],
                    ["vst"], [])
            P.barrier()

        def phase2_frows(l):
            A.reset()
            fl = A.alloc([4, T], F32)
            Fc = A.alloc([4, T], F32)
            fhi = A.alloc([4, T], BF16)
            flo = A.alloc([4, T], BF16)
            f8 = A.alloc([4, T], BF16)
            bfc = A.alloc([4, 1], F32)
            onesT = A.alloc([4, T], F32)
            dma("sp", fl, flT[:, 0:T], (), ["fl"])
            dma("sp", bfc, fox_b_f[l].rearrange("(h o) -> h o", o=1), (), ["bfc"])
            memset("pool", onesT, 1.0, ["onesT"])
            ts("dve", fl, fl, bfc[:, 0:1], -1.0, ALU.add, ALU.mult, ["fl", "bfc"], ["fl"])
            act(fl, fl, AF.Exp, ["fl"], ["fl"])
            act(fl, fl, AF.Ln, ["fl"], ["fl"], bias=1.0)
            ts("dve", fl, fl, -1.0, None, ALU.mult, None, ["fl"], ["fl"])
            P.emit("dve", lambda e: e.tensor_tensor_scan(out=Fc, data0=onesT, data1=fl, initial=0.0, op0=ALU.mult,
                                                         op1=ALU.add), ["fl", "onesT"], ["Fc"])
            cp("dve", fhi, Fc, ["Fc"], ["fhi"])
            tt("dve", flo, Fc, fhi, ALU.subtract, ["Fc", "fhi"], ["flo"])
            ts("dve", f8, fhi, 8.0, None, ALU.mult, None, ["fhi"], ["f8"])
            dma("sp", fka[:, 65, 0:T], fhi, ["fhi"], [])
            dma("sp", fka[:, 66, 0:T], flo, ["flo"], [])
            dma("sp", fqa[:, 64, 0:T], f8, ["f8"], [])
            P.barrier()

        def phase2_rec(l, sample):
            A.reset()
            S, TT, nt = (NS, TS, 1) if sample else (1, 1024, T // 1024)
            tb = T if sample else 0
            NW = S * TT
            cw = A.alloc([128, 2, 4], F32)
            cbias = A.alloc([128, 2], F32)
            ba = A.alloc([128, 2], F32)
            bx = A.alloc([128, 2], F32)
            lam = A.alloc([128, 2], F32)
            cl = A.alloc([128, 2], F32)
            scw = A.alloc([128, 2, 3], F32)
            wa_f = A.alloc([128, 2, 128], F32)
            wx_f = A.alloc([128, 2, 128], F32)
            wa_b = A.alloc([128, 2, 128], BF16)
            wx_b = A.alloc([128, 2, 128], BF16)
            for c_ in range(2):
                dma("sp", cw[:, c_, :], lru_conv_w[l, :, c_ * 128:(c_ + 1) * 128].rearrange("k p -> p k"), (), ["cw"])
            dma("sp", cbias, lru_conv_b[l].rearrange("(c p) -> p c", p=128), (), ["cw"])
            dma("sp", ba, lru_b_a[l].rearrange("(c p) -> p c", p=128), (), ["cw"])
            dma("sp", bx, lru_b_x[l].rearrange("(c p) -> p c", p=128), (), ["cw"])
            dma("sp", lam, lru_lambda[l].rearrange("(c p) -> p c", p=128), (), ["lam"])
            for c_ in range(2):
                dma("sp", scw[:, c_, :], sc_conv_w[l, :, c_ * 128:(c_ + 1) * 128].rearrange("k p -> p k"), (), ["cw"])
            memset("dve", wa_f, 0.0, ["wa_f"])
            memset("dve", wx_f, 0.0, ["wx_f"])
            for c in range(2):
                for b in range(2):
                    dma("sp", wa_f[b * 64:(b + 1) * 64, c, b * 64:(b + 1) * 64], lru_w_a[l, 2 * c + b], (), ["wa_f"])
                    dma("sp", wx_f[b * 64:(b + 1) * 64, c, b * 64:(b + 1) * 64], lru_w_x[l, 2 * c + b], (), ["wx_f"])
            cp("dve", wa_b, wa_f, ["wa_f"], ["wa_b"])
            cp("dve", wx_b, wx_f, ["wx_f"], ["wx_b"])
            act(cl, lam, AF.Exp, ["lam"], ["cl"], scale=-1.0)
            act(cl, cl, AF.Ln, ["cl"], ["cl"], bias=1.0)
            ts("dve", cl, cl, -8.0, None, ALU.mult, None, ["cl"], ["cl"])

            zb = A.alloc([128, NW], BF16)
            xl = A.alloc([128, S, 3 + TT], F32)
            xc = A.alloc([128, S, TT], F32)
            xcb = A.alloc([128, NW], BF16)
            rg = A.alloc([128, NW], F32)
            ig = A.alloc([128, NW], F32)
            av = A.alloc([128, NW], F32)
            uv = A.alloc([128, NW], F32)
            hv = A.alloc([128, NW], F32)
            gt = A.alloc([128, NW], BF16)
            g2 = A.alloc([128, NW], F32)
            yb = A.alloc([128, NW], BF16)
            h0 = A.alloc([128, S], F32)
            hist = A.alloc([128, S, 3], F32)
            cx = A.alloc([128, S, 2 + TT], F32)
            z2 = A.alloc([128, NW], BF16)
            z3 = A.alloc([128, NW], BF16)
            v3 = lambda ap: ap.rearrange("p (s t) -> p s t", t=TT)
            for c in range(2):
                r_ = slice(c * 128, (c + 1) * 128)
                for j in range(nt):
                    t0 = tb + j * NW
                    dma("sp", zb, zfm[c * 128:(c + 1) * 128, t0:t0 + NW], (), ["zb"])
                    if j > 0:
                        cp("dve", hist, xl[:, :, TT:TT + 3], ["xl"], ["hist"])
                    cp("dve", xl[:, :, 3:3 + TT], v3(zb), ["zb"], ["xl"])
                    if j > 0:
                        cp("dve", xl[:, :, 0:3], hist, ["hist"], ["xl"])
                    elif sample:
                        for k_ in range(3):
                            dma("sp", xl[:, :, k_], st_lru_conv[l, :, k_, r_].rearrange("s p -> p s"), (), ["xl"])
                        dma("sp", h0, st_lru_h[l, :, r_].rearrange("s p -> p s"), (), ["h0"])
                    else:
                        memset("dve", xl[:, :, 0:3], 0.0, ["xl"])
                    ts("dve", xc, xl[:, :, 0:TT], cw[:, c, 0:1], cbias[:, c:c + 1], ALU.mult, ALU.add, ["xl", "cw"], ["xc"])
                    for k in range(1, 4):
                        stt(xc, xl[:, :, k:k + TT], cw[:, c, k:k + 1], xc, ALU.mult, ALU.add, ["xl", "cw", "xc"], ["xc"])
                    cp("act", v3(xcb), xc, ["xc"], ["xcb"])
                    for hf in range(0, NW, 512):
                        n = min(512, NW - hf)
                        mm(psb[0][:, 0:n], wa_b[:, c, :], xcb[:, hf:hf + n], True, True, ["wa_b", "xcb"], ["ps0"])
                        act(rg[:, hf:hf + n], psb[0][:, 0:n], AF.Sigmoid, ["ps0", "cw"], ["rg"], bias=ba[:, c:c + 1])
                        mm(psb[1][:, 0:n], wx_b[:, c, :], xcb[:, hf:hf + n], True, True, ["wx_b", "xcb"], ["ps1"])
                        act(ig[:, hf:hf + n], psb[1][:, 0:n], AF.Sigmoid, ["ps1", "cw"], ["ig"], bias=bx[:, c:c + 1])
                    act(av, rg, AF.Exp, ["rg", "cl"], ["av"], scale=cl[:, c:c + 1])
                    tt("dve", uv, av, av, ALU.mult, ["av"], ["uv"])
                    ts("dve", uv, uv, -1.0, 1.0, ALU.mult, ALU.add, ["uv"], ["uv"])
                    ts("dve", uv, uv, 0.0, None, ALU.max, None, ["uv"], ["uv"])
                    act(uv, uv, AF.Sqrt, ["uv"], ["uv"])
                    tt("dve", uv, uv, ig, ALU.mult, ["uv", "ig"], ["uv"])
                    tt("dve", v3(uv), v3(uv), xc, ALU.mult, ["uv", "xc"], ["uv"])
                    if j > 0:
                        cp("dve", h0[:, 0:1], hv[:, NW - 1:NW], ["hv"], ["h0"])
                    for s in range(S):
                        init = 0.0 if (not sample and j == 0) else h0[:, s:s + 1]
                        P.emit("dve", lambda e, s=s, init=init: e.tensor_tensor_scan(
                            out=hv[:, s * TT:(s + 1) * TT], data0=av[:, s * TT:(s + 1) * TT],
                            data1=uv[:, s * TT:(s + 1) * TT], initial=init, op0=ALU.mult, op1=ALU.add),
                            ["av", "uv", "h0"], ["hv"])
                    dma("sp", gt, zfm[256 + c * 128:256 + (c + 1) * 128, t0:t0 + NW], (), ["gt"])
                    tt("dve", g2, gt, gt, ALU.mult, ["gt"], ["g2"])
                    ts("dve", g2, g2, 0.044715 * 0.7978845608028654, 0.7978845608028654, ALU.mult, ALU.add, ["g2"], ["g2"])
                    tt("dve", g2, g2, gt, ALU.mult, ["g2", "gt"], ["g2"])
                    act(g2, g2, AF.Tanh, ["g2"], ["g2"])
                    stt(g2, g2, 1.0, gt, ALU.add, ALU.mult, ["g2", "gt"], ["g2"])
                    stt(yb, g2, 0.5, hv, ALU.mult, ALU.mult, ["g2", "hv"], ["yb"])
                    dma("pool", mixT[c * 128:(c + 1) * 128, t0:t0 + NW], yb, ["yb"], [])
                if sample:
                    dma("pool", o_slh[l, :, r_].rearrange("s p -> p s"), v3(hv)[:, :, TT - 1], ["hv"], [])
                    for k_ in range(3):
                        dma("pool", o_slc[l, :, k_, r_].rearrange("s p -> p s"), xl[:, :, TT + k_], ["xl"], [])
                else:
                    dma("pool", o_plh[l, r_].rearrange("(p o) -> p o", o=1), hv[:, NW - 1:NW], ["hv"], [])
                    dma("pool", o_plc[l, :, r_].rearrange("k p -> p k"), xl[:, 0, TT:TT + 3], ["xl"], [])
                for j in range(nt):
                    t0 = tb + j * NW
                    dma("sp", zb, zfm[512 + c * 128:512 + (c + 1) * 128, t0:t0 + NW], (), ["zb"])
                    dma("sp", z2, zfm[768 + c * 128:768 + (c + 1) * 128, t0:t0 + NW], (), ["z2"])
                    dma("sp", z3, zfm[1024 + c * 128:1024 + (c + 1) * 128, t0:t0 + NW], (), ["z3"])
                    if j > 0:
                        cp("dve", hist[:, :, 0:2], cx[:, :, TT:TT + 2], ["cx"], ["hist"])
                    tt("dve", cx[:, :, 2:2 + TT], v3(z2), v3(z3), ALU.mult, ["z2", "z3"], ["cx"])
                    if j > 0:
                        cp("dve", cx[:, :, 0:2], hist[:, :, 0:2], ["hist"], ["cx"])
                    elif sample:
                        for k_ in range(2):
                            dma("sp", cx[:, :, k_], st_sconv[l, :, k_, r_].rearrange("s p -> p s"), (), ["cx"])
                    else:
                        memset("dve", cx[:, :, 0:2], 0.0, ["cx"])
                    ts("dve", xc, cx[:, :, 0:TT], scw[:, c, 0:1], None, ALU.mult, None, ["cx", "cw"], ["xc"])
                    for k in range(1, 3):
                        stt(xc, cx[:, :, k:k + TT], scw[:, c, k:k + 1], xc, ALU.mult, ALU.add, ["cx", "cw", "xc"], ["xc"])
                    tt("dve", v3(yb), xc, v3(zb), ALU.mult, ["xc", "zb"], ["yb"])
                    dma("pool", mixT[256 + c * 128:256 + (c + 1) * 128, t0:t0 + NW], yb, ["yb"], [])
                if sample:
                    for k_ in range(2):
                        dma("pool", o_ssc[l, :, k_, r_].rearrange("s p -> p s"), cx[:, :, TT + k_], ["cx"], [])
                else:
                    dma("pool", o_psc[l, :, r_].rearrange("k p -> p k"), cx[:, 0, TT:TT + 2], ["cx"], [])
            P.barrier()

        def lam_col(l, dst):
            lp = A.alloc([128, 128], F32)
            pr = A.alloc([128, 64], F32)
            sm = A.alloc([128, 2], F32)
            dma("sp", lp, diff_lambda[l].partition_broadcast(128), (), ["lp"])
            tt("dve", pr[:, 0:32], lp[:, 0:32], lp[:, 32:64], ALU.mult, ["lp"], ["pr"])
            tt("dve", pr[:, 32:64], lp[:, 64:96], lp[:, 96:128], ALU.mult, ["lp"], ["pr"])
            P.emit("dve", lambda e: e.reduce_sum(out=sm, in_=pr.rearrange("p (a b) -> p a b", b=32),
                                                 axis=mybir.AxisListType.X), ["pr"], ["sm"])
            act(sm, sm, AF.Exp, ["sm"], ["sm"])
            tt("dve", dst, sm[:, 0:1], sm[:, 1:2], ALU.subtract, ["sm"], ["lamc"])
            ts("dve", dst, dst, 0.8 - 0.6 * math.exp(-0.3 * l), None, ALU.add, None, ["lamc"], ["lamc"])

        def phase2_attn_prompt(l):
            A.reset()
            lam_init = 0.8 - 0.6 * math.exp(-0.3 * l)
            Qa = A.alloc([67, T], BF16)
            Ka = A.alloc([67, T], BF16)
            Va = A.alloc([128, T // 128, 128], BF16)
            pts = [A.alloc([128, 512], BF16) for _ in range(3)]
            rs = A.alloc([64, 512], F32)
            o1 = A.alloc([64, 512], F32)
            o2 = A.alloc([64, 512], F32)
            sq = A.alloc([64, 512], BF16)
            ys = [A.alloc([64, 512], BF16) for _ in range(2)]
            lamc = A.alloc([128, 1], F32)
            dg = A.alloc([64, 1], F32)
            o64 = A.alloc([64, 64], BF16)
            lam_col(l, lamc)
            dma("sp", dg, diff_norm_g[l].rearrange("(p o) -> p o", o=1), (), ["dg"])
            ts("dve", dg, dg, 1.0 - lam_init, None, ALU.mult, None, ["dg"], ["dg"])
            memset("pool", o64, 1.0 / 64.0, ["o64"])
            memset("pool", Va[:, :, 64:128], 1.0, ["Va1"])
            NQ = T // 512
            yi = [0]

            def run_map(qt, kq, kk, krows, scale, band, acc, acck, sti):
                q0 = qt * 512
                nkc = 4 * qt + 4
                steps = []
                for kc in range(nkc):
                    j = kc - 4 * qt
                    n0 = 128 * max(0, j)
                    steps.append((kc, j, n0, 512 - n0))

                def qk(i):
                    kc, j, n0, N = steps[i]
                    b = sti + (i % 2)
                    mm(psb[b][:, 0:N], kk[krows, kc * 128:(kc + 1) * 128], kq[krows, q0 + n0:q0 + 512], True, True,
                       ["Ka", "Qa"], ["ps%d" % b])

                qk(0)
                for i, (kc, j, n0, N) in enumerate(steps):
                    if i + 1 < len(steps):
                        qk(i + 1)
                    b = sti + (i % 2)
                    pt = pts[i % 3]
                    pk = "pt%d" % (i % 3)
                    act(pt[:, 0:N], psb[b][:, 0:N], AF.Exp, ["ps%d" % b], [pk], scale=scale)
                    if band is not None:
                        if j >= 0:
                            w = 256 if j <= 2 else 128
                            tt("dve", pt[:, 0:w], pt[:, 0:w], band[:, 0:w], ALU.mult, [pk, "eb", "tri"], [pk])
                        elif j == -1 and band is not tri:
                            tt("dve", pt[:, 0:128], pt[:, 0:128], band[:, 128:256], ALU.mult, [pk, "eb"], [pk])
                    mm(acc[:, n0:512], Va[:, kc, :], pt[:, 0:N], i == 0, i == len(steps) - 1, ["Va", "Va1", pk], [acck])

            for typ in ("fox", "diff"):
                for h in range(4):
                    if typ == "fox":
                        dma("sp", Qa, fqa[h, :, 0:T], (), ["Qa"])
                        dma("sp", Ka, fka[h, :, 0:T], (), ["Ka"])
                        vsrc = vsf
                    else:
                        dma("sp", Qa[0:64, :], dqs[h, :, 0:T], (), ["Qa"])
                        dma("sp", Ka[0:64, :], dks[h, :, 0:T], (), ["Ka"])
                        vsrc = vsd
                    dma("sp", Va[:, :, 0:64], vsrc[0:T, h * 64:(h + 1) * 64].rearrange("(c p) e -> p c e", p=128),
                        (), ["Va"])
                    for qt in range(NQ):
                        y = ys[yi[0] % 2]
                        yk = "ys%d" % (yi[0] % 2)
                        yi[0] += 1
                        if typ == "fox":
                            run_map(qt, Qa, Ka, slice(0, 67), 0.125, tri, psb[4], "ps4", 0)
                            P.emit("dve", lambda e: e.reciprocal(out=rs, in_=psb[4][64:128, :]), ["ps4"], ["rs"])
                            tt("dve", y, psb[4][0:64, :], rs, ALU.mult, ["ps4", "rs"], [yk])
                            dma("pool", mixT[512 + h * 64:512 + (h + 1) * 64, qt * 512:(qt + 1) * 512], y, [yk], [])
                        else:
                            sc_ = 32 ** -0.5
                            run_map(qt, Qa, Ka, slice(0, 32), sc_, eb[:, h, :], psb[4], "ps4", 0)
                            run_map(qt, Qa, Ka, slice(32, 64), sc_, eb[:, h, :], psb[5], "ps5", 2)
                            P.emit("dve", lambda e: e.reciprocal(out=rs, in_=psb[4][64:128, :]), ["ps4"], ["rs"])
                            tt("dve", o1, psb[4][0:64, :], rs, ALU.mult, ["ps4", "rs"], ["o1"])
                            P.emit("dve", lambda e: e.reciprocal(out=rs, in_=psb[5][64:128, :]), ["ps5"], ["rs"])
                            tt("dve", o2, psb[5][0:64, :], rs, ALU.mult, ["ps5", "rs"], ["o2"])
                            stt(o1, o2, lamc[0:64, 0:1], o1, ALU.mult, ALU.subtract, ["o1", "o2", "lamc"], ["o1"])
                            tt("dve", sq, o1, o1, ALU.mult, ["o1"], ["sq"])
                            mm(psb[6][0:64, :], o64, sq, True, True, ["o64", "sq"], ["ps6"])
                            ts("dve", rs, psb[6][0:64, :], EPS, None, ALU.add, None, ["ps6"], ["rs"])
                            act(rs, rs, AF.Ln, ["rs"], ["rs"])
                            act(rs, rs, AF.Exp, ["rs"], ["rs"], scale=-0.5)
                            tt("dve", o1, o1, rs, ALU.mult, ["o1", "rs"], ["o1"])
                            ts("dve", y, o1, dg[:, 0:1], -1.0, ALU.mult, ALU.mult, ["o1", "dg"], [yk])
                            dma("pool", mixT[768 + h * 64:768 + (h + 1) * 64, qt * 512:(qt + 1) * 512], y, [yk], [])
            P.barrier()


        def phase2_attn_sample(l):
            A.reset()
            lam_init = 0.8 - 0.6 * math.exp(-0.3 * l)
            triF = A.alloc([128, 128], F32)
            onesF = A.alloc([128, 128], F32)
            memset("dve", onesF, 1.0, ["onesF"])
            memset("pool", triF, 1.0, ["triF"])
            P.emit("pool", lambda e: e.affine_select(out=triF, in_=triF, pattern=[[1, 128]], compare_op=ALU.is_ge,
                                                     fill=0.0, base=0, channel_multiplier=-1), ["triF"], ["triF"])
            pti = A.alloc([128, NS * NPG], I32)
            idxf = A.alloc([128, NS * NPG], F32)
            idxi = A.alloc([128, NS * NPG], I32)
            iopi = A.alloc([128, 1], I32)
            iop = A.alloc([128, 1], F32)
            dma("sp", pti, ptab.partition_broadcast(128), (), ["pti"])
            P.emit("pool", lambda e: e.iota(iopi, pattern=[[0, 1]], base=0, channel_multiplier=1), (), ["iopi"])
            cp("dve", iop, iopi, ["iopi"], ["iop"])
            cp("dve", idxf, pti, ["pti"], ["idxf"])
            ts("dve", idxf, idxf, 128.0, iop[:, 0:1], ALU.mult, ALU.add, ["idxf", "iop"], ["idxf"])
            ts("dve", idxf, idxf, float(l * NPOOL * 128), None, ALU.add, None, ["idxf"], ["idxf"])
            cp("dve", idxi, idxf, ["idxf"], ["idxi"])
            qs = A.alloc([64, 4, NTS], BF16)
            ks = A.alloc([64, 4, NTS], BF16)
            qd = A.alloc([64, 4, NTS], BF16)
            kd = A.alloc([64, 4, NTS], BF16)
            for h in range(4):
                dma("sp", qs[:, h, :], fqa[h, 0:64, T:NT], (), (), pwrites=["qs"])
                dma("sp", ks[:, h, :], fka[h, 0:64, T:NT], (), (), pwrites=["qs"])
                dma("sp", qd[:, h, :], dqs[h, :, T:NT], (), (), pwrites=["qs"])
                dma("sp", kd[:, h, :], dks[h, :, T:NT], (), (), pwrites=["qs"])
            lamc = A.alloc([128, 1], F32)
            lam_col(l, lamc)
            dg2 = A.alloc([128, 1], F32)
            for b in range(2):
                dma("sp", dg2[b * 64:(b + 1) * 64, :], diff_norm_g[l].rearrange("(p o) -> p o", o=1), (), ["dg2"])
            ts("dve", dg2, dg2, -(1.0 - lam_init), None, ALU.mult, None, ["dg2"], ["dg2"])
            o64 = A.alloc([128, 128], BF16)
            memset("pool", o64, 0.0, ["o64"])
            memset("pool", o64[0:64, 0:64], 1.0 / 64.0, ["o64"])
            memset("pool", o64[64:128, 64:128], 1.0 / 64.0, ["o64"])
            gall = A.alloc([128, NPG, 1028], F32)
            glf = A.alloc([128, NPG * 4], F32)
            bk = A.alloc([128, NPG, 256], BF16)
            bv = A.alloc([128, NPG, 256], BF16)
            ktT = A.alloc([64, NPG, 512], BF16)
            p_all = A.alloc([128, NPG * 64], BF16)
            pn = A.alloc([8, 64], BF16)
            tmpS = A.alloc([128, 512], F32)
            bias = A.alloc([128, NPG * 4], F32)
            wth = A.alloc([128, NPG * 4], F32)
            tot = A.alloc([128, NPG * 4], F32)
            inc = A.alloc([128, NPG * 4], F32)
            ones16 = A.alloc([128, NPG], F32)
            memset("dve", ones16, 1.0, ["ones16"])
            lfn = A.alloc([8, 4], F32)
            bnew = A.alloc([8, 4], F32)
            vnew = A.alloc([8, 256], BF16)
            rsum = A.alloc([128, 64], F32)
            yf = A.alloc([128, 2, NTS], BF16)
            od = A.alloc([128, 2, 2, NTS], F32)
            odn = A.alloc([128, 2, NTS], F32)
            sqb = A.alloc([128, 2, NTS], BF16)
            rst = A.alloc([128, 2 * NTS], F32)
            yd = A.alloc([128, 2, NTS], BF16)
            pst2 = psb[3].bitcast(BF16)
            ti = [0]

            def gather_all(s):
                for j in range(NPG):
                    col = s * NPG + j
                    P.emit("pool", lambda e, j=j, col=col: e.indirect_dma_start(
                        out=gall[:, j, :], out_offset=None,
                        in_=c_all, in_offset=bass.IndirectOffsetOnAxis(ap=idxi[:, col:col + 1], axis=0)),
                        ["idxi"], (), dma=True, pwrites=["gall"])

            for s in range(dbg_only_sample or NS):
                c0 = s * TS
                gather_all(s)
                for typ in ("fox", "diff"):
                    fox = typ == "fox"
                    it = ti[0]
                    ti[0] += 1
                    ko_ = 0 if fox else 512
                    cp("dve", bk, gall[:, :, ko_:ko_ + 256], ["gall"], ["bk"])
                    cp("act", bv, gall[:, :, ko_ + 256:ko_ + 512], ["gall"], ["bv"])
                    dma("sp", vnew, (vsf if fox else vsd)[T + c0:T + c0 + TS, :], (), ["vnew"])
                    if fox:
                        cp("dve", glf.rearrange("p (j h) -> p j h", h=4), gall[:, :, 1024:1028], ["gall"], ["glf"])
                        mm(psb[2][:, 0:64], triF, glf, True, True, ["triF", "glf"], ["ps2"])
                        mm(psb[2][:, 64:128], onesF, glf, True, True, ["onesF", "glf"], ["ps2"])
                        cp("dve", wth, psb[2][:, 0:64], ["ps2"], ["wth"])
                        cp("dve", tot, psb[2][:, 64:128], ["ps2"], ["tot"])
                        for h in range(4):
                            P.emit("dve", lambda e, h=h: e.tensor_tensor_scan(
                                out=inc.rearrange("p (j h) -> p j h", h=4)[:, :, h], data0=ones16,
                                data1=tot.rearrange("p (j h) -> p j h", h=4)[:, :, h], initial=0.0, op0=ALU.mult,
                                op1=ALU.add), ["tot", "ones16"], ["inc"])
                        tt("dve", bias, tot, inc, ALU.subtract, ["tot", "inc"], ["bias"])
                        tt("dve", bias, bias, wth, ALU.subtract, ["bias", "wth"], ["bias"])
                        for h in range(4):
                            bh = bias.rearrange("p (j h) -> p j h", h=4)[:, :, h]
                            ts("dve", bh, bh, inc[:, 60 + h:61 + h], 8.0, ALU.add, ALU.mult, ["bias", "inc"], ["bias"])
                        dma("sp", lfn, lfs[T + c0:T + c0 + TS, :], (), ["lfn"])
                        mm(psb[2][0:8, 128:132], triF[0:8, 0:8], lfn, True, True, ["triF", "lfn"], ["ps2"])
                        ts("dve", bnew, psb[2][0:8, 128:132], -1.0, None, ALU.mult, None, ["ps2"], ["bnew"])
                    ncol = 32 if fox else 64
                    qq, kk = (qs, ks) if fox else (qd, kd)
                    for jp in range(NPG // 2):
                        tb_ = pst
                        tk_ = "pst"
                        for jj in range(2):
                            j = 2 * jp + jj
                            for h in range(4):
                                P.emit("pe", lambda e, h=h, j=j, jj=jj, tb_=tb_: e.transpose(
                                    out=tb_[0:64, (jj * 4 + h) * 128:(jj * 4 + h + 1) * 128],
                                    in_=bk[:, j, h * 64:(h + 1) * 64], identity=ident), ["bk", "ident"], [tk_])
                        evac(ktT[:, 2 * jp:2 * jp + 2, :], tb_[0:64, :].rearrange("p (a b) -> p a b", b=512), [tk_], ["ktT"])
                    for j in range(NPG):
                        for h in range(4):
                            for m in range(1 if fox else 2):
                                rows = slice(0, 64) if fox else slice(32 * m, 32 * m + 32)
                                cc = j * 32 + h * 8
                                dst_ = psb[m][:, cc:cc + 8]
                                dk_ = "ps%d" % m
                                mm(dst_, ktT[rows, j, h * 128:(h + 1) * 128], qq[rows, h, c0:c0 + TS], True, True,
                                   ["ktT", "qs"], [dk_])
                    for h in range(4):
                        for m in range(1 if fox else 2):
                            rows = slice(0, 64) if fox else slice(32 * m, 32 * m + 32)
                            nb_ = psb[2][0:8, 192 + h * 8:192 + (h + 1) * 8] if m == 0 else psb[3][0:8, h * 8:(h + 1) * 8]
                            mm(nb_, kk[rows, h, c0:c0 + TS], qq[rows, h, c0:c0 + TS], True, True,
                               ["qs"], ["ps2" if m == 0 else "ps3"])
                    if fox:
                        tt("dve", tmpS.rearrange("p (a q) -> p a q", q=8), psb[0].rearrange("p (a q) -> p a q", q=8),
                           bias.unsqueeze(2).broadcast_to([128, NPG * 4, 8]), ALU.add, ["ps0", "bias"], ["tmpS"])
                        act(p_all[:, 0:512], tmpS, AF.Exp, ["tmpS"], ["p_all"], scale=0.125)
                        for h in range(4):
                            act(pn[:, h * 8:(h + 1) * 8], psb[2][0:8, 192 + h * 8:192 + (h + 1) * 8], AF.Exp,
                                ["ps2", "bnew"], ["pn"], bias=bnew[:, h:h + 1], scale=0.125)
                        tt("dve", pn[:, 0:32].rearrange("p (h q) -> p h q", q=8),
                           pn[:, 0:32].rearrange("p (h q) -> p h q", q=8),
                           tri[0:8, 0:8].unsqueeze(1).broadcast_to([8, 4, 8]), ALU.mult, ["pn", "tri"], ["pn"])
                    else:
                        sc_ = 32 ** -0.5
                        for m in range(2):
                            act(p_all.rearrange("p (j h m q) -> p j h m q", h=4, m=2, q=8)[:, :, :, m, :],
                                psb[m].rearrange("p (j h q) -> p j h q", h=4, q=8), AF.Exp, ["ps%d" % m], ["p_all"],
                                scale=sc_)
                        act(pn.rearrange("p (h m q) -> p h m q", m=2, q=8)[:, :, 0, :],
                            psb[2][0:8, 192:224].rearrange("p (h q) -> p h q", q=8), AF.Exp, ["ps2"], ["pn"], scale=sc_)
                        act(pn.rearrange("p (h m q) -> p h m q", m=2, q=8)[:, :, 1, :],
                            psb[3][0:8, 0:32].rearrange("p (h q) -> p h q", q=8), AF.Exp, ["ps3"], ["pn"], scale=sc_)
                        for h in range(4):
                            v_ = p_all[:, 15 * 64 + h * 16:15 * 64 + (h + 1) * 16].rearrange("p (m q) -> p m q", q=8)
                            tt("dve", v_, v_, eb[:, h, 128:136].unsqueeze(1).broadcast_to([128, 2, 8]), ALU.mult,
                               ["p_all", "eb"], ["p_all"])
                            v2 = pn[:, h * 16:(h + 1) * 16].rearrange("p (m q) -> p m q", q=8)
                            tt("dve", v2, v2, eb[0:8, h, 0:8].unsqueeze(1).broadcast_to([8, 2, 8]), ALU.mult,
                               ["pn", "eb"], ["pn"])
                    for j in range(NPG + 1):
                        new = j == NPG
                        nk = 8 if new else 128
                        rhs_ = pn[0:8, 0:ncol] if new else p_all[:, j * ncol:(j + 1) * ncol]
                        rk = "pn" if new else "p_all"
                        for hp in range(2):
                            lhs = vnew[:, hp * 128:(hp + 1) * 128] if new else bv[:, j, hp * 128:(hp + 1) * 128]
                            mm(psb[4 + hp][:, 0:ncol], lhs, rhs_, j == 0, new,
                               (["vnew"] if new else ["bv"]) + [rk], ["ps%d" % (4 + hp)])
                        mm(psb[6][:, 0:ncol], ones_b[0:nk, :], rhs_, j == 0, new, ["ones_b", rk], ["ps6"])
                    P.emit("dve", lambda e, ncol=ncol: e.reciprocal(out=rsum[:, 0:ncol], in_=psb[6][:, 0:ncol]),
                           ["ps6"], ["rsum"])
                    for h in range(4):
                        pr = slice((h % 2) * 64, (h % 2) * 64 + 64)
                        ab = psb[4 + h // 2]
                        ak = "ps%d" % (4 + h // 2)
                        if fox:
                            tt("dve", yf[pr, h // 2, c0:c0 + TS], ab[pr, h * 8:(h + 1) * 8],
                               rsum[pr, h * 8:(h + 1) * 8], ALU.mult, [ak, "rsum"], ["yf"])
                        else:
                            for m in range(2):
                                cc = (h * 2 + m) * 8
                                tt("dve", od[pr, m, h // 2, c0:c0 + TS], ab[pr, cc:cc + 8], rsum[pr, cc:cc + 8],
                                   ALU.mult, [ak, "rsum"], ["od"])
            stt(odn, od[:, 1], lamc[:, 0:1], od[:, 0], ALU.mult, ALU.subtract, ["od", "lamc"], ["odn"])
            tt("dve", sqb, odn, odn, ALU.mult, ["odn"], ["sqb"])
            mm(psb[0][:, 0:2 * NTS], o64, sqb.rearrange("p a t -> p (a t)"), True, True, ["o64", "sqb"], ["ps0"])
            ts("dve", rst, psb[0][:, 0:2 * NTS], EPS, None, ALU.add, None, ["ps0"], ["rst"])
            act(rst, rst, AF.Sqrt, ["rst"], ["rst"])
            P.emit("dve", lambda e: e.reciprocal(out=rst, in_=rst), ["rst"], ["rst"])
            tt("dve", odn, odn, rst.rearrange("p (a t) -> p a t", t=NTS), ALU.mult, ["odn", "rst"], ["odn"])
            ts("dve", yd, odn, dg2[:, 0:1], None, ALU.mult, None, ["odn", "dg2"], ["yd"])
            for c in range(2):
                dma("pool", mixT[512 + c * 128:512 + (c + 1) * 128, T:NT], yf[:, c, :], ["yf"], [])
                dma("pool", mixT[768 + c * 128:768 + (c + 1) * 128, T:NT], yd[:, c, :], ["yd"], [])
            P.barrier()

        def phase3(l):
            A.reset()
            wA = A.alloc([128, 8, DFF], BF16)
            wB = A.alloc([128, 8, DFF], BF16)
            wC = A.alloc([128, 8, D], BF16)
            W = dict(ssq=A.alloc([128, 4], F32), rstd=A.alloc([128, 4], F32),
                     xsb=A.alloc([128, 4, 1024], BF16))
            xt = A.alloc([128, 4, 1024], F32)
            xnT = A.alloc([128, 8, 512], BF16)
            mts = [A.alloc([128, 8, 512], BF16) for _ in range(2)]
            sg = [A.alloc([128, 512], F32) for _ in range(2)]
            ast = [A.alloc([128, 512], BF16) for _ in range(3)]
            load_w(wC, wb_out[l], (("wb", id(wb_out), l), "wC"))
            load_w(wA, wb_gate[l], (("wb", id(wb_gate), l), "wA"))
            load_w(wB, wb_up[l], (("wb", id(wb_up), l), "wB"))
            load_gb(norm_ffn_g[l])

            def load_m(ti):
                t0, n = tiles[ti]
                dma("sp", mts[ti % 2][:, :, 0:n], mixT[:, t0:t0 + n].rearrange("(k p) t -> p k t", p=128), (),
                    ["mt%d" % (ti % 2)])

            load_m(0)
            for ti, (t0, n) in enumerate(tiles):
                nsub = n // 128
                if l == 0:
                    src = xp[t0:t0 + n, :] if t0 < T else xs
                else:
                    src = xres[t0:t0 + n, :]
                dma("sp", xt[:, 0:nsub, :], src.rearrange("(s p) d -> p s d", p=128), ["xres"], ["xt"])
                if ti + 1 < len(tiles):
                    load_m(ti + 1)
                mt = mts[ti % 2]
                mk_ = "mt%d" % (ti % 2)
                for s in range(nsub):
                    for hf in range(2):
                        b = (2 * s + hf) % 4
                        for k in range(8):
                            mm(psb[b], mt[:, k, s * 128:(s + 1) * 128], wC[:, k, hf * 512:(hf + 1) * 512], k == 0, k == 7,
                               [mk_, "wC"], ["ps%d" % b])
                        tt("dve", xt[:, s, hf * 512:(hf + 1) * 512], xt[:, s, hf * 512:(hf + 1) * 512], psb[b], ALU.add,
                           ["xt", "ps%d" % b], ["xt"])
                dma("pool", xres[t0:t0 + n, :].rearrange("(s p) d -> p s d", p=128), xt[:, 0:nsub, :], ["xt"], ["xres"])
                rmsnorm_T(xt, nsub, xnT, W)
                for fc in range(NFC):
                    bg = (2 * fc) % 4
                    bu = bg + 1
                    for k in range(8):
                        mm(psb[bg][:, 0:n], wA[:, k, fc * 128:(fc + 1) * 128], xnT[:, k, 0:n], k == 0, k == 7,
                           ["wA", "xnT"], ["ps%d" % bg])
                    for k in range(8):
                        mm(psb[bu][:, 0:n], wB[:, k, fc * 128:(fc + 1) * 128], xnT[:, k, 0:n], k == 0, k == 7,
                           ["wB", "xnT"], ["ps%d" % bu])
                    sgt = sg[fc % 2]
                    sgk = "sg%d" % (fc % 2)
                    act(sgt[:, 0:n], psb[bg][:, 0:n], AF.Silu, ["ps%d" % bg], [sgk])
                    a_ = ast[fc % 3]
                    ak = "ast%d" % (fc % 3)
                    tt("dve", a_[:, 0:n], sgt[:, 0:n], psb[bu][:, 0:n], ALU.mult, [sgk, "ps%d" % bu], [ak])
                    dma("pool" if fc % 2 else "sp", aT[fc * 128:(fc + 1) * 128, t0:t0 + n], a_[:, 0:n], [ak], [])
            P.barrier()

        def phase4(l):
            A.reset()
            wA = A.alloc([128, NFC, D], BF16)
            xts = [A.alloc([128, 4, 1024], F32) for _ in range(2)]
            ats = [A.alloc([128, NFC, 512], BF16) for _ in range(2)]
            W = dict(junk=A.alloc([128, 1024], BF16), ssq=A.alloc([128, 4], F32), rstd=A.alloc([128, 4], F32))
            gfb = A.alloc([128, 1024], F32)
            load_w(wA, wb_down[l], (("wb", id(wb_down), l), "wA"))
            last = (l == L - 1)
            if last:
                dma("sp", gfb, norm_final_g.partition_broadcast(128), (), ["gfb"])
            def load_xa(ti):
                t0, n = tiles[ti]
                dma("sp", xts[ti % 2][:, 0:n // 128, :], xres[t0:t0 + n, :].rearrange("(s p) d -> p s d", p=128), (),
                    ["xt%d" % (ti % 2)])
                dma("sp", ats[ti % 2][:, :, 0:n], aT[:, t0:t0 + n].rearrange("(k p) t -> p k t", p=128), (),
                    ["at%d" % (ti % 2)])

            load_xa(0)
            for ti, (t0, n) in enumerate(tiles):
                nsub = n // 128
                if ti + 1 < len(tiles):
                    load_xa(ti + 1)
                xt = xts[ti % 2]
                at = ats[ti % 2]
                xk = "xt%d" % (ti % 2)
                ak_ = "at%d" % (ti % 2)
                for s in range(nsub):
                    for hf in range(2):
                        b = (2 * s + hf) % 4
                        for k in range(NFC):
                            mm(psb[b], at[:, k, s * 128:(s + 1) * 128], wA[:, k, hf * 512:(hf + 1) * 512], k == 0,
                               k == NFC - 1, [ak_, "wA"], ["ps%d" % b])
                        tt("dve", xt[:, s, hf * 512:(hf + 1) * 512], xt[:, s, hf * 512:(hf + 1) * 512], psb[b], ALU.add,
                           [xk, "ps%d" % b], [xk])
                if not last:
                    dma("pool", xres[t0:t0 + n, :].rearrange("(s p) d -> p s d", p=128), xt[:, 0:nsub, :], [xk], [])
                else:
                    for s in range(nsub):
                        act(W["junk"], xt[:, s, :], AF.Square, [xk], ["junk", "ssq"], accum_out=W["ssq"][:, s:s + 1])
                    ts("dve", W["rstd"][:, 0:nsub], W["ssq"][:, 0:nsub], 1.0 / D, EPS, ALU.mult, ALU.add, ["ssq"], ["rstd"])
                    act(W["rstd"][:, 0:nsub], W["rstd"][:, 0:nsub], AF.Ln, ["rstd"], ["rstd"])
                    act(W["rstd"][:, 0:nsub], W["rstd"][:, 0:nsub], AF.Exp, ["rstd"], ["rstd"], scale=-0.5)
                    for s in range(nsub):
                        stt(xt[:, s, :], xt[:, s, :], W["rstd"][:, s:s + 1], gfb, ALU.mult, ALU.mult,
                            [xk, "rstd", "gfb"], [xk])
                    dst = o_yp[t0:t0 + n, :] if t0 < T else o_ys
                    dma("pool", dst.rearrange("(s p) d -> p s d", p=128), xt[:, 0:nsub, :], [xk], [])
            P.barrier()

        if dbg_only_sample:
            phase2_attn_sample(0)
        for l in range(L if not dbg_only_sample else 0):
            phase1(l)
            phase2_frows(l)
            phase2_rec(l, False)
            phase2_rec(l, True)
            phase2_attn_prompt(l)
            if sample_attn:
                phase2_attn_sample(l)
            phase3(l)
            phase4(l)

        P.replay(nc, st)
    return nc


_IN_NAMES = ["w_in", "w_out", "w_gate", "w_up", "w_down", "norm_mix_g", "norm_ffn_g", "norm_final_g", "lru_conv_w",
             "lru_conv_b", "lru_w_a", "lru_b_a", "lru_w_x", "lru_b_x", "lru_lambda", "sc_conv_w", "fox_b_f",
             "diff_norm_g", "rel_bias"]


def make_in_maps(inp, cores, sample_attn=True):
    f = lambda a: np.ascontiguousarray(np.asarray(a, dtype=np.float32))
    shared = {k: f(inp[k]) for k in _IN_NAMES}
    shared["diff_lambda"] = f(inp["diff_lambda"]).reshape(L, 128)
    shared["c_t5"] = t5_onehot()
    if sample_attn:
        R_ = L * NPOOL * 128
        ca = np.empty((R_, 1028), np.float32)
        ca[:, 0:256] = np.asarray(inp["cache_fox_k"], dtype=np.float32).reshape(R_, 256)
        ca[:, 256:512] = np.asarray(inp["cache_fox_v"], dtype=np.float32).reshape(R_, 256)
        ca[:, 512:768] = np.asarray(inp["cache_diff_k"], dtype=np.float32).reshape(R_, 256)
        ca[:, 768:1024] = np.asarray(inp["cache_diff_v"], dtype=np.float32).reshape(R_, 256)
        ca[:, 1024:1028] = np.asarray(inp["cache_fox_logf"], dtype=np.float32).reshape(R_, 4)
        shared["c_all"] = ca
    maps = []
    for c in cores:
        m = dict(shared)
        sl = slice(c * NS, (c + 1) * NS)
        m["xp"] = f(inp["x_prompt"][c % 4])
        m["xs"] = f(inp["x_sample"][sl]).reshape(NTS, D)
        m["st_lru_h"] = f(np.asarray(inp["state_lru_h"])[:, sl])
        m["st_lru_conv"] = f(np.asarray(inp["state_lru_conv"])[:, sl])
        m["st_sconv"] = f(np.asarray(inp["state_sconv"])[:, sl])
        if sample_attn:
            m["ptab"] = np.ascontiguousarray(np.asarray(inp["page_table"], dtype=np.int32)[sl]).reshape(NS * NPG)
        maps.append(m)
    return maps


def assemble(results, cores):
    B = 4
    G = 128
    out = {}
    yp = np.zeros((B, T, D), np.float32)
    ys = np.zeros((G, TS, D), np.float32)
    pk = {n: np.zeros((L, B, T, 256 if n != "flf" else 4), np.float32) for n in ("fk", "fv", "flf", "dk", "dv")}
    plh = np.zeros((L, B, 256), np.float32)
    plc = np.zeros((L, B, 3, 256), np.float32)
    psc = np.zeros((L, B, 2, 256), np.float32)
    sk = {n: np.zeros((L, G, TS, 256 if n != "flf" else 4), np.float32) for n in ("fk", "fv", "flf", "dk", "dv")}
    slh = np.zeros((L, G, 256), np.float32)
    slc = np.zeros((L, G, 3, 256), np.float32)
    ssc = np.zeros((L, G, 2, 256), np.float32)
    for r, c in zip(results, cores):
        sl = slice(c * NS, (c + 1) * NS)
        if c < 4:
            yp[c] = r["o_yp"]
            for n in pk:
                pk[n][:, c] = r["o_p" + n]
            plh[:, c] = r["o_plh"]
            plc[:, c] = r["o_plc"]
            psc[:, c] = r["o_psc"]
        ys[sl] = r["o_ys"].reshape(NS, TS, D)
        for n in sk:
            sk[n][:, sl] = r["o_s" + n].reshape(L, NS, TS, -1)
        slh[:, sl] = r["o_slh"]
        slc[:, sl] = r["o_slc"]
        ssc[:, sl] = r["o_ssc"]
    return (yp, ys,
            pk["fk"].reshape(L, B, T, 4, 64), pk["fv"].reshape(L, B, T, 4, 64), pk["flf"],
            pk["dk"].reshape(L, B, T, 4, 64), pk["dv"].reshape(L, B, T, 4, 64), plh, plc, psc,
            sk["fk"].reshape(L, G, TS, 4, 64), sk["fv"].reshape(L, G, TS, 4, 64), sk["flf"],
            sk["dk"].reshape(L, G, TS, 4, 64), sk["dv"].reshape(L, G, TS, 4, 64), slh, slc, ssc)


def kernel(**inputs):
    cores = list(range(N_CORES))
    nc = build_nc(sample_attn=True)
    in_maps = make_in_maps(inputs, cores, sample_attn=True)
    res = run_bass_kernel_spmd(nc, in_maps, core_ids=cores)
    return assemble(res.results, cores)
```
